# Optimizing a Trainium2 kernel written in Bass

```python
import numpy as np
import jax, jax.numpy as jnp
from jax import lax

D_MODEL = 2048
BATCH = 2
SEQ = 4096
DEPTH = 1
DEC_BATCH = 8
DEC_SEQ = 2048
PAST_LEN = 128

HEAD_DIM = 128
N_HEADS_A = 8
N_KV_HEADS_A = 2
N_HEADS_B = 8
GRID_W = 64
WIN_ROWS_MAX = 8
WIN_COLS = 16
COL_SEG = 16
KEY_COLS = 32
Q_BLOCK = 128
D_FF = 5504
ROPE_THETA = 10000.0
EPS = 1e-6
RPB_STD = 0.02

WIDTH_QA = N_HEADS_A * HEAD_DIM
WIDTH_KVA = N_KV_HEADS_A * HEAD_DIM
WIDTH_B = N_HEADS_B * HEAD_DIM
IN_SPLITS = (WIDTH_QA, WIDTH_KVA, WIDTH_KVA, WIDTH_B, WIDTH_B, WIDTH_B, D_MODEL, D_MODEL)
IN_WIDTH = sum(IN_SPLITS)
IN_OFFSETS = tuple(int(o) for o in np.cumsum(IN_SPLITS)[:-1])

kernel_name = "hybrid_gqa_natten_macaron_encoder"


def rms_norm(x, g):
    xf = x.astype(jnp.float32)
    y = xf * lax.rsqrt(jnp.mean(xf * xf, axis=-1, keepdims=True) + EPS)
    return (y * g.astype(jnp.float32)).astype(x.dtype)


def swiglu(x, w_in, w_out):
    gate, up = jnp.split(x @ w_in, 2, axis=-1)
    return (jax.nn.silu(gate) * up) @ w_out


def axial_rope_tables(T):
    t = np.arange(T)
    row = (t // GRID_W).astype(np.float32)
    col = (t % GRID_W).astype(np.float32)
    n_pairs_axis = HEAD_DIM // 4
    inv = (ROPE_THETA ** (-np.arange(n_pairs_axis, dtype=np.float32) / n_pairs_axis)).astype(np.float32)
    ang = np.concatenate([row[:, None] * inv[None], col[:, None] * inv[None]], axis=-1)
    return jnp.asarray(np.cos(ang), jnp.float32), jnp.asarray(np.sin(ang), jnp.float32)


def apply_rope(x, cos, sin):
    xf = x.astype(jnp.float32).reshape(x.shape[:-1] + (HEAD_DIM // 2, 2))
    x0, x1 = xf[..., 0], xf[..., 1]
    c, s = cos[:, None, :], sin[:, None, :]
    out = jnp.stack([x0 * c - x1 * s, x0 * s + x1 * c], axis=-1)
    return out.reshape(x.shape).astype(x.dtype)


def global_gqa(q, k, v, g_q, g_k):
    B, T = q.shape[0], q.shape[1]
    cos, sin = axial_rope_tables(T)
    q = apply_rope(rms_norm(q, g_q), cos, sin)
    k = apply_rope(rms_norm(k, g_k), cos, sin)
    G = N_HEADS_A // N_KV_HEADS_A
    nb = T // Q_BLOCK
    qb = q.reshape(B, nb, Q_BLOCK, N_KV_HEADS_A, G, HEAD_DIM).transpose(1, 0, 2, 3, 4, 5)
    scale = HEAD_DIM ** -0.5

    def block(qi):
        s = jnp.einsum('bqkgd,bskd->bkgqs', qi, k, preferred_element_type=jnp.float32) * scale
        p = jax.nn.softmax(s, axis=-1).astype(v.dtype)
        return jnp.einsum('bkgqs,bskd->bqkgd', p, v)

    o = lax.map(block, qb)
    return o.transpose(1, 0, 2, 3, 4, 5).reshape(B, T, WIDTH_QA)


def neighbourhood_attn(q, k, v, rpb):
    B, T, H, hd = q.shape
    rows = T // GRID_W
    wr = min(WIN_ROWS_MAX, rows)
    n_seg = GRID_W // COL_SEG
    qc = np.arange(GRID_W).reshape(n_seg, COL_SEG)
    seg_start = np.clip(qc[:, 0] - WIN_COLS // 2, 0, GRID_W - KEY_COLS)
    kc = seg_start[:, None] + np.arange(KEY_COLS)[None, :]
    cs = np.clip(qc - WIN_COLS // 2, 0, GRID_W - WIN_COLS)
    valid = (kc[:, None, :] >= cs[:, :, None]) & (kc[:, None, :] < cs[:, :, None] + WIN_COLS)
    dc_idx = np.clip(kc[:, None, :] - qc[:, :, None] + WIN_COLS - 1, 0, 2 * WIN_COLS - 2)
    mask_add = jnp.asarray(np.where(valid, 0.0, -1e30).astype(np.float32))[:, :, None, :]
    qg = q.reshape(B, rows, n_seg, COL_SEG, H, hd)
    kg = k.reshape(B, rows, GRID_W, H, hd)
    vg = v.reshape(B, rows, GRID_W, H, hd)
    scale = hd ** -0.5
    rpb_f = rpb.astype(jnp.float32)

    def row_block(r):
        rs = jnp.clip(r - wr // 2, 0, rows - wr)
        kb = lax.dynamic_slice_in_dim(kg, rs, wr, axis=1)[:, :, kc]
        vb = lax.dynamic_slice_in_dim(vg, rs, wr, axis=1)[:, :, kc]
        qr = lax.dynamic_index_in_dim(qg, r, axis=1, keepdims=False)
        s = jnp.einsum('bnqhd,bwnkhd->bhnqwk', qr, kb, preferred_element_type=jnp.float32) * scale
        dr_idx = rs + jnp.arange(wr) - r + WIN_ROWS_MAX - 1
        bias = jnp.take(rpb_f, dr_idx, axis=1)[:, :, dc_idx]
        s = s + bias.transpose(0, 2, 3, 1, 4) + mask_add
        p = jax.nn.softmax(s.reshape(s.shape[:4] + (wr * KEY_COLS,)), axis=-1).reshape(s.shape)
        o = jnp.einsum('bhnqwk,bwnkhd->bnqhd', p.astype(vb.dtype), vb)
        return o.reshape(B, GRID_W, H * hd)

    o = lax.map(row_block, jnp.arange(rows))
    return o.transpose(1, 0, 2, 3).reshape(B, T, H * hd)


def encoder_layer(x, g_ffn1, w_ffn1_in, w_ffn1_out, g_mix, w_in, b_gate, g_q_a, g_k_a, rpb_b,
                  w_branch_a, w_branch_b, w_out, g_ffn2, w_ffn2_in, w_ffn2_out):
    B, T, _ = x.shape
    h = x + 0.5 * swiglu(rms_norm(x, g_ffn1), w_ffn1_in, w_ffn1_out)
    u = rms_norm(h, g_mix)
    qa, ka, va, qb, kb, vb, gate_a, gate_b = jnp.split(u @ w_in, IN_OFFSETS, axis=-1)
    ya = global_gqa(qa.reshape(B, T, N_HEADS_A, HEAD_DIM), ka.reshape(B, T, N_KV_HEADS_A, HEAD_DIM),
                    va.reshape(B, T, N_KV_HEADS_A, HEAD_DIM), g_q_a, g_k_a)
    yb = neighbourhood_attn(qb.reshape(B, T, N_HEADS_B, HEAD_DIM), kb.reshape(B, T, N_HEADS_B, HEAD_DIM),
                            vb.reshape(B, T, N_HEADS_B, HEAD_DIM), rpb_b)
    ga, gb = jnp.split(b_gate, 2)
    merged = jax.nn.sigmoid(gate_a + ga) * (ya @ w_branch_a) + jax.nn.sigmoid(gate_b + gb) * (yb @ w_branch_b)
    h = h + merged @ w_out
    h = h + 0.5 * swiglu(rms_norm(h, g_ffn2), w_ffn2_in, w_ffn2_out)
    return h


def setup_inputs(seed: int = 0) -> dict:
    key = jax.random.key(seed)
    ks = jax.random.split(key, 20)
    f32 = jnp.float32

    def w(k, shape, fan_in):
        return jax.random.normal(k, shape, f32) * (fan_in ** -0.5)

    def gain(k, shape):
        return 1.0 + 0.02 * jax.random.normal(k, shape, f32)

    return {
        "x_prompt": jax.random.normal(ks[0], (BATCH, SEQ, D_MODEL), f32),
        "x_sample": jax.random.normal(ks[1], (DEC_BATCH, DEC_SEQ, D_MODEL), f32),
        "g_ffn1": gain(ks[2], (DEPTH, D_MODEL)),
        "w_ffn1_in": w(ks[3], (DEPTH, D_MODEL, 2 * D_FF), D_MODEL),
        "w_ffn1_out": w(ks[4], (DEPTH, D_FF, D_MODEL), D_FF),
        "g_mix": gain(ks[5], (DEPTH, D_MODEL)),
        "w_in": w(ks[6], (DEPTH, D_MODEL, IN_WIDTH), D_MODEL),
        "b_gate": 0.02 * jax.random.normal(ks[7], (DEPTH, 2 * D_MODEL), f32),
        "g_q_a": gain(ks[8], (DEPTH, HEAD_DIM)),
        "g_k_a": gain(ks[9], (DEPTH, HEAD_DIM)),
        "rpb_b": RPB_STD * jax.random.normal(ks[10], (DEPTH, N_HEADS_B, 2 * WIN_ROWS_MAX - 1, 2 * WIN_COLS - 1), f32),
        "w_branch_a": w(ks[11], (DEPTH, WIDTH_QA, D_MODEL), WIDTH_QA),
        "w_branch_b": w(ks[12], (DEPTH, WIDTH_B, D_MODEL), WIDTH_B),
        "w_out": w(ks[13], (DEPTH, D_MODEL, D_MODEL), D_MODEL),
        "g_ffn2": gain(ks[14], (DEPTH, D_MODEL)),
        "w_ffn2_in": w(ks[15], (DEPTH, D_MODEL, 2 * D_FF), D_MODEL),
        "w_ffn2_out": w(ks[16], (DEPTH, D_FF, D_MODEL), D_FF),
        "g_final": gain(ks[17], (D_MODEL,)),
    }


def reference(x_prompt, x_sample, g_ffn1, w_ffn1_in, w_ffn1_out, g_mix, w_in, b_gate, g_q_a, g_k_a,
              rpb_b, w_branch_a, w_branch_b, w_out, g_ffn2, w_ffn2_in, w_ffn2_out, g_final):
    def trunk(x):
        h = x
        for l in range(DEPTH):
            h = encoder_layer(h, g_ffn1[l], w_ffn1_in[l], w_ffn1_out[l], g_mix[l], w_in[l], b_gate[l],
                              g_q_a[l], g_k_a[l], rpb_b[l], w_branch_a[l], w_branch_b[l], w_out[l],
                              g_ffn2[l], w_ffn2_in[l], w_ffn2_out[l])
        return rms_norm(h, g_final)

    y_prompt = trunk(x_prompt)
    y_sample = trunk(x_sample)
    return (y_prompt, y_sample)
```

```python
import math
from contextlib import ExitStack

import numpy as np

import concourse.bass as bass
import concourse.mybir as mybir
from concourse.bass_utils import run_bass_kernel_spmd

F32 = mybir.dt.float32
BF16 = mybir.dt.bfloat16
AF = mybir.ActivationFunctionType
ALU = mybir.AluOpType
AX = mybir.AxisListType

D = 2048
DFF = 5504
NFC = DFF // 128
HD = 128
INW = 8704
EPS = 1e-6
GRID_W = 64
NEG = -30000.0
SCALE = HD ** -0.5

O_QA, O_KA, O_VA, O_QB, O_KB, O_VB, O_GA, O_GB = 0, 1024, 1280, 1536, 2560, 3584, 4608, 6656


class Tok:
    __slots__ = ("key", "sem", "val")

    def __init__(self, key, sem, val):
        self.key, self.sem, self.val = key, sem, val


class Buf:
    def __init__(self, name=""):
        self.name = name
        self.w = None
        self.r = {}

    def inherit(self, others):
        for o in others:
            if o.w is not None:
                self.r[("w", o.w.key)] = o.w if (("w", o.w.key) not in self.r or self.r[("w", o.w.key)].val < o.w.val) else self.r[("w", o.w.key)]
            for k, t in o.r.items():
                if k not in self.r or self.r[k].val < t.val:
                    self.r[k] = t


class DSem:
    def __init__(self, sem, idx):
        self.sem, self.val, self.key = sem, 0, ("d", idx)


class Tr:
    def __init__(self, nc, es):
        self.nc = nc
        self.es = es
        self.engs = {"pe": nc.tensor, "act": nc.scalar, "dve": nc.vector, "pool": nc.gpsimd, "sp": nc.sync}
        self.sem = {k: es.enter_context(nc.semaphore("c_" + k)) for k in ("pe", "act", "dve", "pool")}
        self.cnt = {k: 0 for k in self.sem}
        self.seen = {}
        self.nds = 0
        self.nwait = 0
        self.pend = {k: ([], []) for k in self.sem}
        self.dsems = []
        self.pool_fifo = []
        self.pool_out = []

    def dsem(self):
        self.nds += 1
        d = DSem(self.es.enter_context(self.nc.semaphore("d%d" % self.nds)), self.nds)
        self.dsems.append(d)
        return d

    def barrier(self):
        toks = [Tok(k, self.sem[k], self.cnt[k]) for k in self.sem if self.cnt[k] > 0]
        toks += [Tok(d.key, d.sem, d.val) for d in self.dsems if d.val > 0]
        for e in ("pe", "act", "dve", "pool", "sp"):
            self.wait(e, toks)

    def wait(self, eng, toks):
        for t in toks:
            if t is None:
                continue
            if eng == "pe" and t.key == "pe":
                continue
            k = (eng, t.key)
            if self.seen.get(k, 0) < t.val:
                self.engs[eng].wait_ge(t.sem, t.val)
                self.seen[k] = t.val
                self.nwait += 1

    @staticmethod
    def deps(reads, writes):
        d = []
        for b in reads:
            d.append(b.w)
            if b.name.startswith("ps"):
                d.extend(b.r.values())
        for b in writes:
            d.append(b.w)
            d.extend(b.r.values())
        return d

    @staticmethod
    def mark(tok, reads, writes):
        for b in writes:
            b.w = tok
            b.r = {}
        for b in reads:
            o = b.r.get(tok.key)
            if o is None or o.val < tok.val:
                b.r[tok.key] = tok

    def op(self, eng, fn, reads=(), writes=(), inc=True, extra=()):
        self.wait(eng, self.deps(reads, writes) + list(extra))
        ins = fn()
        if not inc:
            self.pend[eng][0].extend(reads)
            self.pend[eng][1].extend(writes)
            return None
        self.cnt[eng] += 1
        ins.then_inc(self.sem[eng], 1)
        tok = Tok(eng, self.sem[eng], self.cnt[eng])
        pr, pw = self.pend[eng]
        self.mark(tok, list(reads) + pr, list(writes) + pw)
        self.pend[eng] = ([], [])
        return tok

    def dma(self, q, ds, out, in_, reads=(), writes=(), extra=(), ndesc=32, **kw):
        self.wait(q, self.deps(reads, writes) + list(extra))
        if q == "pool":
            self.pool_out.append(None)
            while sum(n_ for _, n_ in self.pool_fifo) + ndesc > 480 and self.pool_fifo:
                t_, _ = self.pool_fifo.pop(0)
                self.wait("pool", [t_])
            self.pool_out.pop()
        ins = self.engs[q].dma_start(out=out, in_=in_, **kw)
        ds.val += 16
        ins.then_inc(ds.sem, 16)
        tok = Tok(ds.key, ds.sem, ds.val)
        self.mark(tok, reads, writes)
        if q == "pool":
            self.pool_fifo.append((tok, ndesc))
        return tok


class Ring:
    def __init__(self, items):
        self.items = items
        self.i = 0

    def next(self):
        it = self.items[self.i % len(self.items)]
        self.i += 1
        return it


class _Stop(Exception):
    pass


def build_nc(T, stop=None):
    NT = T // 512
    NCH = T // 128
    NBLK = NCH
    nc = bass.Bass("TRN2", target_bir_lowering=False)

    def din(name, shape, dt=F32):
        return nc.dram_tensor(name, list(shape), dt, kind="ExternalInput")

    x_d = din("x", [T, D])
    rope_d = din("rope", [T, 128])
    maskA_d = din("maskA", [128, 4])
    maskB_d = din("maskB", [128, 9 * 896])
    ident_d = din("ident", [128, 128])
    g1c_d = din("g1c", [128, 16])
    gmc_d = din("gmc", [128, 16])
    g2c_d = din("g2c", [128, 16])
    gfb_d = din("gfb", [128, D])
    bgc_d = din("bgc", [128, 32])
    gqb_d = din("gqb", [128, 128])
    gkb_d = din("gkb", [128, 128])
    rpb_d = din("rpb", [120, 31])
    w1a_d = din("w1a", [D, 2 * DFF])
    w1b_d = din("w1b", [DFF, D])
    win_d = din("win", [D, INW])
    wa_d = din("wa", [1024, D])
    wb_d = din("wb", [1024, D])
    wo_d = din("wo", [D, D])
    w2a_d = din("w2a", [D, 2 * DFF])
    w2b_d = din("w2b", [DFF, D])
    out_d = nc.dram_tensor("out", [T, D], F32, kind="ExternalOutput")

    def dscr(name, shape, dt=BF16):
        return nc.dram_tensor(name, list(shape), dt)

    w1a_s = dscr("w1a_s", [D, 2 * DFF])
    w1b_s = dscr("w1b_s", [DFF, D])
    win_s = dscr("win_s", [D, INW])
    wa_s = dscr("wa_s", [1024, D])
    wb_s = dscr("wb_s", [1024, D])
    wo_s = dscr("wo_s", [D, D])
    w2a_s = dscr("w2a_s", [D, 2 * DFF])
    w2b_s = dscr("w2b_s", [DFF, D])
    h_s = dscr("h_s", [T, D], F32)
    qaT_s = dscr("qaT_s", [8, 128, T])
    kaT_s = dscr("kaT_s", [2, 128, T])
    va_s = dscr("va_s", [T, 256])
    qbT_s = dscr("qbT_s", [8, 128, T])
    kbT_s = dscr("kbT_s", [8, 128, T])
    vb_s = dscr("vb_s", [T, 1024])
    yaT_s = dscr("yaT_s", [8, 128, T])
    ybT_s = dscr("ybT_s", [8, 128, T])
    pad_s = dscr("pad_s", [120, 160], F32)
    uT_s = dscr("uT_s", [NT, 128, 16 * 512])

    try:
        with ExitStack() as es:
            tr = Tr(nc, es)

            def ckpt(k):
                if stop == k:
                    tr.barrier()
                    raise _Stop()
            op, dma = tr.op, tr.dma
            pe, act, dve, pool = nc.tensor, nc.scalar, nc.vector, nc.gpsimd

            _uid = [0]

            def sb(name, shape, dt, stack=es):
                _uid[0] += 1
                return stack.enter_context(nc.sbuf_tensor("s%d_%s" % (_uid[0], name), list(shape), dt))

            ps = [es.enter_context(nc.psum_tensor("ps%d" % i, [128, 512], F32)) for i in range(8)]
            psb = [p.bitcast(BF16) for p in ps]
            psB = [Buf("ps%d" % i) for i in range(8)]

            ident = sb("ident", [128, 128], F32)
            identb = sb("identb", [128, 128], BF16)
            g1c = sb("g1c", [128, 16], F32)
            gmc = sb("gmc", [128, 16], F32)
            g2c = sb("g2c", [128, 16], F32)
            bgc = sb("bgc", [128, 32], F32)
            gqb = sb("gqb", [128, 128], F32)
            gkb = sb("gkb", [128, 128], F32)
            maskA = sb("maskA", [128, 4], F32)
            biasA = sb("biasA", [128, 4], F32)
            small = sb("small", [128, 8], F32)
            cB = Buf("consts")
            cds = tr.dsem()
            for t_sb, t_d in ((ident, ident_d), (g1c, g1c_d), (gmc, gmc_d), (g2c, g2c_d), (bgc, bgc_d),
                              (gqb, gqb_d), (gkb, gkb_d), (maskA, maskA_d)):
                dma("sp", cds, t_sb[:], t_d.ap(), writes=[cB])
            identbB = Buf("identb")
            op("dve", lambda: dve.tensor_copy(out=identb[:], in_=ident[:]), reads=[cB], writes=[identbB])
            smB = Buf("small")
            op("dve", lambda: dve.tensor_reduce(out=small[:, 0:1], in_=gqb[:], axis=AX.X, op=ALU.max,
                                                apply_absolute_value=True), reads=[cB], writes=[smB])
            op("dve", lambda: dve.tensor_reduce(out=small[:, 1:2], in_=gkb[:], axis=AX.X, op=ALU.max,
                                                apply_absolute_value=True), reads=[cB, smB], writes=[smB])
            op("dve", lambda: dve.tensor_tensor(out=small[:, 2:3], in0=small[:, 0:1], in1=small[:, 1:2], op=ALU.mult),
               reads=[smB], writes=[smB])
            op("dve", lambda: dve.tensor_scalar(out=small[:, 3:4], in0=small[:, 2:3], scalar1=-math.sqrt(128.0),
                                                scalar2=None, op0=ALU.mult), reads=[smB], writes=[smB])
            biasAB = Buf("biasA")
            op("dve", lambda: dve.tensor_scalar(out=biasA[:], in0=maskA[:], scalar1=small[:, 3:4], scalar2=None,
                                                op0=ALU.add), reads=[smB, cB], writes=[biasAB])
            op("dve", lambda: dve.memset(small[:, 4:5], EPS), writes=[smB])
            ckpt(0)

            def convert(src, dst, rows, cols):
                ds = tr.dsem()
                b = Buf("w")
                tok = None
                for r0 in range(0, rows, 128):
                    o = dst.ap()[r0:r0 + 128, :]
                    i = src.ap()[r0:r0 + 128, :]
                    if cols > 2048:
                        o = o.rearrange("r (a b) -> r a b", a=8)
                        i = i.rearrange("r (a b) -> r a b", a=8)
                    tok = dma("pool", ds, o, i, ndesc=(64 if cols > 2048 else 8))
                b.w = tok
                return b

            def convert_cols(src, dst, rows, bw, order):
                bufs = {}
                for a in order:
                    ds = tr.dsem()
                    tok = None
                    for r0 in range(0, rows, 128):
                        tok = dma("pool", ds, dst.ap()[r0:r0 + 128, a * bw:(a + 1) * bw],
                                  src.ap()[r0:r0 + 128, a * bw:(a + 1) * bw], ndesc=8)
                    b = Buf("w")
                    b.w = tok
                    bufs[a] = b
                return bufs

            def conv_jobs(src, dst, rows, cols, b):
                ds = tr.dsem()
                jobs = []
                for r0 in range(0, rows, 128):
                    def job(r0=r0, last=(r0 + 128 >= rows)):
                        o = dst.ap()[r0:r0 + 128, :]
                        i = src.ap()[r0:r0 + 128, :]
                        if cols > 2048:
                            o = o.rearrange("r (a b) -> r a b", a=8)
                            i = i.rearrange("r (a b) -> r a b", a=8)
                        b.w = dma("pool", ds, o, i, ndesc=(64 if cols > 2048 else 8))
                    jobs.append(job)
                return jobs

            OCT = 2 * DFF // 8
            w1a_oct = convert_cols(w1a_d, w1a_s, D, OCT, [0, 4, 1, 5, 2, 6, 3, 7])
            w1b_q = convert_cols(w1b_d, w1b_s, DFF, 512, [0, 1, 2, 3])
            winB = convert(win_d, win_s, D, INW)

            def w1a_cols(c0, c1):
                return [w1a_oct[a] for a in range(c0 // OCT, (c1 - 1) // OCT + 1)]

            def w1b_cols(c0, c1):
                return [w1b_q[a] for a in range(c0 // 512, (c1 - 1) // 512 + 1)]
            ckpt(1)
            WB = {}

            class NS:
                pass

            def mk13(stk, phase):
                n = NS()
                n.xt = sb("xt", [128, 4, D], F32, stk)
                n.xtB = [Buf("xt%d" % i) for i in range(4)]
                n.xnT = sb("xnT", [128, 16, 512], BF16, stk)
                n.xnTB = Buf("xnT")
                n.actT = sb("actT", [128, NFC, 512], BF16, stk)
                n.actTB = Buf("actT")
                wt = [sb("wr%d" % i, [128, 16 * 512], BF16, stk) for i in range(2)]
                n.wring = Ring([(wt[i], Buf("wr%d" % i), tr.dsem()) for i in range(2)])
                w2t = [sb("w2r%d" % i, [128, NFC, 256], BF16, stk) for i in range(2)]
                n.w2ring = Ring([(w2t[i], Buf("w2r%d" % i), tr.dsem()) for i in range(2)])
                xnb = [sb("xnb%d" % i, [128, D], BF16, stk) for i in range(2)]
                n.xnbR = Ring([(xnb[i], Buf("xnb%d" % i)) for i in range(2)])
                n.stat = sb("stat", [128, 16], F32, stk)
                n.statB = Buf("stat")
                n.statRB = Buf("statR")
                tmpf = [sb("tmpf%d" % i, [128, 512], F32, stk) for i in range(4)]
                n.tmpR = Ring([(tmpf[i], Buf("tmpf%d" % i)) for i in range(4)])
                n.psA = Ring([0, 1, 2, 3, 4, 5])
                n.psT = Ring([6, 7])
                n.xds = tr.dsem()
                if phase == 1:
                    stg = [sb("stg%d" % i, [128, 4, 512], BF16, stk) for i in range(2)]
                    n.stgR = Ring([(stg[i], Buf("stg%d" % i), tr.dsem()) for i in range(2)])
                    n.ropet = sb("ropet", [128, 4, 128], F32, stk)
                    n.ropeB = Buf("rope")
                    n.rtmp = [sb("rtmp%d" % i, [128, 128], F32, stk) for i in range(5)]
                    n.rtmpB = [Buf("rtmp%d" % i) for i in range(5)]
                    n.qrot = sb("qrot", [128, 4, 128], BF16, stk)
                    n.qrotB = Buf("qrot")
                else:
                    n.gfb = sb("gfb", [128, D], F32, stk)
                    n.gfbB = Buf("gfb")
                    dma("sp", cds, n.gfb[:], gfb_d.ap(), writes=[n.gfbB])
                    sgt = [sb("sgt%d" % i, [128, 512], F32, stk) for i in range(2)]
                    n.sgR = Ring([(sgt[i], Buf("sgt%d" % i)) for i in range(2)])
                return n

            def load_rows(n, src, t):
                for s in range(4):
                    dma("sp", n.xds, n.xt[:, s, :], src.ap()[t * 512 + s * 128: t * 512 + (s + 1) * 128, :],
                        writes=[n.xtB[s]])

            def row_stats(n):
                junk = n.actT[:, 0:4, :]
                for s in range(4):
                    op("act", lambda s=s: act.activation(out=junk, in_=n.xt[:, s, :].rearrange("p (a b) -> p a b", a=4),
                                                         func=AF.Square, accum_out=n.stat[:, s:s + 1]),
                       reads=[n.xtB[s]], writes=[n.actTB, n.statB])
                op("act", lambda: act.activation(out=n.stat[:, 4:8], in_=n.stat[:, 0:4], func=AF.Sqrt,
                                                 bias=small[:, 4:5], scale=1.0 / D), reads=[n.statB, smB], writes=[n.statB])
                op("dve", lambda: dve.reciprocal(out=n.stat[:, 8:12], in_=n.stat[:, 4:8]), reads=[n.statB], writes=[n.statB])

            def norm_to_T(n, gcol):
                row_stats(n)
                for half in range(2):
                    norm_half(n, gcol, half, n.xnT, n.xnTB)

            def norm_half(n, gcol, half, dst, dstB):
                if True:
                    cur = []
                    for s in (2 * half, 2 * half + 1):
                        xb, xbB = n.xnbR.next()
                        op("dve", lambda s=s, xb=xb: dve.tensor_scalar(out=xb[:], in0=n.xt[:, s, :],
                                                                      scalar1=n.stat[:, 8 + s:9 + s], scalar2=None,
                                                                      op0=ALU.mult),
                           reads=[n.xtB[s], n.statB], writes=[xbB])
                        cur.append((xb, xbB))
                    for kc in range(16):
                        b = n.psT.next()
                        for i, (xb, xbB) in enumerate(cur):
                            op("pe", lambda b=b, i=i, xb=xb, kc=kc: pe.transpose(
                                out=psb[b][:, i * 128:(i + 1) * 128], in_=xb[:, kc * 128:(kc + 1) * 128],
                                identity=identb[:]),
                               reads=[xbB, identbB], writes=[psB[b]], inc=(i == 1))
                        op("act", lambda b=b, kc=kc, half=half: act.activation(
                            out=dst[:, kc, half * 256:(half + 1) * 256], in_=psb[b][:, 0:256], func=AF.Copy,
                            scale=gcol[:, kc:kc + 1]),
                           reads=[psB[b], cB], writes=[dstB])

            def wblock(n, wsrc, wB, kch, c0, ncols=512):
                wt, wtB, wds = n.wring.next()
                view = wt[:, 0:kch * ncols].rearrange("p (k n) -> p k n", n=ncols)
                dma("sp", wds, view, wsrc.ap().rearrange("(k p) n -> p k n", p=128)[:, :, c0:c0 + ncols],
                    reads=[wB], writes=[wtB])
                return view, wtB

            def ffn(n, w_in_s, w_in_B, w_out_s, w_out_B, mid_hook=None):
                inB = w_in_B if callable(w_in_B) else (lambda c0, c1: [w_in_B])
                outB = w_out_B if callable(w_out_B) else (lambda c0, c1: [w_out_B])
                ngrp = (NFC + 1) // 2
                src = w_in_s.ap().rearrange("(k p) n -> p k n", p=128)
                for gi in range(ngrp):
                    nch = min(2, NFC - 2 * gi)
                    wt, wtB, wds = n.wring.next()
                    view = wt[:, :].rearrange("p (a k n) -> p a k n", a=2, k=16)
                    dma("sp", wds, view[:, 0, :, 0:nch * 128], src[:, :, gi * 256: gi * 256 + nch * 128],
                        reads=inB(gi * 256, gi * 256 + nch * 128), writes=[wtB])
                    dma("sp", wds, view[:, 1, :, 0:nch * 128], src[:, :, DFF + gi * 256: DFF + gi * 256 + nch * 128],
                        reads=inB(DFF + gi * 256, DFF + gi * 256 + nch * 128), writes=[wtB])
                    for c in range(nch):
                        fc = 2 * gi + c
                        bg, bu = n.psA.next(), n.psA.next()
                        for a, b in ((0, bg), (1, bu)):
                            for kc in range(16):
                                op("pe", lambda a=a, b=b, kc=kc, c=c, view=view: pe.matmul(
                                    ps[b][:, :], lhsT=view[:, a, kc, c * 128:(c + 1) * 128], rhs=n.xnT[:, kc, :],
                                    start=(kc == 0), stop=(kc == 15)),
                                   reads=[wtB, n.xnTB], writes=[psB[b]], inc=(kc == 15))
                        tf, tfB = n.tmpR.next()
                        op("act", lambda bg=bg, tf=tf: act.activation(out=tf[:], in_=ps[bg][:, :], func=AF.Silu),
                           reads=[psB[bg]], writes=[tfB])
                        op("dve", lambda bu=bu, tf=tf, fc=fc: dve.tensor_tensor(out=n.actT[:, fc, :], in0=tf[:],
                                                                                in1=ps[bu][:, :], op=ALU.mult),
                           reads=[tfB, psB[bu]], writes=[n.actTB])
                if mid_hook is not None:
                    mid_hook()
                srco = w_out_s.ap().rearrange("(k p) n -> p k n", p=128)
                for db in range(8):
                    w2t, w2B, w2ds = n.w2ring.next()
                    dma("sp", w2ds, w2t[:], srco[:, :, db * 256:(db + 1) * 256], reads=outB(db * 256, (db + 1) * 256),
                        writes=[w2B])
                    for s in range(4):
                        b = n.psA.next()
                        for fc in range(NFC):
                            op("pe", lambda b=b, fc=fc, s=s, w2t=w2t: pe.matmul(
                                ps[b][:, 0:256], lhsT=n.actT[:, fc, s * 128:(s + 1) * 128], rhs=w2t[:, fc, :],
                                start=(fc == 0), stop=(fc == NFC - 1)),
                               reads=[n.actTB, w2B], writes=[psB[b]], inc=(fc == NFC - 1))
                        op("dve", lambda b=b, s=s, db=db: dve.scalar_tensor_tensor(
                            out=n.xt[:, s, db * 256:(db + 1) * 256], in0=ps[b][:, 0:256], scalar=0.5,
                            in1=n.xt[:, s, db * 256:(db + 1) * 256], op0=ALU.mult, op1=ALU.add),
                           reads=[psB[b]], writes=[n.xtB[s]])

            with ExitStack() as p1:
                n = mk13(p1, 1)
                hds = tr.dsem()
                rds = tr.dsem()
                uds = tr.dsem()
                noB = Buf("dram")

                def tok_major_mm(b, s, wv, wvB):
                    for kc in range(16):
                        op("pe", lambda kc=kc: pe.matmul(ps[b][:, :], lhsT=n.xnT[:, kc, s * 128:(s + 1) * 128],
                                                         rhs=wv[:, kc, :], start=(kc == 0), stop=(kc == 15)),
                           reads=[n.xnTB, wvB], writes=[psB[b]], inc=(kc == 15))

                qrots = [sb("qrot%d" % i, [128, 4, 128], BF16, p1) for i in range(3)]
                qrotR = Ring([(qrots[i], Buf("qrot%d" % i)) for i in range(3)])

                def qk_chain(b, s, nh, gb_t):
                    tf, tfB = n.tmpR.next()
                    op("act", lambda: act.activation(out=tf[:, :], in_=ps[b][:, :], func=AF.Square),
                       reads=[psB[b]], writes=[tfB])
                    op("dve", lambda: dve.tensor_reduce(out=n.stat[:, 12:12 + nh],
                                                        in_=tf[:, 0:nh * 128].rearrange("p (h d) -> p h d", d=128),
                                                        axis=AX.X, op=ALU.add), reads=[tfB], writes=[n.statRB])
                    op("act", lambda: act.activation(out=n.stat[:, 12:12 + nh], in_=n.stat[:, 12:12 + nh], func=AF.Sqrt,
                                                     bias=small[:, 4:5], scale=1.0 / 128), reads=[n.statRB, smB],
                       writes=[n.statRB])
                    op("dve", lambda: dve.reciprocal(out=n.stat[:, 12:12 + nh], in_=n.stat[:, 12:12 + nh]),
                       reads=[n.statRB], writes=[n.statRB])
                    cs = n.ropet[:, s, 0:64]
                    sn = n.ropet[:, s, 64:128]
                    rt, rtB = n.rtmp, n.rtmpB
                    qr, qrB = qrotR.next()
                    for i in range(nh):
                        op("dve", lambda i=i: dve.scalar_tensor_tensor(
                            out=rt[0][:], in0=ps[b][:, i * 128:(i + 1) * 128], scalar=n.stat[:, 12 + i:13 + i],
                            in1=gb_t[:], op0=ALU.mult, op1=ALU.mult),
                           reads=[psB[b], n.statRB, cB], writes=[rtB[0]])
                        x0 = rt[0][:, 0:128:2]
                        x1 = rt[0][:, 1:128:2]
                        op("dve", lambda: dve.tensor_tensor(out=rt[1][:, 0:64], in0=x0, in1=cs, op=ALU.mult),
                           reads=[rtB[0], n.ropeB], writes=[rtB[1]])
                        op("dve", lambda: dve.tensor_tensor(out=rt[2][:, 0:64], in0=x1, in1=sn, op=ALU.mult),
                           reads=[rtB[0], n.ropeB], writes=[rtB[2]])
                        op("dve", lambda i=i: dve.tensor_tensor(out=qr[:, i, 0:128:2], in0=rt[1][:, 0:64],
                                                                in1=rt[2][:, 0:64], op=ALU.subtract),
                           reads=[rtB[1], rtB[2]], writes=[qrB])
                        op("pool", lambda: pool.tensor_tensor(out=rt[3][:, 0:64], in0=x0, in1=sn, op=ALU.mult),
                           reads=[rtB[0], n.ropeB], writes=[rtB[3]])
                        op("pool", lambda: pool.tensor_tensor(out=rt[4][:, 0:64], in0=x1, in1=cs, op=ALU.mult),
                           reads=[rtB[0], n.ropeB], writes=[rtB[4]])
                        op("pool", lambda i=i: pool.tensor_tensor(out=qr[:, i, 1:128:2], in0=rt[3][:, 0:64],
                                                                  in1=rt[4][:, 0:64], op=ALU.add),
                           reads=[rtB[3], rtB[4]], writes=[qrB])
                    return qr, qrB

                def wslot(slot, c0):
                    wt, wtB, wds = n.wring.items[slot]
                    view = wt[:, 0:16 * 512].rearrange("p (k n) -> p k n", n=512)
                    dma("sp", wds, view, win_s.ap().rearrange("(k p) n -> p k n", p=128)[:, :, c0:c0 + 512],
                        reads=[winB], writes=[wtB])
                    return view, wtB

                RBLK = [O_QA, O_QA + 512, O_KA]
                FBLK = [(O_QB, qbT_s, 0, "f"), (O_QB + 512, qbT_s, 1, "f"), (O_KB, kbT_s, 0, "f"),
                        (O_KB + 512, kbT_s, 1, "f"), (O_VB, vb_s, 0, "t"), (O_VB + 512, vb_s, 1, "t")]
                stFv = [n.actT[:, 4:8, :], n.actT[:, 8:12, :]]
                stFds = [tr.dsem(), tr.dsem()]
                stVv = n.actT[:, 12:16, :]
                stVds = tr.dsem()

                xpv = n.actT[:, 16:32, :]
                for t in range(NT):
                    if t == 0:
                        load_rows(n, x_d, t)
                        norm_to_T(n, g1c)
                    else:
                        hx = 16 * 512 // 2
                        xflat = n.xnT[:, :, :].rearrange("p a b -> p (a b)")
                        pflat = xpv.rearrange("p a b -> p (a b)")
                        op("dve", lambda: dve.tensor_copy(out=xflat[:, 0:hx], in_=pflat[:, 0:hx]),
                           reads=[xpB], writes=[n.xnTB])
                        op("act", lambda: act.activation(out=xflat[:, hx:2 * hx], in_=pflat[:, hx:2 * hx], func=AF.Copy),
                           reads=[xpB, n.xnTB], writes=[n.xnTB])
                        n.actTB.inherit([xpB])
                    dma("sp", rds, n.ropet[:], rope_d.ap()[t * 512:(t + 1) * 512, :].rearrange("(s p) c -> p s c", p=128),
                        writes=[n.ropeB])
                    ckpt(2)
                    ffn(n, w1a_s, w1a_cols, w1b_s, w1b_cols)
                    ckpt(3)
                    for s_ in range(4):
                        dma("pool", hds, h_s.ap()[t * 512 + s_ * 128: t * 512 + (s_ + 1) * 128, :], n.xt[:, s_, :],
                            reads=[n.xtB[s_]], writes=[noB], ndesc=8)
                    norm_to_T(n, gmc)
                    dma("pool", uds, uT_s.ap()[t].rearrange("p (k n) -> p k n", n=512), n.xnT[:],
                        reads=[n.xnTB], writes=[noB])
                    tsl = slice(t * 512, (t + 1) * 512)
                    xpB = Buf("xp")
                    xpB.inherit([n.actTB])
                    stFB = [Buf("stF0"), Buf("stF1")]
                    stVB = Buf("stV")
                    for b_ in stFB + [stVB]:
                        b_.inherit([n.actTB])
                    rstate = {}
                    Rw = {0: wslot(0, RBLK[0])}
                    Fw = {0: wslot(1, FBLK[0][0])}
                    if t + 1 < NT:
                        load_rows(n, x_d, t + 1)
                        row_stats(n)
                    Rst = {}

                    def R_mm(k):
                        blk, s = k // 4, k % 4
                        wv, wvB = Rw[blk]
                        if s == 0:
                            Rst[blk] = [n.stgR.next()] + ([(stVv, stVB, stVds)] if blk == 2 else [])
                        b = n.psA.next()
                        tok_major_mm(b, s, wv, wvB)
                        if s == 3 and blk + 1 < 3:
                            Rw[blk + 1] = wslot(0, RBLK[blk + 1])
                        if blk == 2:
                            st2, st2B, _ = Rst[blk][1]
                            op("dve", lambda: dve.tensor_copy(out=st2[:, s, 0:256], in_=ps[b][:, 256:512]),
                               reads=[psB[b]], writes=[st2B])
                        nh = 2 if blk == 2 else 4
                        rstate[k] = (qk_chain(b, s, nh, gkb if blk == 2 else gqb), nh)

                    def R_tail(k):
                        blk, s = k // 4, k % 4
                        (qr, qrB), nh = rstate.pop(k)
                        st, stB, sds = Rst[blk][0]
                        bt = n.psT.next()
                        for i in range(nh):
                            op("pe", lambda i=i: pe.transpose(out=psb[bt][:, i * 128:(i + 1) * 128], in_=qr[:, i, :],
                                                              identity=identb[:]),
                               reads=[qrB, identbB], writes=[psB[bt]], inc=(i == nh - 1))
                        op("act", lambda: act.activation(
                            out=st[:, 0:nh, s * 128:(s + 1) * 128],
                            in_=psb[bt][:, 0:nh * 128].rearrange("p (h t) -> p h t", h=nh), func=AF.Copy),
                           reads=[psB[bt]], writes=[stB])
                        if s == 3:
                            if blk < 2:
                                dma("pool", sds, qaT_s.ap()[blk * 4:(blk + 1) * 4, :, tsl].rearrange("h d t -> d h t"),
                                    st[:], reads=[stB], writes=[noB])
                            else:
                                st2, st2B, sds2 = Rst[blk][1]
                                dma("pool", sds, kaT_s.ap()[:, :, tsl].rearrange("h d t -> d h t"), st[:, 0:2, :],
                                    reads=[stB], writes=[noB])
                                dma("pool", sds2, va_s.ap()[tsl, :].rearrange("(s p) c -> p s c", p=128),
                                    st2[:, :, 0:256], reads=[st2B], writes=[noB])

                    def F_unit(m):
                        blk, u = m // 4, m % 4
                        c0, dst, half, kind = FBLK[blk]
                        wv, wvB = Fw[blk]
                        stv, stB_, sds_ = stFv[blk % 2], stFB[blk % 2], stFds[blk % 2]
                        b = n.psA.next()
                        if kind == "f":
                            for kc in range(16):
                                op("pe", lambda kc=kc: pe.matmul(
                                    ps[b][:, :], lhsT=wv[:, kc, u * 128:(u + 1) * 128], rhs=n.xnT[:, kc, :],
                                    start=(kc == 0), stop=(kc == 15)),
                                   reads=[n.xnTB, wvB], writes=[psB[b]], inc=(kc == 15))
                        else:
                            tok_major_mm(b, u, wv, wvB)
                        if u == 3 and blk + 1 < 6:
                            Fw[blk + 1] = wslot(1, FBLK[blk + 1][0])
                        if u % 2 == 0:
                            op("act", lambda: act.activation(out=stv[:, u, :], in_=ps[b][:, :], func=AF.Copy),
                               reads=[psB[b]], writes=[stB_])
                        else:
                            op("dve", lambda: dve.tensor_copy(out=stv[:, u, :], in_=ps[b][:, :]),
                               reads=[psB[b]], writes=[stB_])
                        if u == 3:
                            if kind == "f":
                                dma("pool", sds_, dst.ap()[half * 4:(half + 1) * 4, :, tsl].rearrange("h d t -> d h t"),
                                    stv, reads=[stB_], writes=[noB])
                            else:
                                dma("pool", sds_,
                                    dst.ap()[tsl, half * 512:(half + 1) * 512].rearrange("(s p) c -> p s c", p=128),
                                    stv, reads=[stB_], writes=[noB])

                    for k in range(12):
                        R_mm(k)
                        F_unit(2 * k)
                        F_unit(2 * k + 1)
                        if k >= 1:
                            R_tail(k - 1)
                        if t + 1 < NT and k in (5, 8):
                            norm_half(n, g1c, 0 if k == 5 else 1, xpv, xpB)
                    R_tail(11)
                    n.actTB.inherit(stFB + [stVB])
                    ckpt(372)
                    if t == 0:
                        jobs = []
                        for nm, (sd, ss, rr, cc) in (("wa", (wa_d, wa_s, 1024, D)), ("wb", (wb_d, wb_s, 1024, D)),
                                                     ("wo", (wo_d, wo_s, D, D)), ("w2a", (w2a_d, w2a_s, D, 2 * DFF)),
                                                     ("w2b", (w2b_d, w2b_s, DFF, D))):
                            WB[nm] = Buf("w")
                            jobs += conv_jobs(sd, ss, rr, cc, WB[nm])
                    per = (len(jobs) + max(NT - 1, 1) - 1) // max(NT - 1, 1) if t < NT - 1 else len(jobs)
                    for _ in range(min(per, len(jobs))):
                        jobs.pop(0)()
                tr.barrier()
                ckpt(4)

            with ExitStack() as p2:
                ads = [tr.dsem() for _ in range(8)]
                zt = sb("zt", [120, 160], F32, p2)
                ztB = Buf("zt")
                padB = Buf("pad")
                op("dve", lambda: dve.memset(zt[:], 0.0), writes=[ztB])
                dma("pool", ads[0], pad_s.ap(), zt[:], reads=[ztB], writes=[padB])
                dma("pool", ads[0], pad_s.ap()[:, 64:95], rpb_d.ap(), reads=[], writes=[padB])
                BS = sb("BS", [128, 8, 14 * 64], F32, p2)
                BSB = Buf("BS")
                bstok = None
                for ql in range(2):
                    for qc in range(64):
                        p = ql * 64 + qc
                        src = bass.AP(pad_s, (1 - ql) * 160 + 79 - qc, [[1, 1], [2400, 8], [160, 14], [1, 64]])
                        tr.wait("pool", [padB.w])
                        bstok = dma("pool", ads[1], BS[p:p + 1, :, :].rearrange("p h (r c) -> p h r c", c=64), src, ndesc=16)
                BSB.w = bstok
                maskB = sb("maskB", [128, 9, 896], F32, p2)
                maskBB = Buf("maskB")
                ckpt(45)

                KT = sb("KT", [128, T], BF16, p2)
                KTB = Buf("KT")
                Vg = sb("Vg", [128, NCH, 129], BF16, p2)
                VgB = Buf("Vg")
                op("dve", lambda: dve.memset(Vg[:, :, 128:129], 1.0), writes=[VgB])
                QTt = [sb("QT%d" % i, [128, 4, 512], BF16, p2) for i in range(2)]
                QTR = Ring([(QTt[i], Buf("QT%d" % i), ads[3 + i]) for i in range(2)])
                PTt = [sb("PT%d" % i, [128, 512], BF16, p2) for i in range(3)]
                PTR = Ring([(PTt[i], Buf("PT%d" % i)) for i in range(3)])
                ynt = [sb("yn%d" % i, [128, 128], BF16, p2) for i in range(2)]
                ynR = Ring([(ynt[i], Buf("yn%d" % i)) for i in range(2)])
                ystt = [sb("yst%d" % i, [128, 512], BF16, p2) for i in range(2)]
                ystR = Ring([(ystt[i], Buf("yst%d" % i), ads[5 + i]) for i in range(2)])
                rinv = sb("rinv", [128, 4], F32, p2)
                rinvB = Buf("rinv")
                SR = Ring([4, 5, 6])
                noB = Buf("dram2")
                for g in range(2):
                    dma("sp", ads[7], KT[:], kaT_s.ap()[g, :, :], writes=[KTB])
                    dma("sp", ads[7], Vg[:, :, 0:128],
                        va_s.ap()[:, g * 128:(g + 1) * 128].rearrange("(c p) d -> p c d", p=128), writes=[VgB])
                    items = [(qt, hl, kc) for qt in range(NT) for hl in range(4) for kc in range(NCH)]
                    qts = {}

                    def getQ(qt):
                        if qt not in qts:
                            q_, qB_, qds_ = QTR.next()
                            dma("sp", qds_, q_[:],
                                qaT_s.ap()[g * 4:(g + 1) * 4, :, qt * 512:(qt + 1) * 512].rearrange("h d t -> d h t"),
                                writes=[qB_])
                            qts[qt] = (q_, qB_)
                        return qts[qt]

                    sbank = {}

                    def emitS(idx):
                        qt, hl, kc = items[idx]
                        q_, qB_ = getQ(qt)
                        b = SR.next()
                        sbank[idx] = b
                        op("pe", lambda: pe.matmul(ps[b][:, :], lhsT=KT[:, kc * 128:(kc + 1) * 128], rhs=q_[:, hl, :],
                                                   start=True, stop=True), reads=[KTB, qB_], writes=[psB[b]])

                    emitS(0)
                    emitS(1)
                    if g == 0:
                        dma("sp", ads[2], maskB[:], maskB_d.ap().rearrange("p (t c) -> p t c", c=896), writes=[maskBB])
                    for idx, (qt, hl, kc) in enumerate(items):
                        if idx + 2 < len(items):
                            emitS(idx + 2)
                        b = sbank.pop(idx)
                        mi = (2 if kc >= NCH // 2 else 0) + (1 if qt >= NT // 2 else 0)
                        pt, ptB = PTR.next()
                        op("act", lambda: act.activation(out=pt[:], in_=ps[b][:, :], func=AF.Exp,
                                                         bias=biasA[:, mi:mi + 1], scale=SCALE),
                           reads=[psB[b], biasAB], writes=[ptB])
                        for qs in range(4):
                            op("pe", lambda qs=qs: pe.matmul(ps[qs][:, 0:129], lhsT=pt[:, qs * 128:(qs + 1) * 128],
                                                             rhs=Vg[:, kc, :], start=(kc == 0), stop=(kc == NCH - 1)),
                               reads=[ptB, VgB], writes=[psB[qs]], inc=(qs == 3))
                        if kc == NCH - 1:
                            h = g * 4 + hl
                            yst, ystB, yds = ystR.next()
                            for qs in range(4):
                                op("dve", lambda qs=qs: dve.reciprocal(out=rinv[:, qs:qs + 1], in_=ps[qs][:, 128:129]),
                                   reads=[psB[qs]], writes=[rinvB])
                                yn, ynB = ynR.next()
                                op("dve", lambda qs=qs, yn=yn: dve.tensor_scalar(out=yn[:], in0=ps[qs][:, 0:128],
                                                                                 scalar1=rinv[:, qs:qs + 1], scalar2=None,
                                                                                 op0=ALU.mult),
                                   reads=[psB[qs], rinvB], writes=[ynB])
                                op("pe", lambda qs=qs, yn=yn: pe.transpose(out=psb[7][:, qs * 128:(qs + 1) * 128],
                                                                           in_=yn[:], identity=identb[:]),
                                   reads=[ynB, identbB], writes=[psB[7]])
                            op("act", lambda yst=yst: act.activation(out=yst[:], in_=psb[7][:, 0:512], func=AF.Copy),
                               reads=[psB[7]], writes=[ystB])
                            dma("pool", yds, yaT_s.ap()[h, :, qt * 512:(qt + 1) * 512], yst[:], reads=[ystB], writes=[noB])

                ckpt(5)
                master = sb("master", [128, 8, 896], F32, p2)
                masterB = Buf("master")
                for h in range(8):
                    for (bk, d0, nd) in ((4, 0, 4), (5, 4, 3)):
                        for i in range(nd):
                            di = d0 + i
                            op("pe", lambda i=i, di=di, bk=bk: pe.transpose(
                                out=ps[bk][:, i * 128:(i + 1) * 128], in_=BS[:, h, di * 128:(di + 1) * 128],
                                identity=ident[:]),
                               reads=[BSB, cB], writes=[psB[bk]], inc=(i == nd - 1))
                        op("dve", lambda bk=bk, d0=d0, nd=nd: dve.tensor_copy(
                            out=master[:, h, d0 * 128:(d0 + nd) * 128], in_=ps[bk][:, 0:nd * 128]),
                           reads=[psB[bk]], writes=[masterB])
                combt = [sb("comb%d" % i, [128, 896], F32, p2) for i in range(2)]
                combBs = [Buf("comb%d" % i) for i in range(2)]

                def build_comb(h):
                    op("pool", lambda: pool.tensor_tensor(out=combt[h % 2][:], in0=master[:, h, :], in1=maskB[:, 2, :],
                                                          op=ALU.add),
                       reads=[masterB, maskBB], writes=[combBs[h % 2]])
                KTb = sb("KTb", [128, T], BF16, p2)
                QTb = sb("QTb", [128, T], BF16, p2)
                Vb = sb("Vb", [128, NCH, 129], BF16, p2)
                QTb2 = sb("QTb2", [128, T], BF16, p2)
                vinitB = Buf("vinit")
                op("dve", lambda: dve.memset(Vb[:, :, 128:129], 1.0), writes=[vinitB])
                sets = [(KTb, QTb, Vb, [Buf("k0"), Buf("q0"), Buf("v0")], [tr.dsem(), tr.dsem(), tr.dsem()]),
                        (KT, QTb2, Vg, [Buf("k1"), Buf("q1"), Buf("v1")], [tr.dsem(), tr.dsem(), tr.dsem()])]
                sets[0][3][2].w = vinitB.w
                sets[1][3][0].inherit([KTB])
                sets[1][3][2].inherit([VgB])

                def loadset(h):
                    k_, q_, v_, bs_, ds_ = sets[h % 2]
                    dma("sp", ds_[0], k_[:], kbT_s.ap()[h, :, :], writes=[bs_[0]])
                    dma("sp", ds_[1], q_[:], qbT_s.ap()[h, :, :], writes=[bs_[1]])
                    dma("sp", ds_[2], v_[:, :, 0:128],
                        vb_s.ap()[:, h * 128:(h + 1) * 128].rearrange("(c p) d -> p c d", p=128), writes=[bs_[2]])

                sTt = [sb("sT%d" % i, [128, 896], F32, p2) for i in range(2)]
                sTR = Ring([(sTt[i], Buf("sT%d" % i)) for i in range(2)])
                PBt = [sb("PB%d" % i, [128, 896], BF16, p2) for i in range(3)]
                PBR = Ring([(PBt[i], Buf("PB%d" % i)) for i in range(3)])
                ynb = [sb("ynb%d" % i, [128, 128], BF16, p2) for i in range(3)]
                ynbR = Ring([(ynb[i], Buf("ynb%d" % i)) for i in range(3)])
                rinvb = sb("rinvb", [128, 4], F32, p2)
                rinvbB = Buf("rinvb")
                SBR = Ring([(0, 1), (2, 3)])
                ACR = Ring([4, 5])
                TBR = Ring([6, 7])

                def ti_of(j):
                    if j == 0:
                        return 0
                    if j == 1:
                        return 1
                    if NBLK // 2 - 2 <= j <= NBLK // 2 + 1:
                        return 3 + j - (NBLK // 2 - 2)
                    if j == NBLK - 2:
                        return 7
                    if j == NBLK - 1:
                        return 8
                    return 2

                loadset(0)
                build_comb(0)
                for h in range(8):
                    if h + 1 < 8:
                        loadset(h + 1)
                        build_comb(h + 1)
                    comb, combB = combt[h % 2], combBs[h % 2]
                    KTh, QTh, Vh, (kB_, qB_, vB_), _ = sets[h % 2]
                    st_ = {}

                    def stA(j):
                        ba, bb = SBR.next()
                        st_[j] = {"s": (ba, bb)}
                        for di in range(7):
                            c = min(max(j - 3 + di, 0), NCH - 1)
                            bk, off = (ba, di * 128) if di < 4 else (bb, (di - 4) * 128)
                            op("pe", lambda c=c, bk=bk, off=off: pe.matmul(
                                ps[bk][:, off:off + 128], lhsT=KTh[:, c * 128:(c + 1) * 128],
                                rhs=QTh[:, j * 128:(j + 1) * 128], start=True, stop=True),
                               reads=[kB_, qB_], writes=[psB[bk]], inc=(di == 3 or di == 6))

                    def stB(j):
                        ba, bb = st_[j]["s"]
                        ti = ti_of(j)
                        sT, sTB = sTR.next()
                        if ti == 2:
                            bsrc, bsB = comb, combB
                        else:
                            bsrc, bsB = master[:, h, :], masterB
                        op("dve", lambda: dve.scalar_tensor_tensor(out=sT[:, 0:512], in0=ps[ba][:, :], scalar=SCALE,
                                                                   in1=bsrc[:, 0:512], op0=ALU.mult, op1=ALU.add),
                           reads=[psB[ba], bsB], writes=[sTB])
                        op("dve", lambda: dve.scalar_tensor_tensor(out=sT[:, 512:896], in0=ps[bb][:, 0:384],
                                                                   scalar=SCALE, in1=bsrc[:, 512:896],
                                                                   op0=ALU.mult, op1=ALU.add),
                           reads=[psB[bb], bsB], writes=[sTB])
                        if ti != 2:
                            op("pool", lambda: pool.tensor_tensor(out=sT[:], in0=sT[:], in1=maskB[:, ti, :], op=ALU.add),
                               reads=[maskBB], writes=[sTB])
                        pb, pbB = PBR.next()
                        op("act", lambda: act.activation(out=pb[:], in_=sT[:], func=AF.Exp), reads=[sTB], writes=[pbB])
                        st_[j]["p"] = (pb, pbB)

                    def stC(j):
                        pb, pbB = st_[j]["p"]
                        ac = ACR.next()
                        st_[j]["ac"] = ac
                        for di in range(7):
                            c = min(max(j - 3 + di, 0), NCH - 1)
                            op("pe", lambda di=di, c=c: pe.matmul(ps[ac][:, 0:129], lhsT=pb[:, di * 128:(di + 1) * 128],
                                                                  rhs=Vh[:, c, :], start=(di == 0), stop=(di == 6)),
                               reads=[pbB, vB_], writes=[psB[ac]], inc=(di == 6))

                    def stD(j):
                        ac = st_[j]["ac"]
                        op("dve", lambda: dve.reciprocal(out=rinvb[:, j % 4:j % 4 + 1], in_=ps[ac][:, 128:129]),
                           reads=[psB[ac]], writes=[rinvbB])
                        yn, ynB = ynbR.next()
                        op("dve", lambda: dve.tensor_scalar(out=yn[:], in0=ps[ac][:, 0:128],
                                                            scalar1=rinvb[:, j % 4:j % 4 + 1], scalar2=None, op0=ALU.mult),
                           reads=[psB[ac], rinvbB], writes=[ynB])
                        st_[j]["yn"] = (yn, ynB)

                    tbs = {}

                    def stE(j):
                        yn, ynB = st_[j]["yn"]
                        if j % 4 == 0:
                            tbs[j // 4] = TBR.next()
                        tb = tbs[j // 4]
                        op("pe", lambda: pe.transpose(out=psb[tb][:, (j % 4) * 128:(j % 4 + 1) * 128], in_=yn[:],
                                                      identity=identb[:]), reads=[ynB, identbB], writes=[psB[tb]])
                        if j % 4 == 3:
                            yst, ystB, yds = ystR.next()
                            op("act", lambda: act.activation(out=yst[:], in_=psb[tb][:, 0:512], func=AF.Copy),
                               reads=[psB[tb]], writes=[ystB])
                            dma("pool", yds, ybT_s.ap()[h, :, (j - 3) * 128:(j + 1) * 128], yst[:], reads=[ystB],
                                writes=[noB], ndesc=8)
                        del st_[j]

                    for i in range(-2, NBLK + 2):
                        if 0 <= i + 2 < NBLK:
                            stA(i + 2)
                        if 0 <= i + 1 < NBLK:
                            stB(i + 1)
                        if 0 <= i < NBLK:
                            stC(i)
                        if 0 <= i - 1 < NBLK:
                            stD(i - 1)
                        if 0 <= i - 2 < NBLK:
                            stE(i - 2)
                tr.barrier()
                ckpt(6)

            with ExitStack() as p3:
                n = mk13(p3, 3)
                mT = n.actT[:, 0:16, :]
                yaT = n.actT[:, 16:24, :]
                ybT = n.actT[:, 24:32, :]
                yds = tr.dsem()
                ods = tr.dsem()
                noB = Buf("dram3")
                out_toks = []
                uds3 = tr.dsem()

                def prefetch_u(t):
                    dma("sp", uds3, n.xnT[:], uT_s.ap()[t].rearrange("p (k n) -> p k n", n=512), writes=[n.xnTB])

                def gslot(slot, c0):
                    wt, wtB, wds = n.wring.items[slot]
                    view = wt[:, 0:16 * 512].rearrange("p (k n) -> p k n", n=512)
                    dma("sp", wds, view, win_s.ap().rearrange("(k p) n -> p k n", p=128)[:, :, c0:c0 + 512],
                        reads=[winB], writes=[wtB])
                    return view, wtB

                def abslot(fb):
                    w2t, w2B, w2ds = n.w2ring.items[fb % 2]
                    view = w2t[:, :, :].rearrange("p a b -> p (a b)")[:, 0:8192].rearrange("p (r k n) -> p r k n", r=2, k=8)
                    for r_, (wsrc, wk) in enumerate(((wa_s, "wa"), (wb_s, "wb"))):
                        dma("sp", w2ds, view[:, r_, :, :],
                            wsrc.ap().rearrange("(k p) n -> p k n", p=128)[:, :, fb * 512:(fb + 1) * 512],
                            reads=[WB[wk]], writes=[w2B])
                    return view, w2B

                gpre = {}

                def issue_first(t):
                    gpre[t] = {"ab": {0: abslot(0)}, "ga": {0: gslot(0, O_GA)}, "gb": {0: gslot(1, O_GB)}}

                prefetch_u(0)
                issue_first(0)
                for t in range(NT):
                    tsl = slice(t * 512, (t + 1) * 512)
                    dma("sp", yds, yaT, yaT_s.ap()[:, :, tsl].rearrange("h d t -> d h t"), writes=[n.actTB])
                    dma("sp", yds, ybT, ybT_s.ap()[:, :, tsl].rearrange("h d t -> d h t"), writes=[n.actTB])
                    yB = Buf("y")
                    yB.w = n.actTB.w
                    mB = Buf("m")
                    mB.inherit([n.actTB])
                    G = gpre.pop(t)
                    for fb in range(4):
                        if fb + 1 < 4:
                            G["ab"][fb + 1] = abslot(fb + 1)
                        abv, abB = G["ab"][fb]
                        tA = []
                        for br in range(2):
                            yv = (yaT, ybT)[br]
                            wg, wgB = G[("ga", "gb")[br]][fb]
                            for fc in range(4):
                                ch = fb * 4 + fc
                                bz, bgt = n.psA.next(), n.psA.next()
                                for kc in range(16):
                                    op("pe", lambda kc=kc, fc=fc: pe.matmul(
                                        ps[bgt][:, :], lhsT=wg[:, kc, fc * 128:(fc + 1) * 128], rhs=n.xnT[:, kc, :],
                                        start=(kc == 0), stop=(kc == 15)),
                                       reads=[wgB, n.xnTB], writes=[psB[bgt]], inc=(kc == 15))
                                for h in range(8):
                                    op("pe", lambda h=h, fc=fc: pe.matmul(
                                        ps[bz][:, :], lhsT=abv[:, br, h, fc * 128:(fc + 1) * 128], rhs=yv[:, h, :],
                                        start=(h == 0), stop=(h == 7)),
                                       reads=[abB, yB], writes=[psB[bz]], inc=(h == 7))
                                if br == 0:
                                    tf, tfB = n.tmpR.next()
                                    tA.append((tf, tfB))
                                else:
                                    tf, tfB = n.sgR.next()
                                op("act", lambda: act.activation(out=tf[:], in_=ps[bgt][:, :], func=AF.Sigmoid,
                                                                 bias=bgc[:, br * 16 + ch: br * 16 + ch + 1]),
                                   reads=[psB[bgt], cB], writes=[tfB])
                                op("dve", lambda: dve.tensor_tensor(out=tf[:], in0=tf[:], in1=ps[bz][:, :], op=ALU.mult),
                                   reads=[tfB, psB[bz]], writes=[tfB])
                                if br == 1:
                                    ta, taB = tA[fc]
                                    op("pool", lambda: pool.tensor_tensor(out=mT[:, ch, :], in0=ta[:], in1=tf[:],
                                                                          op=ALU.add),
                                       reads=[taB, tfB], writes=[mB])
                            if fb + 1 < 4:
                                key = ("ga", "gb")[br]
                                G[key][fb + 1] = gslot(br, (O_GA, O_GB)[br] + (fb + 1) * 512)
                    load_rows(n, h_s, t)
                    for db in range(4):
                        wv, wvB = wblock(n, wo_s, WB["wo"], 16, db * 512)
                        for s in range(4):
                            b = n.psA.next()
                            for kc in range(16):
                                op("pe", lambda kc=kc, s=s: pe.matmul(ps[b][:, :], lhsT=mT[:, kc, s * 128:(s + 1) * 128],
                                                                      rhs=wv[:, kc, :], start=(kc == 0), stop=(kc == 15)),
                                   reads=[mB, wvB], writes=[psB[b]], inc=(kc == 15))
                            op("dve", lambda b=b, s=s, db=db: dve.tensor_tensor(
                                out=n.xt[:, s, db * 512:(db + 1) * 512], in0=ps[b][:, :],
                                in1=n.xt[:, s, db * 512:(db + 1) * 512], op=ALU.add),
                               reads=[psB[b]], writes=[n.xtB[s]])
                    n.actTB.inherit([yB, mB])
                    norm_to_T(n, g2c)
                    ffn(n, w2a_s, WB["w2a"], w2b_s, WB["w2b"],
                        mid_hook=((lambda t=t: prefetch_u(t + 1)) if t + 1 < NT else None))
                    if t + 1 < NT:
                        issue_first(t + 1)
                    row_stats(n)
                    for s in range(4):
                        eng_, e_ = ("dve", dve)
                        op(eng_, lambda s=s, e_=e_: e_.scalar_tensor_tensor(out=n.xt[:, s, :], in0=n.xt[:, s, :],
                                                                            scalar=n.stat[:, 8 + s:9 + s], in1=n.gfb[:],
                                                                            op0=ALU.mult, op1=ALU.mult),
                           reads=[n.statB, n.gfbB], writes=[n.xtB[s]])
                        out_toks.append(dma("pool", ods, out_d.ap()[t * 512 + s * 128: t * 512 + (s + 1) * 128, :],
                                            n.xt[:, s, :], reads=[n.xtB[s]], writes=[noB], ndesc=8))
                tr.barrier()
    except _Stop:
        pass
    return nc


def _geometry(T, mode):
    tok = np.arange(T)
    if mode == "prompt":
        return np.zeros(T, np.int64), tok, T // GRID_W
    half = T // 2
    return tok // half, tok % half, half // GRID_W


def _rope_table(T, mode):
    _, pos, _ = _geometry(T, mode)
    row = (pos // GRID_W).astype(np.float32)
    col = (pos % GRID_W).astype(np.float32)
    npairs = HD // 4
    inv = (10000.0 ** (-np.arange(npairs, dtype=np.float32) / npairs)).astype(np.float32)
    ang = np.concatenate([row[:, None] * inv[None], col[:, None] * inv[None]], axis=-1)
    return np.concatenate([np.cos(ang), np.sin(ang)], axis=-1).astype(np.float32)


def _ti_blocks(nblk):
    reps = [0, 1, None] + [nblk // 2 - 2 + i for i in range(4)] + [nblk - 2, nblk - 1]
    special = set(r for r in reps if r is not None)
    interior = [j for j in range(nblk) if j not in special]
    reps[2] = interior[0] if interior else None
    return reps


def _mask_b(T, mode):
    seq, pos, rows = _geometry(T, mode)
    nblk = T // 128
    r_of = pos // GRID_W
    c_of = pos % GRID_W
    rs = np.clip(r_of - 4, 0, rows - 8)
    cs = np.clip(c_of - 8, 0, GRID_W - 16)
    out = np.full((128, 9, 7, 128), NEG, np.float32)
    for ti, j in enumerate(_ti_blocks(nblk)):
        if j is None:
            continue
        q = j * 128 + np.arange(128)
        for di in range(7):
            c = j - 3 + di
            if c < 0 or c >= nblk:
                continue
            k = c * 128 + np.arange(128)
            ok = (seq[k][:, None] == seq[q][None, :])
            ok &= (r_of[k][:, None] >= rs[q][None, :]) & (r_of[k][:, None] < rs[q][None, :] + 8)
            ok &= (c_of[k][:, None] >= cs[q][None, :]) & (c_of[k][:, None] < cs[q][None, :] + 16)
            out[:, ti, di, :] = np.where(ok, 0.0, NEG)
    return out.reshape(128, 9 * 896)


def _mask_a(mode):
    m = np.zeros((128, 4), np.float32)
    if mode != "prompt":
        m[:, 1] = NEG
        m[:, 2] = NEG
    return m


def _shared_inputs(g_ffn1, w_ffn1_in, w_ffn1_out, g_mix, w_in, b_gate, g_q_a, g_k_a, rpb_b, w_branch_a,
                   w_branch_b, w_out, g_ffn2, w_ffn2_in, w_ffn2_out, g_final):
    f = lambda a: np.ascontiguousarray(np.asarray(a, dtype=np.float32))
    col = lambda g: f(np.asarray(g).reshape(-1, 128).T)
    return {
        "ident": np.eye(128, dtype=np.float32),
        "g1c": col(g_ffn1), "gmc": col(g_mix), "g2c": col(g_ffn2),
        "gfb": f(np.broadcast_to(np.asarray(g_final).reshape(1, D), (128, D))),
        "bgc": col(b_gate),
        "gqb": f(np.broadcast_to(np.asarray(g_q_a).reshape(1, 128), (128, 128))),
        "gkb": f(np.broadcast_to(np.asarray(g_k_a).reshape(1, 128), (128, 128))),
        "rpb": f(np.asarray(rpb_b).reshape(120, 31)),
        "w1a": f(np.asarray(w_ffn1_in)[0]), "w1b": f(np.asarray(w_ffn1_out)[0]),
        "win": f(np.asarray(w_in)[0]), "wa": f(np.asarray(w_branch_a)[0]), "wb": f(np.asarray(w_branch_b)[0]),
        "wo": f(np.asarray(w_out)[0]), "w2a": f(np.asarray(w_ffn2_in)[0]), "w2b": f(np.asarray(w_ffn2_out)[0]),
    }


def _slot_inputs(T, mode, x):
    return {"x": np.ascontiguousarray(x, dtype=np.float32), "rope": _rope_table(T, mode),
            "maskA": _mask_a(mode), "maskB": _mask_b(T, mode)}


_NC_CACHE = {}


def kernel(x_prompt, x_sample, g_ffn1, w_ffn1_in, w_ffn1_out, g_mix, w_in, b_gate, g_q_a, g_k_a,
           rpb_b, w_branch_a, w_branch_b, w_out, g_ffn2, w_ffn2_in, w_ffn2_out, g_final):
    x_prompt = np.asarray(x_prompt, dtype=np.float32)
    x_sample = np.asarray(x_sample, dtype=np.float32)
    T = x_prompt.shape[1]
    assert x_prompt.shape[0] == 2 and x_sample.shape[0] == 8 and x_sample.shape[1] * 2 == T
    shared = _shared_inputs(g_ffn1, w_ffn1_in, w_ffn1_out, g_mix, w_in, b_gate, g_q_a, g_k_a, rpb_b,
                            w_branch_a, w_branch_b, w_out, g_ffn2, w_ffn2_in, w_ffn2_out, g_final)
    in_maps = []
    for c in range(8):
        if c < 2:
            m = _slot_inputs(T, "prompt", x_prompt[c])
        elif c < 6:
            i = c - 2
            m = _slot_inputs(T, "sample", np.concatenate([x_sample[2 * i], x_sample[2 * i + 1]], axis=0))
        else:
            m = _slot_inputs(T, "prompt", np.zeros((T, D), np.float32))
        m.update(shared)
        in_maps.append(m)
    if T not in _NC_CACHE:
        _NC_CACHE[T] = build_nc(T)
    res = run_bass_kernel_spmd(_NC_CACHE[T], in_maps, core_ids=list(range(8)))
    outs = [np.asarray(r["out"], dtype=np.float32) for r in res.results]
    y_prompt = np.stack([outs[0], outs[1]], axis=0)
    half = T // 2
    y_sample = np.stack([outs[2 + i // 2][(i % 2) * half:(i % 2 + 1) * half] for i in range(8)], axis=0)
    return (y_prompt, y_sample)
```

```python
import math
from contextlib import ExitStack

import numpy as np

import concourse.bass as bass
import concourse.mybir as mybir
from concourse.bass_utils import run_bass_kernel_spmd

F32 = mybir.dt.float32
BF16 = mybir.dt.bfloat16
AF = mybir.ActivationFunctionType
ALU = mybir.AluOpType
AX = mybir.AxisListType

D = 2048
DFF = 5504
NFC = DFF // 128
HD = 128
INW = 8704
EPS = 1e-6
GRID_W = 64
NEG = -30000.0
SCALE = HD ** -0.5

O_QA, O_KA, O_VA, O_QB, O_KB, O_VB, O_GA, O_GB = 0, 1024, 1280, 1536, 2560, 3584, 4608, 6656


class Tok:
    __slots__ = ("key", "sem", "val")

    def __init__(self, key, sem, val):
        self.key, self.sem, self.val = key, sem, val


class Buf:
    def __init__(self, name=""):
        self.name = name
        self.w = None
        self.r = {}

    def inherit(self, others):
        for o in others:
            if o.w is not None:
                self.r[("w", o.w.key)] = o.w if (("w", o.w.key) not in self.r or self.r[("w", o.w.key)].val < o.w.val) else self.r[("w", o.w.key)]
            for k, t in o.r.items():
                if k not in self.r or self.r[k].val < t.val:
                    self.r[k] = t


class DSem:
    def __init__(self, sem, idx):
        self.sem, self.val, self.key = sem, 0, ("d", idx)


class Tr:
    def __init__(self, nc, es):
        self.nc = nc
        self.es = es
        self.engs = {"pe": nc.tensor, "act": nc.scalar, "dve": nc.vector, "pool": nc.gpsimd, "sp": nc.sync}
        self.sem = {k: es.enter_context(nc.semaphore("c_" + k)) for k in ("pe", "act", "dve", "pool")}
        self.cnt = {k: 0 for k in self.sem}
        self.seen = {}
        self.nds = 0
        self.nwait = 0
        self.pend = {k: ([], []) for k in self.sem}
        self.dsems = []
        self.pool_fifo = []
        self.pool_out = []

    def dsem(self):
        self.nds += 1
        d = DSem(self.es.enter_context(self.nc.semaphore("d%d" % self.nds)), self.nds)
        self.dsems.append(d)
        return d

    def barrier(self):
        toks = [Tok(k, self.sem[k], self.cnt[k]) for k in self.sem if self.cnt[k] > 0]
        toks += [Tok(d.key, d.sem, d.val) for d in self.dsems if d.val > 0]
        for e in ("pe", "act", "dve", "pool", "sp"):
            self.wait(e, toks)

    def wait(self, eng, toks):
        for t in toks:
            if t is None:
                continue
            if eng == "pe" and t.key == "pe":
                continue
            k = (eng, t.key)
            if self.seen.get(k, 0) < t.val:
                self.engs[eng].wait_ge(t.sem, t.val)
                self.seen[k] = t.val
                self.nwait += 1

    @staticmethod
    def deps(reads, writes):
        d = []
        for b in reads:
            d.append(b.w)
            if b.name.startswith("ps"):
                d.extend(b.r.values())
        for b in writes:
            d.append(b.w)
            d.extend(b.r.values())
        return d

    @staticmethod
    def mark(tok, reads, writes):
        for b in writes:
            b.w = tok
            b.r = {}
        for b in reads:
            o = b.r.get(tok.key)
            if o is None or o.val < tok.val:
                b.r[tok.key] = tok

    def op(self, eng, fn, reads=(), writes=(), inc=True, extra=()):
        self.wait(eng, self.deps(reads, writes) + list(extra))
        ins = fn()
        if not inc:
            self.pend[eng][0].extend(reads)
            self.pend[eng][1].extend(writes)
            return None
        self.cnt[eng] += 1
        ins.then_inc(self.sem[eng], 1)
        tok = Tok(eng, self.sem[eng], self.cnt[eng])
        pr, pw = self.pend[eng]
        self.mark(tok, list(reads) + pr, list(writes) + pw)
        self.pend[eng] = ([], [])
        return tok

    def dma(self, q, ds, out, in_, reads=(), writes=(), extra=(), ndesc=32, **kw):
        self.wait(q, self.deps(reads, writes) + list(extra))
        if q == "pool":
            self.pool_out.append(None)
            while sum(n_ for _, n_ in self.pool_fifo) + ndesc > 480 and self.pool_fifo:
                t_, _ = self.pool_fifo.pop(0)
                self.wait("pool", [t_])
            self.pool_out.pop()
        ins = self.engs[q].dma_start(out=out, in_=in_, **kw)
        ds.val += 16
        ins.then_inc(ds.sem, 16)
        tok = Tok(ds.key, ds.sem, ds.val)
        self.mark(tok, reads, writes)
        if q == "pool":
            self.pool_fifo.append((tok, ndesc))
        return tok


class Ring:
    def __init__(self, items):
        self.items = items
        self.i = 0

    def next(self):
        it = self.items[self.i % len(self.items)]
        self.i += 1
        return it


class _Stop(Exception):
    pass


def build_nc(T, stop=None):
    NT = T // 512
    NCH = T // 128
    NBLK = NCH
    nc = bass.Bass("TRN2", target_bir_lowering=False)

    def din(name, shape, dt=F32):
        return nc.dram_tensor(name, list(shape), dt, kind="ExternalInput")

    x_d = din("x", [T, D])
    rope_d = din("rope", [T, 128])
    maskA_d = din("maskA", [128, 4])
    maskB_d = din("maskB", [128, 9 * 896])
    ident_d = din("ident", [128, 128])
    g1c_d = din("g1c", [128, 16])
    gmc_d = din("gmc", [128, 16])
    g2c_d = din("g2c", [128, 16])
    gfb_d = din("gfb", [128, D])
    bgc_d = din("bgc", [128, 32])
    gqb_d = din("gqb", [128, 128])
    gkb_d = din("gkb", [128, 128])
    rpb_d = din("rpb", [120, 31])
    w1a_d = din("w1a", [D, 2 * DFF])
    w1b_d = din("w1b", [DFF, D])
    win_d = din("win", [D, INW])
    wa_d = din("wa", [1024, D])
    wb_d = din("wb", [1024, D])
    wo_d = din("wo", [D, D])
    w2a_d = din("w2a", [D, 2 * DFF])
    w2b_d = din("w2b", [DFF, D])
    out_d = nc.dram_tensor("out", [T, D], F32, kind="ExternalOutput")

    def dscr(name, shape, dt=BF16):
        return nc.dram_tensor(name, list(shape), dt)

    w1a_s = dscr("w1a_s", [D, 2 * DFF])
    w1b_s = dscr("w1b_s", [DFF, D])
    win_s = dscr("win_s", [D, INW])
    wa_s = dscr("wa_s", [1024, D])
    wb_s = dscr("wb_s", [1024, D])
    wo_s = dscr("wo_s", [D, D])
    w2a_s = dscr("w2a_s", [D, 2 * DFF])
    w2b_s = dscr("w2b_s", [DFF, D])
    h_s = dscr("h_s", [T, D], F32)
    qaT_s = dscr("qaT_s", [8, 128, T])
    kaT_s = dscr("kaT_s", [2, 128, T])
    va_s = dscr("va_s", [T, 256])
    qbT_s = dscr("qbT_s", [8, 128, T])
    kbT_s = dscr("kbT_s", [8, 128, T])
    vb_s = dscr("vb_s", [T, 1024])
    yaT_s = dscr("yaT_s", [8, 128, T])
    ybT_s = dscr("ybT_s", [8, 128, T])
    pad_s = dscr("pad_s", [120, 160], F32)
    uT_s = dscr("uT_s", [NT, 128, 16 * 512])

    try:
        with ExitStack() as es:
            tr = Tr(nc, es)

            def ckpt(k):
                if stop == k:
                    tr.barrier()
                    raise _Stop()
            op, dma = tr.op, tr.dma
            pe, act, dve, pool = nc.tensor, nc.scalar, nc.vector, nc.gpsimd

            _uid = [0]

            def sb(name, shape, dt, stack=es):
                _uid[0] += 1
                return stack.enter_context(nc.sbuf_tensor("s%d_%s" % (_uid[0], name), list(shape), dt))

            ps = [es.enter_context(nc.psum_tensor("ps%d" % i, [128, 512], F32)) for i in range(8)]
            psb = [p.bitcast(BF16) for p in ps]
            psB = [Buf("ps%d" % i) for i in range(8)]

            ident = sb("ident", [128, 128], F32)
            identb = sb("identb", [128, 128], BF16)
            g1c = sb("g1c", [128, 16], F32)
            gmc = sb("gmc", [128, 16], F32)
            g2c = sb("g2c", [128, 16], F32)
            bgc = sb("bgc", [128, 32], F32)
            gqb = sb("gqb", [128, 128], F32)
            gkb = sb("gkb", [128, 128], F32)
            maskA = sb("maskA", [128, 4], F32)
            biasA = sb("biasA", [128, 4], F32)
            small = sb("small", [128, 8], F32)
            cB = Buf("consts")
            cds = tr.dsem()
            for t_sb, t_d in ((ident, ident_d), (g1c, g1c_d), (gmc, gmc_d), (g2c, g2c_d), (bgc, bgc_d),
                              (gqb, gqb_d), (gkb, gkb_d), (maskA, maskA_d)):
                dma("sp", cds, t_sb[:], t_d.ap(), writes=[cB])
            identbB = Buf("identb")
            op("dve", lambda: dve.tensor_copy(out=identb[:], in_=ident[:]), reads=[cB], writes=[identbB])
            smB = Buf("small")
            op("dve", lambda: dve.tensor_reduce(out=small[:, 0:1], in_=gqb[:], axis=AX.X, op=ALU.max,
                                                apply_absolute_value=True), reads=[cB], writes=[smB])
            op("dve", lambda: dve.tensor_reduce(out=small[:, 1:2], in_=gkb[:], axis=AX.X, op=ALU.max,
                                                apply_absolute_value=True), reads=[cB, smB], writes=[smB])
            op("dve", lambda: dve.tensor_tensor(out=small[:, 2:3], in0=small[:, 0:1], in1=small[:, 1:2], op=ALU.mult),
               reads=[smB], writes=[smB])
            op("dve", lambda: dve.tensor_scalar(out=small[:, 3:4], in0=small[:, 2:3], scalar1=-math.sqrt(128.0),
                                                scalar2=None, op0=ALU.mult), reads=[smB], writes=[smB])
            biasAB = Buf("biasA")
            op("dve", lambda: dve.tensor_scalar(out=biasA[:], in0=maskA[:], scalar1=small[:, 3:4], scalar2=None,
                                                op0=ALU.add), reads=[smB, cB], writes=[biasAB])
            op("dve", lambda: dve.memset(small[:, 4:5], EPS), writes=[smB])
            ckpt(0)

            def convert(src, dst, rows, cols):
                ds = tr.dsem()
                b = Buf("w")
                tok = None
                for r0 in range(0, rows, 128):
                    o = dst.ap()[r0:r0 + 128, :]
                    i = src.ap()[r0:r0 + 128, :]
                    if cols > 2048:
                        o = o.rearrange("r (a b) -> r a b", a=8)
                        i = i.rearrange("r (a b) -> r a b", a=8)
                    tok = dma("pool", ds, o, i, ndesc=(64 if cols > 2048 else 8))
                b.w = tok
                return b

            def convert_cols(src, dst, rows, bw, order):
                bufs = {}
                for a in order:
                    ds = tr.dsem()
                    tok = None
                    for r0 in range(0, rows, 128):
                        tok = dma("pool", ds, dst.ap()[r0:r0 + 128, a * bw:(a + 1) * bw],
                                  src.ap()[r0:r0 + 128, a * bw:(a + 1) * bw], ndesc=8)
                    b = Buf("w")
                    b.w = tok
                    bufs[a] = b
                return bufs

            def conv_jobs(src, dst, rows, cols, b):
                ds = tr.dsem()
                jobs = []
                for r0 in range(0, rows, 128):
                    def job(r0=r0, last=(r0 + 128 >= rows)):
                        o = dst.ap()[r0:r0 + 128, :]
                        i = src.ap()[r0:r0 + 128, :]
                        if cols > 2048:
                            o = o.rearrange("r (a b) -> r a b", a=8)
                            i = i.rearrange("r (a b) -> r a b", a=8)
                        b.w = dma("pool", ds, o, i, ndesc=(64 if cols > 2048 else 8))
                    jobs.append(job)
                return jobs

            OCT = 2 * DFF // 8
            w1a_oct = convert_cols(w1a_d, w1a_s, D, OCT, [0, 4, 1, 5, 2, 6, 3, 7])
            w1b_q = convert_cols(w1b_d, w1b_s, DFF, 512, [0, 1, 2, 3])
            winB = convert(win_d, win_s, D, INW)

            def w1a_cols(c0, c1):
                return [w1a_oct[a] for a in range(c0 // OCT, (c1 - 1) // OCT + 1)]

            def w1b_cols(c0, c1):
                return [w1b_q[a] for a in range(c0 // 512, (c1 - 1) // 512 + 1)]
            ckpt(1)
            WB = {}

            class NS:
                pass

            def mk13(stk, phase):
                n = NS()
                n.xt = sb("xt", [128, 4, D], F32, stk)
                n.xtB = [Buf("xt%d" % i) for i in range(4)]
                n.xnT = sb("xnT", [128, 16, 512], BF16, stk)
                n.xnTB = Buf("xnT")
                n.actT = sb("actT", [128, NFC, 512], BF16, stk)
                n.actTB = Buf("actT")
                wt = [sb("wr%d" % i, [128, 16 * 512], BF16, stk) for i in range(2)]
                n.wring = Ring([(wt[i], Buf("wr%d" % i), tr.dsem()) for i in range(2)])
                w2t = [sb("w2r%d" % i, [128, NFC, 256], BF16, stk) for i in range(2)]
                n.w2ring = Ring([(w2t[i], Buf("w2r%d" % i), tr.dsem()) for i in range(2)])
                xnb = [sb("xnb%d" % i, [128, D], BF16, stk) for i in range(2)]
                n.xnbR = Ring([(xnb[i], Buf("xnb%d" % i)) for i in range(2)])
                n.stat = sb("stat", [128, 16], F32, stk)
                n.statB = Buf("stat")
                tmpf = [sb("tmpf%d" % i, [128, 512], F32, stk) for i in range(4)]
                n.tmpR = Ring([(tmpf[i], Buf("tmpf%d" % i)) for i in range(4)])
                n.psA = Ring([0, 1, 2, 3, 4, 5])
                n.psT = Ring([6, 7])
                n.xds = tr.dsem()
                if phase == 1:
                    stg = [sb("stg%d" % i, [128, 4, 512], BF16, stk) for i in range(2)]
                    n.stgR = Ring([(stg[i], Buf("stg%d" % i), tr.dsem()) for i in range(2)])
                    n.ropet = sb("ropet", [128, 4, 128], F32, stk)
                    n.ropeB = Buf("rope")
                    n.rtmp = [sb("rtmp%d" % i, [128, 128], F32, stk) for i in range(5)]
                    n.rtmpB = [Buf("rtmp%d" % i) for i in range(5)]
                    n.qrot = sb("qrot", [128, 4, 128], BF16, stk)
                    n.qrotB = Buf("qrot")
                else:
                    n.gfb = sb("gfb", [128, D], F32, stk)
                    n.gfbB = Buf("gfb")
                    dma("sp", cds, n.gfb[:], gfb_d.ap(), writes=[n.gfbB])
                    sgt = [sb("sgt%d" % i, [128, 512], F32, stk) for i in range(2)]
                    n.sgR = Ring([(sgt[i], Buf("sgt%d" % i)) for i in range(2)])
                return n

            def load_rows(n, src, t):
                for s in range(4):
                    dma("sp", n.xds, n.xt[:, s, :], src.ap()[t * 512 + s * 128: t * 512 + (s + 1) * 128, :],
                        writes=[n.xtB[s]])

            def row_stats(n):
                junk = n.actT[:, 0:4, :]
                for s in range(4):
                    op("act", lambda s=s: act.activation(out=junk, in_=n.xt[:, s, :].rearrange("p (a b) -> p a b", a=4),
                                                         func=AF.Square, accum_out=n.stat[:, s:s + 1]),
                       reads=[n.xtB[s]], writes=[n.actTB, n.statB])
                op("act", lambda: act.activation(out=n.stat[:, 4:8], in_=n.stat[:, 0:4], func=AF.Sqrt,
                                                 bias=small[:, 4:5], scale=1.0 / D), reads=[n.statB, smB], writes=[n.statB])
                op("dve", lambda: dve.reciprocal(out=n.stat[:, 8:12], in_=n.stat[:, 4:8]), reads=[n.statB], writes=[n.statB])

            def norm_to_T(n, gcol):
                row_stats(n)
                for half in range(2):
                    cur = []
                    for s in (2 * half, 2 * half + 1):
                        xb, xbB = n.xnbR.next()
                        op("dve", lambda s=s, xb=xb: dve.tensor_scalar(out=xb[:], in0=n.xt[:, s, :],
                                                                      scalar1=n.stat[:, 8 + s:9 + s], scalar2=None,
                                                                      op0=ALU.mult),
                           reads=[n.xtB[s], n.statB], writes=[xbB])
                        cur.append((xb, xbB))
                    for kc in range(16):
                        b = n.psT.next()
                        for i, (xb, xbB) in enumerate(cur):
                            op("pe", lambda b=b, i=i, xb=xb, kc=kc: pe.transpose(
                                out=psb[b][:, i * 128:(i + 1) * 128], in_=xb[:, kc * 128:(kc + 1) * 128],
                                identity=identb[:]),
                               reads=[xbB, identbB], writes=[psB[b]], inc=(i == 1))
                        op("act", lambda b=b, kc=kc, half=half: act.activation(
                            out=n.xnT[:, kc, half * 256:(half + 1) * 256], in_=psb[b][:, 0:256], func=AF.Copy,
                            scale=gcol[:, kc:kc + 1]),
                           reads=[psB[b], cB], writes=[n.xnTB])

            def wblock(n, wsrc, wB, kch, c0, ncols=512):
                wt, wtB, wds = n.wring.next()
                view = wt[:, 0:kch * ncols].rearrange("p (k n) -> p k n", n=ncols)
                dma("sp", wds, view, wsrc.ap().rearrange("(k p) n -> p k n", p=128)[:, :, c0:c0 + ncols],
                    reads=[wB], writes=[wtB])
                return view, wtB

            def ffn(n, w_in_s, w_in_B, w_out_s, w_out_B, mid_hook=None):
                inB = w_in_B if callable(w_in_B) else (lambda c0, c1: [w_in_B])
                outB = w_out_B if callable(w_out_B) else (lambda c0, c1: [w_out_B])
                ngrp = (NFC + 1) // 2
                src = w_in_s.ap().rearrange("(k p) n -> p k n", p=128)
                for gi in range(ngrp):
                    nch = min(2, NFC - 2 * gi)
                    wt, wtB, wds = n.wring.next()
                    view = wt[:, :].rearrange("p (a k n) -> p a k n", a=2, k=16)
                    dma("sp", wds, view[:, 0, :, 0:nch * 128], src[:, :, gi * 256: gi * 256 + nch * 128],
                        reads=inB(gi * 256, gi * 256 + nch * 128), writes=[wtB])
                    dma("sp", wds, view[:, 1, :, 0:nch * 128], src[:, :, DFF + gi * 256: DFF + gi * 256 + nch * 128],
                        reads=inB(DFF + gi * 256, DFF + gi * 256 + nch * 128), writes=[wtB])
                    for c in range(nch):
                        fc = 2 * gi + c
                        bg, bu = n.psA.next(), n.psA.next()
                        for a, b in ((0, bg), (1, bu)):
                            for kc in range(16):
                                op("pe", lambda a=a, b=b, kc=kc, c=c, view=view: pe.matmul(
                                    ps[b][:, :], lhsT=view[:, a, kc, c * 128:(c + 1) * 128], rhs=n.xnT[:, kc, :],
                                    start=(kc == 0), stop=(kc == 15)),
                                   reads=[wtB, n.xnTB], writes=[psB[b]], inc=(kc == 15))
                        tf, tfB = n.tmpR.next()
                        op("act", lambda bg=bg, tf=tf: act.activation(out=tf[:], in_=ps[bg][:, :], func=AF.Silu),
                           reads=[psB[bg]], writes=[tfB])
                        op("dve", lambda bu=bu, tf=tf, fc=fc: dve.tensor_tensor(out=n.actT[:, fc, :], in0=tf[:],
                                                                                in1=ps[bu][:, :], op=ALU.mult),
                           reads=[tfB, psB[bu]], writes=[n.actTB])
                if mid_hook is not None:
                    mid_hook()
                srco = w_out_s.ap().rearrange("(k p) n -> p k n", p=128)
                for db in range(8):
                    w2t, w2B, w2ds = n.w2ring.next()
                    dma("sp", w2ds, w2t[:], srco[:, :, db * 256:(db + 1) * 256], reads=outB(db * 256, (db + 1) * 256),
                        writes=[w2B])
                    for s in range(4):
                        b = n.psA.next()
                        for fc in range(NFC):
                            op("pe", lambda b=b, fc=fc, s=s, w2t=w2t: pe.matmul(
                                ps[b][:, 0:256], lhsT=n.actT[:, fc, s * 128:(s + 1) * 128], rhs=w2t[:, fc, :],
                                start=(fc == 0), stop=(fc == NFC - 1)),
                               reads=[n.actTB, w2B], writes=[psB[b]], inc=(fc == NFC - 1))
                        op("dve", lambda b=b, s=s, db=db: dve.scalar_tensor_tensor(
                            out=n.xt[:, s, db * 256:(db + 1) * 256], in0=ps[b][:, 0:256], scalar=0.5,
                            in1=n.xt[:, s, db * 256:(db + 1) * 256], op0=ALU.mult, op1=ALU.add),
                           reads=[psB[b]], writes=[n.xtB[s]])

            with ExitStack() as p1:
                n = mk13(p1, 1)
                hds = tr.dsem()
                rds = tr.dsem()
                uds = tr.dsem()
                noB = Buf("dram")

                def tok_major_mm(b, s, wv, wvB):
                    for kc in range(16):
                        op("pe", lambda kc=kc: pe.matmul(ps[b][:, :], lhsT=n.xnT[:, kc, s * 128:(s + 1) * 128],
                                                         rhs=wv[:, kc, :], start=(kc == 0), stop=(kc == 15)),
                           reads=[n.xnTB, wvB], writes=[psB[b]], inc=(kc == 15))

                qrots = [sb("qrot%d" % i, [128, 4, 128], BF16, p1) for i in range(3)]
                qrotR = Ring([(qrots[i], Buf("qrot%d" % i)) for i in range(3)])

                def qk_chain(b, s, nh, gb_t):
                    tf, tfB = n.tmpR.next()
                    op("act", lambda: act.activation(out=tf[:, :], in_=ps[b][:, :], func=AF.Square),
                       reads=[psB[b]], writes=[tfB])
                    op("dve", lambda: dve.tensor_reduce(out=n.stat[:, 12:12 + nh],
                                                        in_=tf[:, 0:nh * 128].rearrange("p (h d) -> p h d", d=128),
                                                        axis=AX.X, op=ALU.add), reads=[tfB], writes=[n.statB])
                    op("act", lambda: act.activation(out=n.stat[:, 12:12 + nh], in_=n.stat[:, 12:12 + nh], func=AF.Sqrt,
                                                     bias=small[:, 4:5], scale=1.0 / 128), reads=[n.statB, smB],
                       writes=[n.statB])
                    op("dve", lambda: dve.reciprocal(out=n.stat[:, 12:12 + nh], in_=n.stat[:, 12:12 + nh]),
                       reads=[n.statB], writes=[n.statB])
                    cs = n.ropet[:, s, 0:64]
                    sn = n.ropet[:, s, 64:128]
                    rt, rtB = n.rtmp, n.rtmpB
                    qr, qrB = qrotR.next()
                    for i in range(nh):
                        op("dve", lambda i=i: dve.scalar_tensor_tensor(
                            out=rt[0][:], in0=ps[b][:, i * 128:(i + 1) * 128], scalar=n.stat[:, 12 + i:13 + i],
                            in1=gb_t[:], op0=ALU.mult, op1=ALU.mult),
                           reads=[psB[b], n.statB, cB], writes=[rtB[0]])
                        x0 = rt[0][:, 0:128:2]
                        x1 = rt[0][:, 1:128:2]
                        op("dve", lambda: dve.tensor_tensor(out=rt[1][:, 0:64], in0=x0, in1=cs, op=ALU.mult),
                           reads=[rtB[0], n.ropeB], writes=[rtB[1]])
                        op("dve", lambda: dve.tensor_tensor(out=rt[2][:, 0:64], in0=x1, in1=sn, op=ALU.mult),
                           reads=[rtB[0], n.ropeB], writes=[rtB[2]])
                        op("dve", lambda i=i: dve.tensor_tensor(out=qr[:, i, 0:128:2], in0=rt[1][:, 0:64],
                                                                in1=rt[2][:, 0:64], op=ALU.subtract),
                           reads=[rtB[1], rtB[2]], writes=[qrB])
                        op("pool", lambda: pool.tensor_tensor(out=rt[3][:, 0:64], in0=x0, in1=sn, op=ALU.mult),
                           reads=[rtB[0], n.ropeB], writes=[rtB[3]])
                        op("pool", lambda: pool.tensor_tensor(out=rt[4][:, 0:64], in0=x1, in1=cs, op=ALU.mult),
                           reads=[rtB[0], n.ropeB], writes=[rtB[4]])
                        op("pool", lambda i=i: pool.tensor_tensor(out=qr[:, i, 1:128:2], in0=rt[3][:, 0:64],
                                                                  in1=rt[4][:, 0:64], op=ALU.add),
                           reads=[rtB[3], rtB[4]], writes=[qrB])
                    return qr, qrB

                def wslot(slot, c0):
                    wt, wtB, wds = n.wring.items[slot]
                    view = wt[:, 0:16 * 512].rearrange("p (k n) -> p k n", n=512)
                    dma("sp", wds, view, win_s.ap().rearrange("(k p) n -> p k n", p=128)[:, :, c0:c0 + 512],
                        reads=[winB], writes=[wtB])
                    return view, wtB

                RBLK = [O_QA, O_QA + 512, O_KA]
                FBLK = [(O_QB, qbT_s, 0, "f"), (O_QB + 512, qbT_s, 1, "f"), (O_KB, kbT_s, 0, "f"),
                        (O_KB + 512, kbT_s, 1, "f"), (O_VB, vb_s, 0, "t"), (O_VB + 512, vb_s, 1, "t")]
                stFv = [n.actT[:, 4:8, :], n.actT[:, 8:12, :]]
                stFds = [tr.dsem(), tr.dsem()]
                stVv = n.actT[:, 12:16, :]
                stVds = tr.dsem()

                for t in range(NT):
                    load_rows(n, x_d, t)
                    dma("sp", rds, n.ropet[:], rope_d.ap()[t * 512:(t + 1) * 512, :].rearrange("(s p) c -> p s c", p=128),
                        writes=[n.ropeB])
                    norm_to_T(n, g1c)
                    ckpt(2)
                    ffn(n, w1a_s, w1a_cols, w1b_s, w1b_cols)
                    ckpt(3)
                    for s_ in range(4):
                        dma("pool", hds, h_s.ap()[t * 512 + s_ * 128: t * 512 + (s_ + 1) * 128, :], n.xt[:, s_, :],
                            reads=[n.xtB[s_]], writes=[noB], ndesc=8)
                    norm_to_T(n, gmc)
                    dma("pool", uds, uT_s.ap()[t].rearrange("p (k n) -> p k n", n=512), n.xnT[:],
                        reads=[n.xnTB], writes=[noB], ndesc=128)
                    tsl = slice(t * 512, (t + 1) * 512)
                    stFB = [Buf("stF0"), Buf("stF1")]
                    stVB = Buf("stV")
                    for b_ in stFB + [stVB]:
                        b_.inherit([n.actTB])
                    rstate = {}
                    Rw = {0: wslot(0, RBLK[0])}
                    Fw = {0: wslot(1, FBLK[0][0])}
                    Rst = {}

                    def R_mm(k):
                        blk, s = k // 4, k % 4
                        wv, wvB = Rw[blk]
                        if s == 0:
                            Rst[blk] = [n.stgR.next()] + ([(stVv, stVB, stVds)] if blk == 2 else [])
                        b = n.psA.next()
                        tok_major_mm(b, s, wv, wvB)
                        if s == 3 and blk + 1 < 3:
                            Rw[blk + 1] = wslot(0, RBLK[blk + 1])
                        if blk == 2:
                            st2, st2B, _ = Rst[blk][1]
                            op("dve", lambda: dve.tensor_copy(out=st2[:, s, 0:256], in_=ps[b][:, 256:512]),
                               reads=[psB[b]], writes=[st2B])
                        nh = 2 if blk == 2 else 4
                        rstate[k] = (qk_chain(b, s, nh, gkb if blk == 2 else gqb), nh)

                    def R_tail(k):
                        blk, s = k // 4, k % 4
                        (qr, qrB), nh = rstate.pop(k)
                        st, stB, sds = Rst[blk][0]
                        bt = n.psT.next()
                        for i in range(nh):
                            op("pe", lambda i=i: pe.transpose(out=psb[bt][:, i * 128:(i + 1) * 128], in_=qr[:, i, :],
                                                              identity=identb[:]),
                               reads=[qrB, identbB], writes=[psB[bt]], inc=(i == nh - 1))
                        op("act", lambda: act.activation(
                            out=st[:, 0:nh, s * 128:(s + 1) * 128],
                            in_=psb[bt][:, 0:nh * 128].rearrange("p (h t) -> p h t", h=nh), func=AF.Copy),
                           reads=[psB[bt]], writes=[stB])
                        if s == 3:
                            if blk < 2:
                                dma("pool", sds, qaT_s.ap()[blk * 4:(blk + 1) * 4, :, tsl].rearrange("h d t -> d h t"),
                                    st[:], reads=[stB], writes=[noB])
                            else:
                                st2, st2B, sds2 = Rst[blk][1]
                                dma("pool", sds, kaT_s.ap()[:, :, tsl].rearrange("h d t -> d h t"), st[:, 0:2, :],
                                    reads=[stB], writes=[noB])
                                dma("pool", sds2, va_s.ap()[tsl, :].rearrange("(s p) c -> p s c", p=128),
                                    st2[:, :, 0:256], reads=[st2B], writes=[noB])

                    def F_unit(m):
                        blk, u = m // 4, m % 4
                        c0, dst, half, kind = FBLK[blk]
                        wv, wvB = Fw[blk]
                        stv, stB_, sds_ = stFv[blk % 2], stFB[blk % 2], stFds[blk % 2]
                        b = n.psA.next()
                        if kind == "f":
                            for kc in range(16):
                                op("pe", lambda kc=kc: pe.matmul(
                                    ps[b][:, :], lhsT=wv[:, kc, u * 128:(u + 1) * 128], rhs=n.xnT[:, kc, :],
                                    start=(kc == 0), stop=(kc == 15)),
                                   reads=[n.xnTB, wvB], writes=[psB[b]], inc=(kc == 15))
                        else:
                            tok_major_mm(b, u, wv, wvB)
                        if u == 3 and blk + 1 < 6:
                            Fw[blk + 1] = wslot(1, FBLK[blk + 1][0])
                        if u % 2 == 0:
                            op("act", lambda: act.activation(out=stv[:, u, :], in_=ps[b][:, :], func=AF.Copy),
                               reads=[psB[b]], writes=[stB_])
                        else:
                            op("dve", lambda: dve.tensor_copy(out=stv[:, u, :], in_=ps[b][:, :]),
                               reads=[psB[b]], writes=[stB_])
                        if u == 3:
                            if kind == "f":
                                dma("pool", sds_, dst.ap()[half * 4:(half + 1) * 4, :, tsl].rearrange("h d t -> d h t"),
                                    stv, reads=[stB_], writes=[noB])
                            else:
                                dma("pool", sds_,
                                    dst.ap()[tsl, half * 512:(half + 1) * 512].rearrange("(s p) c -> p s c", p=128),
                                    stv, reads=[stB_], writes=[noB])

                    for k in range(12):
                        R_mm(k)
                        F_unit(2 * k)
                        F_unit(2 * k + 1)
                        if k >= 1:
                            R_tail(k - 1)
                    R_tail(11)
                    n.actTB.inherit(stFB + [stVB])
                    ckpt(372)
                    if t == 0:
                        jobs = []
                        for nm, (sd, ss, rr, cc) in (("wa", (wa_d, wa_s, 1024, D)), ("wb", (wb_d, wb_s, 1024, D)),
                                                     ("wo", (wo_d, wo_s, D, D)), ("w2a", (w2a_d, w2a_s, D, 2 * DFF)),
                                                     ("w2b", (w2b_d, w2b_s, DFF, D))):
                            WB[nm] = Buf("w")
                            jobs += conv_jobs(sd, ss, rr, cc, WB[nm])
                    per = (len(jobs) + max(NT - 1, 1) - 1) // max(NT - 1, 1) if t < NT - 1 else len(jobs)
                    for _ in range(min(per, len(jobs))):
                        jobs.pop(0)()
                tr.barrier()
                ckpt(4)

            with ExitStack() as p2:
                ads = [tr.dsem() for _ in range(8)]
                zt = sb("zt", [120, 160], F32, p2)
                ztB = Buf("zt")
                padB = Buf("pad")
                op("dve", lambda: dve.memset(zt[:], 0.0), writes=[ztB])
                dma("pool", ads[0], pad_s.ap(), zt[:], reads=[ztB], writes=[padB])
                dma("pool", ads[0], pad_s.ap()[:, 64:95], rpb_d.ap(), reads=[], writes=[padB])
                BS = sb("BS", [128, 8, 14 * 64], F32, p2)
                BSB = Buf("BS")
                bstok = None
                for ql in range(2):
                    for qc in range(64):
                        p = ql * 64 + qc
                        src = bass.AP(pad_s, (1 - ql) * 160 + 79 - qc, [[1, 1], [2400, 8], [160, 14], [1, 64]])
                        tr.wait("pool", [padB.w])
                        bstok = dma("pool", ads[1], BS[p:p + 1, :, :].rearrange("p h (r c) -> p h r c", c=64), src, ndesc=16)
                BSB.w = bstok
                maskB = sb("maskB", [128, 9, 896], F32, p2)
                maskBB = Buf("maskB")
                ckpt(45)

                KT = sb("KT", [128, T], BF16, p2)
                KTB = Buf("KT")
                Vg = sb("Vg", [128, NCH, 129], BF16, p2)
                VgB = Buf("Vg")
                op("dve", lambda: dve.memset(Vg[:, :, 128:129], 1.0), writes=[VgB])
                QTt = [sb("QT%d" % i, [128, 4, 512], BF16, p2) for i in range(2)]
                QTR = Ring([(QTt[i], Buf("QT%d" % i), ads[3 + i]) for i in range(2)])
                PTt = [sb("PT%d" % i, [128, 512], BF16, p2) for i in range(3)]
                PTR = Ring([(PTt[i], Buf("PT%d" % i)) for i in range(3)])
                ynt = [sb("yn%d" % i, [128, 128], BF16, p2) for i in range(2)]
                ynR = Ring([(ynt[i], Buf("yn%d" % i)) for i in range(2)])
                ystt = [sb("yst%d" % i, [128, 512], BF16, p2) for i in range(2)]
                ystR = Ring([(ystt[i], Buf("yst%d" % i), ads[5 + i]) for i in range(2)])
                rinv = sb("rinv", [128, 4], F32, p2)
                rinvB = Buf("rinv")
                SR = Ring([4, 5, 6])
                noB = Buf("dram2")
                for g in range(2):
                    dma("sp", ads[7], KT[:], kaT_s.ap()[g, :, :], writes=[KTB])
                    dma("sp", ads[7], Vg[:, :, 0:128],
                        va_s.ap()[:, g * 128:(g + 1) * 128].rearrange("(c p) d -> p c d", p=128), writes=[VgB])
                    items = [(qt, hl, kc) for qt in range(NT) for hl in range(4) for kc in range(NCH)]
                    qts = {}

                    def getQ(qt):
                        if qt not in qts:
                            q_, qB_, qds_ = QTR.next()
                            dma("sp", qds_, q_[:],
                                qaT_s.ap()[g * 4:(g + 1) * 4, :, qt * 512:(qt + 1) * 512].rearrange("h d t -> d h t"),
                                writes=[qB_])
                            qts[qt] = (q_, qB_)
                        return qts[qt]

                    sbank = {}

                    def emitS(idx):
                        qt, hl, kc = items[idx]
                        q_, qB_ = getQ(qt)
                        b = SR.next()
                        sbank[idx] = b
                        op("pe", lambda: pe.matmul(ps[b][:, :], lhsT=KT[:, kc * 128:(kc + 1) * 128], rhs=q_[:, hl, :],
                                                   start=True, stop=True), reads=[KTB, qB_], writes=[psB[b]])

                    emitS(0)
                    emitS(1)
                    if g == 0:
                        dma("sp", ads[2], maskB[:], maskB_d.ap().rearrange("p (t c) -> p t c", c=896), writes=[maskBB])
                    for idx, (qt, hl, kc) in enumerate(items):
                        if idx + 2 < len(items):
                            emitS(idx + 2)
                        b = sbank.pop(idx)
                        mi = (2 if kc >= NCH // 2 else 0) + (1 if qt >= NT // 2 else 0)
                        pt, ptB = PTR.next()
                        op("act", lambda: act.activation(out=pt[:], in_=ps[b][:, :], func=AF.Exp,
                                                         bias=biasA[:, mi:mi + 1], scale=SCALE),
                           reads=[psB[b], biasAB], writes=[ptB])
                        for qs in range(4):
                            op("pe", lambda qs=qs: pe.matmul(ps[qs][:, 0:129], lhsT=pt[:, qs * 128:(qs + 1) * 128],
                                                             rhs=Vg[:, kc, :], start=(kc == 0), stop=(kc == NCH - 1)),
                               reads=[ptB, VgB], writes=[psB[qs]], inc=(qs == 3))
                        if kc == NCH - 1:
                            h = g * 4 + hl
                            yst, ystB, yds = ystR.next()
                            for qs in range(4):
                                op("dve", lambda qs=qs: dve.reciprocal(out=rinv[:, qs:qs + 1], in_=ps[qs][:, 128:129]),
                                   reads=[psB[qs]], writes=[rinvB])
                                yn, ynB = ynR.next()
                                op("dve", lambda qs=qs, yn=yn: dve.tensor_scalar(out=yn[:], in0=ps[qs][:, 0:128],
                                                                                 scalar1=rinv[:, qs:qs + 1], scalar2=None,
                                                                                 op0=ALU.mult),
                                   reads=[psB[qs], rinvB], writes=[ynB])
                                op("pe", lambda qs=qs, yn=yn: pe.transpose(out=psb[7][:, qs * 128:(qs + 1) * 128],
                                                                           in_=yn[:], identity=identb[:]),
                                   reads=[ynB, identbB], writes=[psB[7]])
                            op("act", lambda yst=yst: act.activation(out=yst[:], in_=psb[7][:, 0:512], func=AF.Copy),
                               reads=[psB[7]], writes=[ystB])
                            dma("pool", yds, yaT_s.ap()[h, :, qt * 512:(qt + 1) * 512], yst[:], reads=[ystB], writes=[noB])

                ckpt(5)
                master = sb("master", [128, 8, 896], F32, p2)
                masterB = Buf("master")
                for h in range(8):
                    for (bk, d0, nd) in ((4, 0, 4), (5, 4, 3)):
                        for i in range(nd):
                            di = d0 + i
                            op("pe", lambda i=i, di=di, bk=bk: pe.transpose(
                                out=ps[bk][:, i * 128:(i + 1) * 128], in_=BS[:, h, di * 128:(di + 1) * 128],
                                identity=ident[:]),
                               reads=[BSB, cB], writes=[psB[bk]], inc=(i == nd - 1))
                        op("dve", lambda bk=bk, d0=d0, nd=nd: dve.tensor_copy(
                            out=master[:, h, d0 * 128:(d0 + nd) * 128], in_=ps[bk][:, 0:nd * 128]),
                           reads=[psB[bk]], writes=[masterB])
                combt = [sb("comb%d" % i, [128, 896], F32, p2) for i in range(2)]
                combBs = [Buf("comb%d" % i) for i in range(2)]

                def build_comb(h):
                    op("pool", lambda: pool.tensor_tensor(out=combt[h % 2][:], in0=master[:, h, :], in1=maskB[:, 2, :],
                                                          op=ALU.add),
                       reads=[masterB, maskBB], writes=[combBs[h % 2]])
                KTb = sb("KTb", [128, T], BF16, p2)
                QTb = sb("QTb", [128, T], BF16, p2)
                Vb = sb("Vb", [128, NCH, 129], BF16, p2)
                QTb2 = sb("QTb2", [128, T], BF16, p2)
                vinitB = Buf("vinit")
                op("dve", lambda: dve.memset(Vb[:, :, 128:129], 1.0), writes=[vinitB])
                sets = [(KTb, QTb, Vb, [Buf("k0"), Buf("q0"), Buf("v0")], [tr.dsem(), tr.dsem(), tr.dsem()]),
                        (KT, QTb2, Vg, [Buf("k1"), Buf("q1"), Buf("v1")], [tr.dsem(), tr.dsem(), tr.dsem()])]
                sets[0][3][2].w = vinitB.w
                sets[1][3][0].inherit([KTB])
                sets[1][3][2].inherit([VgB])

                def loadset(h):
                    k_, q_, v_, bs_, ds_ = sets[h % 2]
                    dma("sp", ds_[0], k_[:], kbT_s.ap()[h, :, :], writes=[bs_[0]])
                    dma("sp", ds_[1], q_[:], qbT_s.ap()[h, :, :], writes=[bs_[1]])
                    dma("sp", ds_[2], v_[:, :, 0:128],
                        vb_s.ap()[:, h * 128:(h + 1) * 128].rearrange("(c p) d -> p c d", p=128), writes=[bs_[2]])

                sTt = [sb("sT%d" % i, [128, 896], F32, p2) for i in range(2)]
                sTR = Ring([(sTt[i], Buf("sT%d" % i)) for i in range(2)])
                PBt = [sb("PB%d" % i, [128, 896], BF16, p2) for i in range(3)]
                PBR = Ring([(PBt[i], Buf("PB%d" % i)) for i in range(3)])
                ynb = [sb("ynb%d" % i, [128, 128], BF16, p2) for i in range(3)]
                ynbR = Ring([(ynb[i], Buf("ynb%d" % i)) for i in range(3)])
                rinvb = sb("rinvb", [128, 4], F32, p2)
                rinvbB = Buf("rinvb")
                SBR = Ring([(0, 1), (2, 3)])
                ACR = Ring([4, 5])
                TBR = Ring([6, 7])

                def ti_of(j):
                    if j == 0:
                        return 0
                    if j == 1:
                        return 1
                    if NBLK // 2 - 2 <= j <= NBLK // 2 + 1:
                        return 3 + j - (NBLK // 2 - 2)
                    if j == NBLK - 2:
                        return 7
                    if j == NBLK - 1:
                        return 8
                    return 2

                loadset(0)
                build_comb(0)
                for h in range(8):
                    if h + 1 < 8:
                        loadset(h + 1)
                        build_comb(h + 1)
                    comb, combB = combt[h % 2], combBs[h % 2]
                    KTh, QTh, Vh, (kB_, qB_, vB_), _ = sets[h % 2]
                    st_ = {}

                    def stA(j):
                        ba, bb = SBR.next()
                        st_[j] = {"s": (ba, bb)}
                        for di in range(7):
                            c = min(max(j - 3 + di, 0), NCH - 1)
                            bk, off = (ba, di * 128) if di < 4 else (bb, (di - 4) * 128)
                            op("pe", lambda c=c, bk=bk, off=off: pe.matmul(
                                ps[bk][:, off:off + 128], lhsT=KTh[:, c * 128:(c + 1) * 128],
                                rhs=QTh[:, j * 128:(j + 1) * 128], start=True, stop=True),
                               reads=[kB_, qB_], writes=[psB[bk]], inc=(di == 3 or di == 6))

                    def stB(j):
                        ba, bb = st_[j]["s"]
                        ti = ti_of(j)
                        sT, sTB = sTR.next()
                        if ti == 2:
                            bsrc, bsB = comb, combB
                        else:
                            bsrc, bsB = master[:, h, :], masterB
                        op("dve", lambda: dve.scalar_tensor_tensor(out=sT[:, 0:512], in0=ps[ba][:, :], scalar=SCALE,
                                                                   in1=bsrc[:, 0:512], op0=ALU.mult, op1=ALU.add),
                           reads=[psB[ba], bsB], writes=[sTB])
                        op("dve", lambda: dve.scalar_tensor_tensor(out=sT[:, 512:896], in0=ps[bb][:, 0:384],
                                                                   scalar=SCALE, in1=bsrc[:, 512:896],
                                                                   op0=ALU.mult, op1=ALU.add),
                           reads=[psB[bb], bsB], writes=[sTB])
                        if ti != 2:
                            op("pool", lambda: pool.tensor_tensor(out=sT[:], in0=sT[:], in1=maskB[:, ti, :], op=ALU.add),
                               reads=[maskBB], writes=[sTB])
                        pb, pbB = PBR.next()
                        op("act", lambda: act.activation(out=pb[:], in_=sT[:], func=AF.Exp), reads=[sTB], writes=[pbB])
                        st_[j]["p"] = (pb, pbB)

                    def stC(j):
                        pb, pbB = st_[j]["p"]
                        ac = ACR.next()
                        st_[j]["ac"] = ac
                        for di in range(7):
                            c = min(max(j - 3 + di, 0), NCH - 1)
                            op("pe", lambda di=di, c=c: pe.matmul(ps[ac][:, 0:129], lhsT=pb[:, di * 128:(di + 1) * 128],
                                                                  rhs=Vh[:, c, :], start=(di == 0), stop=(di == 6)),
                               reads=[pbB, vB_], writes=[psB[ac]], inc=(di == 6))

                    def stD(j):
                        ac = st_[j]["ac"]
                        op("dve", lambda: dve.reciprocal(out=rinvb[:, j % 4:j % 4 + 1], in_=ps[ac][:, 128:129]),
                           reads=[psB[ac]], writes=[rinvbB])
                        yn, ynB = ynbR.next()
                        op("dve", lambda: dve.tensor_scalar(out=yn[:], in0=ps[ac][:, 0:128],
                                                            scalar1=rinvb[:, j % 4:j % 4 + 1], scalar2=None, op0=ALU.mult),
                           reads=[psB[ac], rinvbB], writes=[ynB])
                        st_[j]["yn"] = (yn, ynB)

                    tbs = {}

                    def stE(j):
                        yn, ynB = st_[j]["yn"]
                        if j % 4 == 0:
                            tbs[j // 4] = TBR.next()
                        tb = tbs[j // 4]
                        op("pe", lambda: pe.transpose(out=psb[tb][:, (j % 4) * 128:(j % 4 + 1) * 128], in_=yn[:],
                                                      identity=identb[:]), reads=[ynB, identbB], writes=[psB[tb]])
                        if j % 4 == 3:
                            yst, ystB, yds = ystR.next()
                            op("act", lambda: act.activation(out=yst[:], in_=psb[tb][:, 0:512], func=AF.Copy),
                               reads=[psB[tb]], writes=[ystB])
                            dma("pool", yds, ybT_s.ap()[h, :, (j - 3) * 128:(j + 1) * 128], yst[:], reads=[ystB],
                                writes=[noB], ndesc=8)
                        del st_[j]

                    for i in range(-2, NBLK + 2):
                        if 0 <= i + 2 < NBLK:
                            stA(i + 2)
                        if 0 <= i + 1 < NBLK:
                            stB(i + 1)
                        if 0 <= i < NBLK:
                            stC(i)
                        if 0 <= i - 1 < NBLK:
                            stD(i - 1)
                        if 0 <= i - 2 < NBLK:
                            stE(i - 2)
                tr.barrier()
                ckpt(6)

            with ExitStack() as p3:
                n = mk13(p3, 3)
                mT = n.actT[:, 0:16, :]
                yaT = n.actT[:, 16:24, :]
                ybT = n.actT[:, 24:32, :]
                yds = tr.dsem()
                ods = tr.dsem()
                noB = Buf("dram3")
                out_toks = []
                uds3 = tr.dsem()

                def prefetch_u(t):
                    dma("sp", uds3, n.xnT[:], uT_s.ap()[t].rearrange("p (k n) -> p k n", n=512), writes=[n.xnTB])

                def gslot(slot, c0):
                    wt, wtB, wds = n.wring.items[slot]
                    view = wt[:, 0:16 * 512].rearrange("p (k n) -> p k n", n=512)
                    dma("sp", wds, view, win_s.ap().rearrange("(k p) n -> p k n", p=128)[:, :, c0:c0 + 512],
                        reads=[winB], writes=[wtB])
                    return view, wtB

                def abslot(fb):
                    w2t, w2B, w2ds = n.w2ring.items[fb % 2]
                    view = w2t[:, :, :].rearrange("p a b -> p (a b)")[:, 0:8192].rearrange("p (r k n) -> p r k n", r=2, k=8)
                    for r_, (wsrc, wk) in enumerate(((wa_s, "wa"), (wb_s, "wb"))):
                        dma("sp", w2ds, view[:, r_, :, :],
                            wsrc.ap().rearrange("(k p) n -> p k n", p=128)[:, :, fb * 512:(fb + 1) * 512],
                            reads=[WB[wk]], writes=[w2B])
                    return view, w2B

                gpre = {}

                def issue_first(t):
                    gpre[t] = {"ab": {0: abslot(0)}, "ga": {0: gslot(0, O_GA)}, "gb": {0: gslot(1, O_GB)}}

                prefetch_u(0)
                issue_first(0)
                for t in range(NT):
                    tsl = slice(t * 512, (t + 1) * 512)
                    dma("sp", yds, yaT, yaT_s.ap()[:, :, tsl].rearrange("h d t -> d h t"), writes=[n.actTB])
                    dma("sp", yds, ybT, ybT_s.ap()[:, :, tsl].rearrange("h d t -> d h t"), writes=[n.actTB])
                    yB = Buf("y")
                    yB.w = n.actTB.w
                    mB = Buf("m")
                    mB.inherit([n.actTB])
                    G = gpre.pop(t)
                    for fb in range(4):
                        if fb + 1 < 4:
                            G["ab"][fb + 1] = abslot(fb + 1)
                        abv, abB = G["ab"][fb]
                        tA = []
                        for br in range(2):
                            yv = (yaT, ybT)[br]
                            wg, wgB = G[("ga", "gb")[br]][fb]
                            for fc in range(4):
                                ch = fb * 4 + fc
                                bz, bgt = n.psA.next(), n.psA.next()
                                for kc in range(16):
                                    op("pe", lambda kc=kc, fc=fc: pe.matmul(
                                        ps[bgt][:, :], lhsT=wg[:, kc, fc * 128:(fc + 1) * 128], rhs=n.xnT[:, kc, :],
                                        start=(kc == 0), stop=(kc == 15)),
                                       reads=[wgB, n.xnTB], writes=[psB[bgt]], inc=(kc == 15))
                                for h in range(8):
                                    op("pe", lambda h=h, fc=fc: pe.matmul(
                                        ps[bz][:, :], lhsT=abv[:, br, h, fc * 128:(fc + 1) * 128], rhs=yv[:, h, :],
                                        start=(h == 0), stop=(h == 7)),
                                       reads=[abB, yB], writes=[psB[bz]], inc=(h == 7))
                                if br == 0:
                                    tf, tfB = n.tmpR.next()
                                    tA.append((tf, tfB))
                                else:
                                    tf, tfB = n.sgR.next()
                                op("act", lambda: act.activation(out=tf[:], in_=ps[bgt][:, :], func=AF.Sigmoid,
                                                                 bias=bgc[:, br * 16 + ch: br * 16 + ch + 1]),
                                   reads=[psB[bgt], cB], writes=[tfB])
                                op("dve", lambda: dve.tensor_tensor(out=tf[:], in0=tf[:], in1=ps[bz][:, :], op=ALU.mult),
                                   reads=[tfB, psB[bz]], writes=[tfB])
                                if br == 1:
                                    ta, taB = tA[fc]
                                    op("pool", lambda: pool.tensor_tensor(out=mT[:, ch, :], in0=ta[:], in1=tf[:],
                                                                          op=ALU.add),
                                       reads=[taB, tfB], writes=[mB])
                            if fb + 1 < 4:
                                key = ("ga", "gb")[br]
                                G[key][fb + 1] = gslot(br, (O_GA, O_GB)[br] + (fb + 1) * 512)
                    load_rows(n, h_s, t)
                    for db in range(4):
                        wv, wvB = wblock(n, wo_s, WB["wo"], 16, db * 512)
                        for s in range(4):
                            b = n.psA.next()
                            for kc in range(16):
                                op("pe", lambda kc=kc, s=s: pe.matmul(ps[b][:, :], lhsT=mT[:, kc, s * 128:(s + 1) * 128],
                                                                      rhs=wv[:, kc, :], start=(kc == 0), stop=(kc == 15)),
                                   reads=[mB, wvB], writes=[psB[b]], inc=(kc == 15))
                            op("dve", lambda b=b, s=s, db=db: dve.tensor_tensor(
                                out=n.xt[:, s, db * 512:(db + 1) * 512], in0=ps[b][:, :],
                                in1=n.xt[:, s, db * 512:(db + 1) * 512], op=ALU.add),
                               reads=[psB[b]], writes=[n.xtB[s]])
                    n.actTB.inherit([yB, mB])
                    norm_to_T(n, g2c)
                    ffn(n, w2a_s, WB["w2a"], w2b_s, WB["w2b"],
                        mid_hook=((lambda t=t: prefetch_u(t + 1)) if t + 1 < NT else None))
                    if t + 1 < NT:
                        issue_first(t + 1)
                    row_stats(n)
                    for s in range(4):
                        eng_, e_ = ("dve", dve)
                        op(eng_, lambda s=s, e_=e_: e_.scalar_tensor_tensor(out=n.xt[:, s, :], in0=n.xt[:, s, :],
                                                                            scalar=n.stat[:, 8 + s:9 + s], in1=n.gfb[:],
                                                                            op0=ALU.mult, op1=ALU.mult),
                           reads=[n.statB, n.gfbB], writes=[n.xtB[s]])
                        out_toks.append(dma("pool", ods, out_d.ap()[t * 512 + s * 128: t * 512 + (s + 1) * 128, :],
                                            n.xt[:, s, :], reads=[n.xtB[s]], writes=[noB], ndesc=8))
                tr.barrier()
    except _Stop:
        pass
    return nc


def _geometry(T, mode):
    tok = np.arange(T)
    if mode == "prompt":
        return np.zeros(T, np.int64), tok, T // GRID_W
    half = T // 2
    return tok // half, tok % half, half // GRID_W


def _rope_table(T, mode):
    _, pos, _ = _geometry(T, mode)
    row = (pos // GRID_W).astype(np.float32)
    col = (pos % GRID_W).astype(np.float32)
    npairs = HD // 4
    inv = (10000.0 ** (-np.arange(npairs, dtype=np.float32) / npairs)).astype(np.float32)
    ang = np.concatenate([row[:, None] * inv[None], col[:, None] * inv[None]], axis=-1)
    return np.concatenate([np.cos(ang), np.sin(ang)], axis=-1).astype(np.float32)


def _ti_blocks(nblk):
    reps = [0, 1, None] + [nblk // 2 - 2 + i for i in range(4)] + [nblk - 2, nblk - 1]
    special = set(r for r in reps if r is not None)
    interior = [j for j in range(nblk) if j not in special]
    reps[2] = interior[0] if interior else None
    return reps


def _mask_b(T, mode):
    seq, pos, rows = _geometry(T, mode)
    nblk = T // 128
    r_of = pos // GRID_W
    c_of = pos % GRID_W
    rs = np.clip(r_of - 4, 0, rows - 8)
    cs = np.clip(c_of - 8, 0, GRID_W - 16)
    out = np.full((128, 9, 7, 128), NEG, np.float32)
    for ti, j in enumerate(_ti_blocks(nblk)):
        if j is None:
            continue
        q = j * 128 + np.arange(128)
        for di in range(7):
            c = j - 3 + di
            if c < 0 or c >= nblk:
                continue
            k = c * 128 + np.arange(128)
            ok = (seq[k][:, None] == seq[q][None, :])
            ok &= (r_of[k][:, None] >= rs[q][None, :]) & (r_of[k][:, None] < rs[q][None, :] + 8)
            ok &= (c_of[k][:, None] >= cs[q][None, :]) & (c_of[k][:, None] < cs[q][None, :] + 16)
            out[:, ti, di, :] = np.where(ok, 0.0, NEG)
    return out.reshape(128, 9 * 896)


def _mask_a(mode):
    m = np.zeros((128, 4), np.float32)
    if mode != "prompt":
        m[:, 1] = NEG
        m[:, 2] = NEG
    return m


def _shared_inputs(g_ffn1, w_ffn1_in, w_ffn1_out, g_mix, w_in, b_gate, g_q_a, g_k_a, rpb_b, w_branch_a,
                   w_branch_b, w_out, g_ffn2, w_ffn2_in, w_ffn2_out, g_final):
    f = lambda a: np.ascontiguousarray(np.asarray(a, dtype=np.float32))
    col = lambda g: f(np.asarray(g).reshape(-1, 128).T)
    return {
        "ident": np.eye(128, dtype=np.float32),
        "g1c": col(g_ffn1), "gmc": col(g_mix), "g2c": col(g_ffn2),
        "gfb": f(np.broadcast_to(np.asarray(g_final).reshape(1, D), (128, D))),
        "bgc": col(b_gate),
        "gqb": f(np.broadcast_to(np.asarray(g_q_a).reshape(1, 128), (128, 128))),
        "gkb": f(np.broadcast_to(np.asarray(g_k_a).reshape(1, 128), (128, 128))),
        "rpb": f(np.asarray(rpb_b).reshape(120, 31)),
        "w1a": f(np.asarray(w_ffn1_in)[0]), "w1b": f(np.asarray(w_ffn1_out)[0]),
        "win": f(np.asarray(w_in)[0]), "wa": f(np.asarray(w_branch_a)[0]), "wb": f(np.asarray(w_branch_b)[0]),
        "wo": f(np.asarray(w_out)[0]), "w2a": f(np.asarray(w_ffn2_in)[0]), "w2b": f(np.asarray(w_ffn2_out)[0]),
    }


def _slot_inputs(T, mode, x):
    return {"x": np.ascontiguousarray(x, dtype=np.float32), "rope": _rope_table(T, mode),
            "maskA": _mask_a(mode), "maskB": _mask_b(T, mode)}


_NC_CACHE = {}


def kernel(x_prompt, x_sample, g_ffn1, w_ffn1_in, w_ffn1_out, g_mix, w_in, b_gate, g_q_a, g_k_a,
           rpb_b, w_branch_a, w_branch_b, w_out, g_ffn2, w_ffn2_in, w_ffn2_out, g_final):
    x_prompt = np.asarray(x_prompt, dtype=np.float32)
    x_sample = np.asarray(x_sample, dtype=np.float32)
    T = x_prompt.shape[1]
    assert x_prompt.shape[0] == 2 and x_sample.shape[0] == 8 and x_sample.shape[1] * 2 == T
    shared = _shared_inputs(g_ffn1, w_ffn1_in, w_ffn1_out, g_mix, w_in, b_gate, g_q_a, g_k_a, rpb_b,
                            w_branch_a, w_branch_b, w_out, g_ffn2, w_ffn2_in, w_ffn2_out, g_final)
    in_maps = []
    for c in range(8):
        if c < 2:
            m = _slot_inputs(T, "prompt", x_prompt[c])
        elif c < 6:
            i = c - 2
            m = _slot_inputs(T, "sample", np.concatenate([x_sample[2 * i], x_sample[2 * i + 1]], axis=0))
        else:
            m = _slot_inputs(T, "prompt", np.zeros((T, D), np.float32))
        m.update(shared)
        in_maps.append(m)
    if T not in _NC_CACHE:
        _NC_CACHE[T] = build_nc(T)
    res = run_bass_kernel_spmd(_NC_CACHE[T], in_maps, core_ids=list(range(8)))
    outs = [np.asarray(r["out"], dtype=np.float32) for r in res.results]
    y_prompt = np.stack([outs[0], outs[1]], axis=0)
    half = T // 2
    y_sample = np.stack([outs[2 + i // 2][(i % 2) * half:(i % 2 + 1) * half] for i in range(8)], axis=0)
    return (y_prompt, y_sample)
```

```python
import math
from contextlib import ExitStack

import numpy as np

import concourse.bass as bass
import concourse.mybir as mybir
from concourse.bass_utils import run_bass_kernel_spmd

F32 = mybir.dt.float32
BF16 = mybir.dt.bfloat16
AF = mybir.ActivationFunctionType
ALU = mybir.AluOpType
AX = mybir.AxisListType

D = 2048
DFF = 5504
NFC = DFF // 128
HD = 128
INW = 8704
EPS = 1e-6
GRID_W = 64
NEG = -30000.0
SCALE = HD ** -0.5

O_QA, O_KA, O_VA, O_QB, O_KB, O_VB, O_GA, O_GB = 0, 1024, 1280, 1536, 2560, 3584, 4608, 6656


class Tok:
    __slots__ = ("key", "sem", "val")

    def __init__(self, key, sem, val):
        self.key, self.sem, self.val = key, sem, val


class Buf:
    def __init__(self, name=""):
        self.name = name
        self.w = None
        self.r = {}

    def inherit(self, others):
        for o in others:
            if o.w is not None:
                self.r[("w", o.w.key)] = o.w if (("w", o.w.key) not in self.r or self.r[("w", o.w.key)].val < o.w.val) else self.r[("w", o.w.key)]
            for k, t in o.r.items():
                if k not in self.r or self.r[k].val < t.val:
                    self.r[k] = t


class DSem:
    def __init__(self, sem, idx):
        self.sem, self.val, self.key = sem, 0, ("d", idx)


class Tr:
    def __init__(self, nc, es):
        self.nc = nc
        self.es = es
        self.engs = {"pe": nc.tensor, "act": nc.scalar, "dve": nc.vector, "pool": nc.gpsimd, "sp": nc.sync}
        self.sem = {k: es.enter_context(nc.semaphore("c_" + k)) for k in ("pe", "act", "dve", "pool")}
        self.cnt = {k: 0 for k in self.sem}
        self.seen = {}
        self.nds = 0
        self.nwait = 0
        self.pend = {k: ([], []) for k in self.sem}
        self.dsems = []
        self.pool_fifo = []
        self.pool_out = []

    def dsem(self):
        self.nds += 1
        d = DSem(self.es.enter_context(self.nc.semaphore("d%d" % self.nds)), self.nds)
        self.dsems.append(d)
        return d

    def barrier(self):
        toks = [Tok(k, self.sem[k], self.cnt[k]) for k in self.sem if self.cnt[k] > 0]
        toks += [Tok(d.key, d.sem, d.val) for d in self.dsems if d.val > 0]
        for e in ("pe", "act", "dve", "pool", "sp"):
            self.wait(e, toks)

    def wait(self, eng, toks):
        for t in toks:
            if t is None:
                continue
            if eng == "pe" and t.key == "pe":
                continue
            k = (eng, t.key)
            if self.seen.get(k, 0) < t.val:
                self.engs[eng].wait_ge(t.sem, t.val)
                self.seen[k] = t.val
                self.nwait += 1

    @staticmethod
    def deps(reads, writes):
        d = []
        for b in reads:
            d.append(b.w)
            if b.name.startswith("ps"):
                d.extend(b.r.values())
        for b in writes:
            d.append(b.w)
            d.extend(b.r.values())
        return d

    @staticmethod
    def mark(tok, reads, writes):
        for b in writes:
            b.w = tok
            b.r = {}
        for b in reads:
            o = b.r.get(tok.key)
            if o is None or o.val < tok.val:
                b.r[tok.key] = tok

    def op(self, eng, fn, reads=(), writes=(), inc=True, extra=()):
        self.wait(eng, self.deps(reads, writes) + list(extra))
        ins = fn()
        if not inc:
            self.pend[eng][0].extend(reads)
            self.pend[eng][1].extend(writes)
            return None
        self.cnt[eng] += 1
        ins.then_inc(self.sem[eng], 1)
        tok = Tok(eng, self.sem[eng], self.cnt[eng])
        pr, pw = self.pend[eng]
        self.mark(tok, list(reads) + pr, list(writes) + pw)
        self.pend[eng] = ([], [])
        return tok

    def dma(self, q, ds, out, in_, reads=(), writes=(), extra=(), ndesc=32, **kw):
        self.wait(q, self.deps(reads, writes) + list(extra))
        if q == "pool":
            self.pool_out.append(None)
            while sum(n_ for _, n_ in self.pool_fifo) + ndesc > 480 and self.pool_fifo:
                t_, _ = self.pool_fifo.pop(0)
                self.wait("pool", [t_])
            self.pool_out.pop()
        ins = self.engs[q].dma_start(out=out, in_=in_, **kw)
        ds.val += 16
        ins.then_inc(ds.sem, 16)
        tok = Tok(ds.key, ds.sem, ds.val)
        self.mark(tok, reads, writes)
        if q == "pool":
            self.pool_fifo.append((tok, ndesc))
        return tok


class Ring:
    def __init__(self, items):
        self.items = items
        self.i = 0

    def next(self):
        it = self.items[self.i % len(self.items)]
        self.i += 1
        return it


class _Stop(Exception):
    pass


def build_nc(T, stop=None):
    NT = T // 512
    NCH = T // 128
    NBLK = NCH
    nc = bass.Bass("TRN2", target_bir_lowering=False)

    def din(name, shape, dt=F32):
        return nc.dram_tensor(name, list(shape), dt, kind="ExternalInput")

    x_d = din("x", [T, D])
    rope_d = din("rope", [T, 128])
    maskA_d = din("maskA", [128, 4])
    maskB_d = din("maskB", [128, 9 * 896])
    ident_d = din("ident", [128, 128])
    g1c_d = din("g1c", [128, 16])
    gmc_d = din("gmc", [128, 16])
    g2c_d = din("g2c", [128, 16])
    gfb_d = din("gfb", [128, D])
    bgc_d = din("bgc", [128, 32])
    gqb_d = din("gqb", [128, 128])
    gkb_d = din("gkb", [128, 128])
    rpb_d = din("rpb", [120, 31])
    w1a_d = din("w1a", [D, 2 * DFF])
    w1b_d = din("w1b", [DFF, D])
    win_d = din("win", [D, INW])
    wa_d = din("wa", [1024, D])
    wb_d = din("wb", [1024, D])
    wo_d = din("wo", [D, D])
    w2a_d = din("w2a", [D, 2 * DFF])
    w2b_d = din("w2b", [DFF, D])
    out_d = nc.dram_tensor("out", [T, D], F32, kind="ExternalOutput")

    def dscr(name, shape, dt=BF16):
        return nc.dram_tensor(name, list(shape), dt)

    w1a_s = dscr("w1a_s", [D, 2 * DFF])
    w1b_s = dscr("w1b_s", [DFF, D])
    win_s = dscr("win_s", [D, INW])
    wa_s = dscr("wa_s", [1024, D])
    wb_s = dscr("wb_s", [1024, D])
    wo_s = dscr("wo_s", [D, D])
    w2a_s = dscr("w2a_s", [D, 2 * DFF])
    w2b_s = dscr("w2b_s", [DFF, D])
    h_s = dscr("h_s", [T, D], F32)
    qaT_s = dscr("qaT_s", [8, 128, T])
    kaT_s = dscr("kaT_s", [2, 128, T])
    va_s = dscr("va_s", [T, 256])
    qbT_s = dscr("qbT_s", [8, 128, T])
    kbT_s = dscr("kbT_s", [8, 128, T])
    vb_s = dscr("vb_s", [T, 1024])
    yaT_s = dscr("yaT_s", [8, 128, T])
    ybT_s = dscr("ybT_s", [8, 128, T])
    pad_s = dscr("pad_s", [120, 160], F32)
    uT_s = dscr("uT_s", [NT, 128, 16 * 512])

    try:
        with ExitStack() as es:
            tr = Tr(nc, es)

            def ckpt(k):
                if stop == k:
                    tr.barrier()
                    raise _Stop()
            op, dma = tr.op, tr.dma
            pe, act, dve, pool = nc.tensor, nc.scalar, nc.vector, nc.gpsimd

            _uid = [0]

            def sb(name, shape, dt, stack=es):
                _uid[0] += 1
                return stack.enter_context(nc.sbuf_tensor("s%d_%s" % (_uid[0], name), list(shape), dt))

            ps = [es.enter_context(nc.psum_tensor("ps%d" % i, [128, 512], F32)) for i in range(8)]
            psb = [p.bitcast(BF16) for p in ps]
            psB = [Buf("ps%d" % i) for i in range(8)]

            ident = sb("ident", [128, 128], F32)
            identb = sb("identb", [128, 128], BF16)
            g1c = sb("g1c", [128, 16], F32)
            gmc = sb("gmc", [128, 16], F32)
            g2c = sb("g2c", [128, 16], F32)
            bgc = sb("bgc", [128, 32], F32)
            gqb = sb("gqb", [128, 128], F32)
            gkb = sb("gkb", [128, 128], F32)
            maskA = sb("maskA", [128, 4], F32)
            biasA = sb("biasA", [128, 4], F32)
            small = sb("small", [128, 8], F32)
            cB = Buf("consts")
            cds = tr.dsem()
            for t_sb, t_d in ((ident, ident_d), (g1c, g1c_d), (gmc, gmc_d), (g2c, g2c_d), (bgc, bgc_d),
                              (gqb, gqb_d), (gkb, gkb_d), (maskA, maskA_d)):
                dma("sp", cds, t_sb[:], t_d.ap(), writes=[cB])
            identbB = Buf("identb")
            op("dve", lambda: dve.tensor_copy(out=identb[:], in_=ident[:]), reads=[cB], writes=[identbB])
            smB = Buf("small")
            op("dve", lambda: dve.tensor_reduce(out=small[:, 0:1], in_=gqb[:], axis=AX.X, op=ALU.max,
                                                apply_absolute_value=True), reads=[cB], writes=[smB])
            op("dve", lambda: dve.tensor_reduce(out=small[:, 1:2], in_=gkb[:], axis=AX.X, op=ALU.max,
                                                apply_absolute_value=True), reads=[cB, smB], writes=[smB])
            op("dve", lambda: dve.tensor_tensor(out=small[:, 2:3], in0=small[:, 0:1], in1=small[:, 1:2], op=ALU.mult),
               reads=[smB], writes=[smB])
            op("dve", lambda: dve.tensor_scalar(out=small[:, 3:4], in0=small[:, 2:3], scalar1=-math.sqrt(128.0),
                                                scalar2=None, op0=ALU.mult), reads=[smB], writes=[smB])
            biasAB = Buf("biasA")
            op("dve", lambda: dve.tensor_scalar(out=biasA[:], in0=maskA[:], scalar1=small[:, 3:4], scalar2=None,
                                                op0=ALU.add), reads=[smB, cB], writes=[biasAB])
            op("dve", lambda: dve.memset(small[:, 4:5], EPS), writes=[smB])
            ckpt(0)

            def convert(src, dst, rows, cols):
                ds = tr.dsem()
                b = Buf("w")
                tok = None
                for r0 in range(0, rows, 128):
                    o = dst.ap()[r0:r0 + 128, :]
                    i = src.ap()[r0:r0 + 128, :]
                    if cols > 2048:
                        o = o.rearrange("r (a b) -> r a b", a=8)
                        i = i.rearrange("r (a b) -> r a b", a=8)
                    tok = dma("pool", ds, o, i, ndesc=(64 if cols > 2048 else 8))
                b.w = tok
                return b

            def convert_cols(src, dst, rows, bw, order):
                bufs = {}
                for a in order:
                    ds = tr.dsem()
                    tok = None
                    for r0 in range(0, rows, 128):
                        tok = dma("pool", ds, dst.ap()[r0:r0 + 128, a * bw:(a + 1) * bw],
                                  src.ap()[r0:r0 + 128, a * bw:(a + 1) * bw], ndesc=8)
                    b = Buf("w")
                    b.w = tok
                    bufs[a] = b
                return bufs

            def conv_jobs(src, dst, rows, cols, b):
                ds = tr.dsem()
                jobs = []
                cw = cols // 8 if cols > 2048 else cols
                for r0 in range(0, rows, 128):
                    for c0 in range(0, cols, cw):
                        def job(r0=r0, c0=c0):
                            b.w = dma("pool", ds, dst.ap()[r0:r0 + 128, c0:c0 + cw], src.ap()[r0:r0 + 128, c0:c0 + cw],
                                      ndesc=8)
                        jobs.append(job)
                return jobs

            OCT = 2 * DFF // 8
            w1a_oct = convert_cols(w1a_d, w1a_s, D, OCT, [0, 4, 1, 5, 2, 6, 3, 7])
            w1b_q = convert_cols(w1b_d, w1b_s, DFF, 512, [0, 1, 2, 3])
            winB = convert(win_d, win_s, D, INW)

            def w1a_cols(c0, c1):
                return [w1a_oct[a] for a in range(c0 // OCT, (c1 - 1) // OCT + 1)]

            def w1b_cols(c0, c1):
                return [w1b_q[a] for a in range(c0 // 512, (c1 - 1) // 512 + 1)]
            ckpt(1)
            WB = {}

            class NS:
                pass

            def mk13(stk, phase):
                n = NS()
                n.xt = sb("xt", [128, 4, D], F32, stk)
                n.xtB = [Buf("xt%d" % i) for i in range(4)]
                n.xnT = sb("xnT", [128, 16, 512], BF16, stk)
                n.xnTB = Buf("xnT")
                n.actT = sb("actT", [128, NFC, 512], BF16, stk)
                n.actTB = Buf("actT")
                wt = [sb("wr%d" % i, [128, 16 * 512], BF16, stk) for i in range(2)]
                n.wring = Ring([(wt[i], Buf("wr%d" % i), tr.dsem()) for i in range(2)])
                w2t = [sb("w2r%d" % i, [128, NFC, 256], BF16, stk) for i in range(2)]
                n.w2ring = Ring([(w2t[i], Buf("w2r%d" % i), tr.dsem()) for i in range(2)])
                xnb = [sb("xnb%d" % i, [128, D], BF16, stk) for i in range(2)]
                n.xnbR = Ring([(xnb[i], Buf("xnb%d" % i)) for i in range(2)])
                n.stat = sb("stat", [128, 16], F32, stk)
                n.statB = Buf("stat")
                tmpf = [sb("tmpf%d" % i, [128, 512], F32, stk) for i in range(4)]
                n.tmpR = Ring([(tmpf[i], Buf("tmpf%d" % i)) for i in range(4)])
                n.psA = Ring([0, 1, 2, 3, 4, 5])
                n.psT = Ring([6, 7])
                n.xds = tr.dsem()
                if phase == 1:
                    stg = [sb("stg%d" % i, [128, 4, 512], BF16, stk) for i in range(2)]
                    n.stgR = Ring([(stg[i], Buf("stg%d" % i), tr.dsem()) for i in range(2)])
                    n.ropet = sb("ropet", [128, 4, 128], F32, stk)
                    n.ropeB = Buf("rope")
                    n.rtmp = [sb("rtmp%d" % i, [128, 128], F32, stk) for i in range(5)]
                    n.rtmpB = [Buf("rtmp%d" % i) for i in range(5)]
                    n.qrot = sb("qrot", [128, 4, 128], BF16, stk)
                    n.qrotB = Buf("qrot")
                else:
                    n.gfb = sb("gfb", [128, D], F32, stk)
                    n.gfbB = Buf("gfb")
                    dma("sp", cds, n.gfb[:], gfb_d.ap(), writes=[n.gfbB])
                    sgt = [sb("sgt%d" % i, [128, 512], F32, stk) for i in range(2)]
                    n.sgR = Ring([(sgt[i], Buf("sgt%d" % i)) for i in range(2)])
                return n

            def load_rows(n, src, t):
                for s in range(4):
                    dma("sp", n.xds, n.xt[:, s, :], src.ap()[t * 512 + s * 128: t * 512 + (s + 1) * 128, :],
                        writes=[n.xtB[s]])

            def row_stats(n):
                junk = n.actT[:, 0:4, :]
                for s in range(4):
                    op("act", lambda s=s: act.activation(out=junk, in_=n.xt[:, s, :].rearrange("p (a b) -> p a b", a=4),
                                                         func=AF.Square, accum_out=n.stat[:, s:s + 1]),
                       reads=[n.xtB[s]], writes=[n.actTB, n.statB])
                op("act", lambda: act.activation(out=n.stat[:, 4:8], in_=n.stat[:, 0:4], func=AF.Sqrt,
                                                 bias=small[:, 4:5], scale=1.0 / D), reads=[n.statB, smB], writes=[n.statB])
                op("dve", lambda: dve.reciprocal(out=n.stat[:, 8:12], in_=n.stat[:, 4:8]), reads=[n.statB], writes=[n.statB])

            def norm_to_T(n, gcol):
                row_stats(n)
                for half in range(2):
                    cur = []
                    for s in (2 * half, 2 * half + 1):
                        xb, xbB = n.xnbR.next()
                        op("dve", lambda s=s, xb=xb: dve.tensor_scalar(out=xb[:], in0=n.xt[:, s, :],
                                                                      scalar1=n.stat[:, 8 + s:9 + s], scalar2=None,
                                                                      op0=ALU.mult),
                           reads=[n.xtB[s], n.statB], writes=[xbB])
                        cur.append((xb, xbB))
                    for kc in range(16):
                        b = n.psT.next()
                        for i, (xb, xbB) in enumerate(cur):
                            op("pe", lambda b=b, i=i, xb=xb, kc=kc: pe.transpose(
                                out=psb[b][:, i * 128:(i + 1) * 128], in_=xb[:, kc * 128:(kc + 1) * 128],
                                identity=identb[:]),
                               reads=[xbB, identbB], writes=[psB[b]], inc=(i == 1))
                        op("act", lambda b=b, kc=kc, half=half: act.activation(
                            out=n.xnT[:, kc, half * 256:(half + 1) * 256], in_=psb[b][:, 0:256], func=AF.Copy,
                            scale=gcol[:, kc:kc + 1]),
                           reads=[psB[b], cB], writes=[n.xnTB])

            def wblock(n, wsrc, wB, kch, c0, ncols=512):
                wt, wtB, wds = n.wring.next()
                view = wt[:, 0:kch * ncols].rearrange("p (k n) -> p k n", n=ncols)
                dma("sp", wds, view, wsrc.ap().rearrange("(k p) n -> p k n", p=128)[:, :, c0:c0 + ncols],
                    reads=[wB], writes=[wtB])
                return view, wtB

            def ffn(n, w_in_s, w_in_B, w_out_s, w_out_B, mid_hook=None, bg=None):
                inB = w_in_B if callable(w_in_B) else (lambda c0, c1: [w_in_B])
                outB = w_out_B if callable(w_out_B) else (lambda c0, c1: [w_out_B])
                ngrp = (NFC + 1) // 2
                src = w_in_s.ap().rearrange("(k p) n -> p k n", p=128)
                for gi in range(ngrp):
                    nch = min(2, NFC - 2 * gi)
                    wt, wtB, wds = n.wring.next()
                    view = wt[:, :].rearrange("p (a k n) -> p a k n", a=2, k=16)
                    dma("sp", wds, view[:, 0, :, 0:nch * 128], src[:, :, gi * 256: gi * 256 + nch * 128],
                        reads=inB(gi * 256, gi * 256 + nch * 128), writes=[wtB])
                    dma("sp", wds, view[:, 1, :, 0:nch * 128], src[:, :, DFF + gi * 256: DFF + gi * 256 + nch * 128],
                        reads=inB(DFF + gi * 256, DFF + gi * 256 + nch * 128), writes=[wtB])
                    if bg is not None:
                        bg()
                    for c in range(nch):
                        fc = 2 * gi + c
                        bg_, bu = n.psA.next(), n.psA.next()
                        for a, b in ((0, bg_), (1, bu)):
                            for kc in range(16):
                                op("pe", lambda a=a, b=b, kc=kc, c=c, view=view: pe.matmul(
                                    ps[b][:, :], lhsT=view[:, a, kc, c * 128:(c + 1) * 128], rhs=n.xnT[:, kc, :],
                                    start=(kc == 0), stop=(kc == 15)),
                                   reads=[wtB, n.xnTB], writes=[psB[b]], inc=(kc == 15))
                        tf, tfB = n.tmpR.next()
                        op("act", lambda tf=tf: act.activation(out=tf[:], in_=ps[bg_][:, :], func=AF.Silu),
                           reads=[psB[bg_]], writes=[tfB])
                        op("dve", lambda bu=bu, tf=tf, fc=fc: dve.tensor_tensor(out=n.actT[:, fc, :], in0=tf[:],
                                                                                in1=ps[bu][:, :], op=ALU.mult),
                           reads=[tfB, psB[bu]], writes=[n.actTB])
                if mid_hook is not None:
                    mid_hook()
                srco = w_out_s.ap().rearrange("(k p) n -> p k n", p=128)
                for db in range(8):
                    if bg is not None:
                        bg()
                    w2t, w2B, w2ds = n.w2ring.next()
                    dma("sp", w2ds, w2t[:], srco[:, :, db * 256:(db + 1) * 256], reads=outB(db * 256, (db + 1) * 256),
                        writes=[w2B])
                    for s in range(4):
                        b = n.psA.next()
                        for fc in range(NFC):
                            op("pe", lambda b=b, fc=fc, s=s, w2t=w2t: pe.matmul(
                                ps[b][:, 0:256], lhsT=n.actT[:, fc, s * 128:(s + 1) * 128], rhs=w2t[:, fc, :],
                                start=(fc == 0), stop=(fc == NFC - 1)),
                               reads=[n.actTB, w2B], writes=[psB[b]], inc=(fc == NFC - 1))
                        op("dve", lambda b=b, s=s, db=db: dve.scalar_tensor_tensor(
                            out=n.xt[:, s, db * 256:(db + 1) * 256], in0=ps[b][:, 0:256], scalar=0.5,
                            in1=n.xt[:, s, db * 256:(db + 1) * 256], op0=ALU.mult, op1=ALU.add),
                           reads=[psB[b]], writes=[n.xtB[s]])

            with ExitStack() as p1:
                n = mk13(p1, 1)
                hds = tr.dsem()
                rds = tr.dsem()
                uds = tr.dsem()
                noB = Buf("dram")

                def tok_major_mm(b, s, wv, wvB):
                    for kc in range(16):
                        op("pe", lambda kc=kc: pe.matmul(ps[b][:, :], lhsT=n.xnT[:, kc, s * 128:(s + 1) * 128],
                                                         rhs=wv[:, kc, :], start=(kc == 0), stop=(kc == 15)),
                           reads=[n.xnTB, wvB], writes=[psB[b]], inc=(kc == 15))

                qrots = [sb("qrot%d" % i, [128, 4, 128], BF16, p1) for i in range(3)]
                qrotR = Ring([(qrots[i], Buf("qrot%d" % i)) for i in range(3)])

                def qk_chain(b, s, nh, gb_t):
                    tf, tfB = n.tmpR.next()
                    op("act", lambda: act.activation(out=tf[:, :], in_=ps[b][:, :], func=AF.Square),
                       reads=[psB[b]], writes=[tfB])
                    op("dve", lambda: dve.tensor_reduce(out=n.stat[:, 12:12 + nh],
                                                        in_=tf[:, 0:nh * 128].rearrange("p (h d) -> p h d", d=128),
                                                        axis=AX.X, op=ALU.add), reads=[tfB], writes=[n.statB])
                    op("act", lambda: act.activation(out=n.stat[:, 12:12 + nh], in_=n.stat[:, 12:12 + nh], func=AF.Sqrt,
                                                     bias=small[:, 4:5], scale=1.0 / 128), reads=[n.statB, smB],
                       writes=[n.statB])
                    op("dve", lambda: dve.reciprocal(out=n.stat[:, 12:12 + nh], in_=n.stat[:, 12:12 + nh]),
                       reads=[n.statB], writes=[n.statB])
                    cs = n.ropet[:, s, 0:64]
                    sn = n.ropet[:, s, 64:128]
                    rt, rtB = n.rtmp, n.rtmpB
                    qr, qrB = qrotR.next()
                    for i in range(nh):
                        op("dve", lambda i=i: dve.scalar_tensor_tensor(
                            out=rt[0][:], in0=ps[b][:, i * 128:(i + 1) * 128], scalar=n.stat[:, 12 + i:13 + i],
                            in1=gb_t[:], op0=ALU.mult, op1=ALU.mult),
                           reads=[psB[b], n.statB, cB], writes=[rtB[0]])
                        x0 = rt[0][:, 0:128:2]
                        x1 = rt[0][:, 1:128:2]
                        op("dve", lambda: dve.tensor_tensor(out=rt[1][:, 0:64], in0=x0, in1=cs, op=ALU.mult),
                           reads=[rtB[0], n.ropeB], writes=[rtB[1]])
                        op("dve", lambda: dve.tensor_tensor(out=rt[2][:, 0:64], in0=x1, in1=sn, op=ALU.mult),
                           reads=[rtB[0], n.ropeB], writes=[rtB[2]])
                        op("dve", lambda i=i: dve.tensor_tensor(out=qr[:, i, 0:128:2], in0=rt[1][:, 0:64],
                                                                in1=rt[2][:, 0:64], op=ALU.subtract),
                           reads=[rtB[1], rtB[2]], writes=[qrB])
                        op("pool", lambda: pool.tensor_tensor(out=rt[3][:, 0:64], in0=x0, in1=sn, op=ALU.mult),
                           reads=[rtB[0], n.ropeB], writes=[rtB[3]])
                        op("pool", lambda: pool.tensor_tensor(out=rt[4][:, 0:64], in0=x1, in1=cs, op=ALU.mult),
                           reads=[rtB[0], n.ropeB], writes=[rtB[4]])
                        op("pool", lambda i=i: pool.tensor_tensor(out=qr[:, i, 1:128:2], in0=rt[3][:, 0:64],
                                                                  in1=rt[4][:, 0:64], op=ALU.add),
                           reads=[rtB[3], rtB[4]], writes=[qrB])
                    return qr, qrB

                def wslot(slot, c0):
                    wt, wtB, wds = n.wring.items[slot]
                    view = wt[:, 0:16 * 512].rearrange("p (k n) -> p k n", n=512)
                    dma("sp", wds, view, win_s.ap().rearrange("(k p) n -> p k n", p=128)[:, :, c0:c0 + 512],
                        reads=[winB], writes=[wtB])
                    return view, wtB

                RBLK = [O_QA, O_QA + 512, O_KA]
                FBLK = [(O_QB, qbT_s, 0, "f"), (O_QB + 512, qbT_s, 1, "f"), (O_KB, kbT_s, 0, "f"),
                        (O_KB + 512, kbT_s, 1, "f"), (O_VB, vb_s, 0, "t"), (O_VB + 512, vb_s, 1, "t")]
                stFv = [n.actT[:, 4:8, :], n.actT[:, 8:12, :]]
                stFds = [tr.dsem(), tr.dsem()]
                stVv = n.actT[:, 12:16, :]
                stVds = tr.dsem()

                jobs = []
                for nm, (sd, ss, rr, cc) in (("wa", (wa_d, wa_s, 1024, D)), ("wb", (wb_d, wb_s, 1024, D)),
                                             ("wo", (wo_d, wo_s, D, D)), ("w2a", (w2a_d, w2a_s, D, 2 * DFF)),
                                             ("w2b", (w2b_d, w2b_s, DFF, D))):
                    WB[nm] = Buf("w")
                    jobs += conv_jobs(sd, ss, rr, cc, WB[nm])
                per_call = (len(jobs) + 30 * max(NT - 1, 1) - 1) // (30 * max(NT - 1, 1))

                def bgjob():
                    for _ in range(per_call):
                        if jobs:
                            jobs.pop(0)()

                for t in range(NT):
                    load_rows(n, x_d, t)
                    dma("sp", rds, n.ropet[:], rope_d.ap()[t * 512:(t + 1) * 512, :].rearrange("(s p) c -> p s c", p=128),
                        writes=[n.ropeB])
                    norm_to_T(n, g1c)
                    ckpt(2)
                    ffn(n, w1a_s, w1a_cols, w1b_s, w1b_cols, bg=(bgjob if t >= 1 else None))
                    ckpt(3)
                    for s_ in range(4):
                        dma("pool", hds, h_s.ap()[t * 512 + s_ * 128: t * 512 + (s_ + 1) * 128, :], n.xt[:, s_, :],
                            reads=[n.xtB[s_]], writes=[noB], ndesc=8)
                    norm_to_T(n, gmc)
                    dma("pool", uds, uT_s.ap()[t].rearrange("p (k n) -> p k n", n=512), n.xnT[:],
                        reads=[n.xnTB], writes=[noB], ndesc=128)
                    tsl = slice(t * 512, (t + 1) * 512)
                    stFB = [Buf("stF0"), Buf("stF1")]
                    stVB = Buf("stV")
                    for b_ in stFB + [stVB]:
                        b_.inherit([n.actTB])
                    rstate = {}
                    Rw = {0: wslot(0, RBLK[0])}
                    Fw = {0: wslot(1, FBLK[0][0])}
                    Rst = {}

                    def R_mm(k):
                        blk, s = k // 4, k % 4
                        wv, wvB = Rw[blk]
                        if s == 0:
                            Rst[blk] = [n.stgR.next()] + ([(stVv, stVB, stVds)] if blk == 2 else [])
                        b = n.psA.next()
                        tok_major_mm(b, s, wv, wvB)
                        if s == 3 and blk + 1 < 3:
                            Rw[blk + 1] = wslot(0, RBLK[blk + 1])
                        if blk == 2:
                            st2, st2B, _ = Rst[blk][1]
                            op("dve", lambda: dve.tensor_copy(out=st2[:, s, 0:256], in_=ps[b][:, 256:512]),
                               reads=[psB[b]], writes=[st2B])
                        nh = 2 if blk == 2 else 4
                        rstate[k] = (qk_chain(b, s, nh, gkb if blk == 2 else gqb), nh)

                    def R_tail(k):
                        blk, s = k // 4, k % 4
                        (qr, qrB), nh = rstate.pop(k)
                        st, stB, sds = Rst[blk][0]
                        bt = n.psT.next()
                        for i in range(nh):
                            op("pe", lambda i=i: pe.transpose(out=psb[bt][:, i * 128:(i + 1) * 128], in_=qr[:, i, :],
                                                              identity=identb[:]),
                               reads=[qrB, identbB], writes=[psB[bt]], inc=(i == nh - 1))
                        op("act", lambda: act.activation(
                            out=st[:, 0:nh, s * 128:(s + 1) * 128],
                            in_=psb[bt][:, 0:nh * 128].rearrange("p (h t) -> p h t", h=nh), func=AF.Copy),
                           reads=[psB[bt]], writes=[stB])
                        if s == 3:
                            if blk < 2:
                                dma("pool", sds, qaT_s.ap()[blk * 4:(blk + 1) * 4, :, tsl].rearrange("h d t -> d h t"),
                                    st[:], reads=[stB], writes=[noB])
                            else:
                                st2, st2B, sds2 = Rst[blk][1]
                                dma("pool", sds, kaT_s.ap()[:, :, tsl].rearrange("h d t -> d h t"), st[:, 0:2, :],
                                    reads=[stB], writes=[noB])
                                dma("pool", sds2, va_s.ap()[tsl, :].rearrange("(s p) c -> p s c", p=128),
                                    st2[:, :, 0:256], reads=[st2B], writes=[noB])

                    def F_unit(m):
                        blk, u = m // 4, m % 4
                        c0, dst, half, kind = FBLK[blk]
                        wv, wvB = Fw[blk]
                        stv, stB_, sds_ = stFv[blk % 2], stFB[blk % 2], stFds[blk % 2]
                        b = n.psA.next()
                        if kind == "f":
                            for kc in range(16):
                                op("pe", lambda kc=kc: pe.matmul(
                                    ps[b][:, :], lhsT=wv[:, kc, u * 128:(u + 1) * 128], rhs=n.xnT[:, kc, :],
                                    start=(kc == 0), stop=(kc == 15)),
                                   reads=[n.xnTB, wvB], writes=[psB[b]], inc=(kc == 15))
                        else:
                            tok_major_mm(b, u, wv, wvB)
                        if u == 3 and blk + 1 < 6:
                            Fw[blk + 1] = wslot(1, FBLK[blk + 1][0])
                        if u % 2 == 0:
                            op("act", lambda: act.activation(out=stv[:, u, :], in_=ps[b][:, :], func=AF.Copy),
                               reads=[psB[b]], writes=[stB_])
                        else:
                            op("dve", lambda: dve.tensor_copy(out=stv[:, u, :], in_=ps[b][:, :]),
                               reads=[psB[b]], writes=[stB_])
                        if u == 3:
                            if kind == "f":
                                dma("pool", sds_, dst.ap()[half * 4:(half + 1) * 4, :, tsl].rearrange("h d t -> d h t"),
                                    stv, reads=[stB_], writes=[noB])
                            else:
                                dma("pool", sds_,
                                    dst.ap()[tsl, half * 512:(half + 1) * 512].rearrange("(s p) c -> p s c", p=128),
                                    stv, reads=[stB_], writes=[noB])

                    for k in range(12):
                        R_mm(k)
                        F_unit(2 * k)
                        F_unit(2 * k + 1)
                        if k >= 1:
                            R_tail(k - 1)
                    R_tail(11)
                    n.actTB.inherit(stFB + [stVB])
                    ckpt(372)
                    if t == NT - 1:
                        while jobs:
                            jobs.pop(0)()
                tr.barrier()
                ckpt(4)

            with ExitStack() as p2:
                ads = [tr.dsem() for _ in range(8)]
                zt = sb("zt", [120, 160], F32, p2)
                ztB = Buf("zt")
                padB = Buf("pad")
                op("dve", lambda: dve.memset(zt[:], 0.0), writes=[ztB])
                dma("pool", ads[0], pad_s.ap(), zt[:], reads=[ztB], writes=[padB])
                dma("pool", ads[0], pad_s.ap()[:, 64:95], rpb_d.ap(), reads=[], writes=[padB])
                BS = sb("BS", [128, 8, 14 * 64], F32, p2)
                BSB = Buf("BS")
                bstok = None
                for ql in range(2):
                    for qc in range(64):
                        p = ql * 64 + qc
                        src = bass.AP(pad_s, (1 - ql) * 160 + 79 - qc, [[1, 1], [2400, 8], [160, 14], [1, 64]])
                        tr.wait("pool", [padB.w])
                        bstok = dma("pool", ads[1], BS[p:p + 1, :, :].rearrange("p h (r c) -> p h r c", c=64), src, ndesc=16)
                BSB.w = bstok
                maskB = sb("maskB", [128, 9, 896], F32, p2)
                maskBB = Buf("maskB")
                ckpt(45)

                KT = sb("KT", [128, T], BF16, p2)
                KTB = Buf("KT")
                Vg = sb("Vg", [128, NCH, 129], BF16, p2)
                VgB = Buf("Vg")
                op("dve", lambda: dve.memset(Vg[:, :, 128:129], 1.0), writes=[VgB])
                QTt = [sb("QT%d" % i, [128, 4, 512], BF16, p2) for i in range(2)]
                QTR = Ring([(QTt[i], Buf("QT%d" % i), ads[3 + i]) for i in range(2)])
                PTt = [sb("PT%d" % i, [128, 512], BF16, p2) for i in range(3)]
                PTR = Ring([(PTt[i], Buf("PT%d" % i)) for i in range(3)])
                ynt = [sb("yn%d" % i, [128, 128], BF16, p2) for i in range(2)]
                ynR = Ring([(ynt[i], Buf("yn%d" % i)) for i in range(2)])
                ystt = [sb("yst%d" % i, [128, 512], BF16, p2) for i in range(2)]
                ystR = Ring([(ystt[i], Buf("yst%d" % i), ads[5 + i]) for i in range(2)])
                rinv = sb("rinv", [128, 4], F32, p2)
                rinvB = Buf("rinv")
                SR = Ring([4, 5, 6])
                noB = Buf("dram2")
                for g in range(2):
                    dma("sp", ads[7], KT[:], kaT_s.ap()[g, :, :], writes=[KTB])
                    dma("sp", ads[7], Vg[:, :, 0:128],
                        va_s.ap()[:, g * 128:(g + 1) * 128].rearrange("(c p) d -> p c d", p=128), writes=[VgB])
                    items = [(qt, hl, kc) for qt in range(NT) for hl in range(4) for kc in range(NCH)]
                    qts = {}

                    def getQ(qt):
                        if qt not in qts:
                            q_, qB_, qds_ = QTR.next()
                            dma("sp", qds_, q_[:],
                                qaT_s.ap()[g * 4:(g + 1) * 4, :, qt * 512:(qt + 1) * 512].rearrange("h d t -> d h t"),
                                writes=[qB_])
                            qts[qt] = (q_, qB_)
                        return qts[qt]

                    sbank = {}

                    def emitS(idx):
                        qt, hl, kc = items[idx]
                        q_, qB_ = getQ(qt)
                        b = SR.next()
                        sbank[idx] = b
                        op("pe", lambda: pe.matmul(ps[b][:, :], lhsT=KT[:, kc * 128:(kc + 1) * 128], rhs=q_[:, hl, :],
                                                   start=True, stop=True), reads=[KTB, qB_], writes=[psB[b]])

                    emitS(0)
                    emitS(1)
                    if g == 0:
                        dma("sp", ads[2], maskB[:], maskB_d.ap().rearrange("p (t c) -> p t c", c=896), writes=[maskBB])
                    for idx, (qt, hl, kc) in enumerate(items):
                        if idx + 2 < len(items):
                            emitS(idx + 2)
                        b = sbank.pop(idx)
                        mi = (2 if kc >= NCH // 2 else 0) + (1 if qt >= NT // 2 else 0)
                        pt, ptB = PTR.next()
                        op("act", lambda: act.activation(out=pt[:], in_=ps[b][:, :], func=AF.Exp,
                                                         bias=biasA[:, mi:mi + 1], scale=SCALE),
                           reads=[psB[b], biasAB], writes=[ptB])
                        for qs in range(4):
                            op("pe", lambda qs=qs: pe.matmul(ps[qs][:, 0:129], lhsT=pt[:, qs * 128:(qs + 1) * 128],
                                                             rhs=Vg[:, kc, :], start=(kc == 0), stop=(kc == NCH - 1)),
                               reads=[ptB, VgB], writes=[psB[qs]], inc=(qs == 3))
                        if kc == NCH - 1:
                            h = g * 4 + hl
                            yst, ystB, yds = ystR.next()
                            for qs in range(4):
                                op("dve", lambda qs=qs: dve.reciprocal(out=rinv[:, qs:qs + 1], in_=ps[qs][:, 128:129]),
                                   reads=[psB[qs]], writes=[rinvB])
                                yn, ynB = ynR.next()
                                op("dve", lambda qs=qs, yn=yn: dve.tensor_scalar(out=yn[:], in0=ps[qs][:, 0:128],
                                                                                 scalar1=rinv[:, qs:qs + 1], scalar2=None,
                                                                                 op0=ALU.mult),
                                   reads=[psB[qs], rinvB], writes=[ynB])
                                op("pe", lambda qs=qs, yn=yn: pe.transpose(out=psb[7][:, qs * 128:(qs + 1) * 128],
                                                                           in_=yn[:], identity=identb[:]),
                                   reads=[ynB, identbB], writes=[psB[7]])
                            op("act", lambda yst=yst: act.activation(out=yst[:], in_=psb[7][:, 0:512], func=AF.Copy),
                               reads=[psB[7]], writes=[ystB])
                            dma("pool", yds, yaT_s.ap()[h, :, qt * 512:(qt + 1) * 512], yst[:], reads=[ystB], writes=[noB])

                ckpt(5)
                master = sb("master", [128, 8, 896], F32, p2)
                masterB = Buf("master")
                for h in range(8):
                    for (bk, d0, nd) in ((4, 0, 4), (5, 4, 3)):
                        for i in range(nd):
                            di = d0 + i
                            op("pe", lambda i=i, di=di, bk=bk: pe.transpose(
                                out=ps[bk][:, i * 128:(i + 1) * 128], in_=BS[:, h, di * 128:(di + 1) * 128],
                                identity=ident[:]),
                               reads=[BSB, cB], writes=[psB[bk]], inc=(i == nd - 1))
                        op("dve", lambda bk=bk, d0=d0, nd=nd: dve.tensor_copy(
                            out=master[:, h, d0 * 128:(d0 + nd) * 128], in_=ps[bk][:, 0:nd * 128]),
                           reads=[psB[bk]], writes=[masterB])
                combt = [sb("comb%d" % i, [128, 896], F32, p2) for i in range(2)]
                combBs = [Buf("comb%d" % i) for i in range(2)]

                def build_comb(h):
                    op("pool", lambda: pool.tensor_tensor(out=combt[h % 2][:], in0=master[:, h, :], in1=maskB[:, 2, :],
                                                          op=ALU.add),
                       reads=[masterB, maskBB], writes=[combBs[h % 2]])
                KTb = sb("KTb", [128, T], BF16, p2)
                QTb = sb("QTb", [128, T], BF16, p2)
                Vb = sb("Vb", [128, NCH, 129], BF16, p2)
                QTb2 = sb("QTb2", [128, T], BF16, p2)
                vinitB = Buf("vinit")
                op("dve", lambda: dve.memset(Vb[:, :, 128:129], 1.0), writes=[vinitB])
                sets = [(KTb, QTb, Vb, [Buf("k0"), Buf("q0"), Buf("v0")], [tr.dsem(), tr.dsem(), tr.dsem()]),
                        (KT, QTb2, Vg, [Buf("k1"), Buf("q1"), Buf("v1")], [tr.dsem(), tr.dsem(), tr.dsem()])]
                sets[0][3][2].w = vinitB.w
                sets[1][3][0].inherit([KTB])
                sets[1][3][2].inherit([VgB])

                def loadset(h):
                    k_, q_, v_, bs_, ds_ = sets[h % 2]
                    dma("sp", ds_[0], k_[:], kbT_s.ap()[h, :, :], writes=[bs_[0]])
                    dma("sp", ds_[1], q_[:], qbT_s.ap()[h, :, :], writes=[bs_[1]])
                    dma("sp", ds_[2], v_[:, :, 0:128],
                        vb_s.ap()[:, h * 128:(h + 1) * 128].rearrange("(c p) d -> p c d", p=128), writes=[bs_[2]])

                sTt = [sb("sT%d" % i, [128, 896], F32, p2) for i in range(2)]
                sTR = Ring([(sTt[i], Buf("sT%d" % i)) for i in range(2)])
                PBt = [sb("PB%d" % i, [128, 896], BF16, p2) for i in range(3)]
                PBR = Ring([(PBt[i], Buf("PB%d" % i)) for i in range(3)])
                ynb = [sb("ynb%d" % i, [128, 128], BF16, p2) for i in range(3)]
                ynbR = Ring([(ynb[i], Buf("ynb%d" % i)) for i in range(3)])
                rinvb = sb("rinvb", [128, 4], F32, p2)
                rinvbB = Buf("rinvb")
                SBR = Ring([(0, 1), (2, 3)])
                ACR = Ring([4, 5])
                TBR = Ring([6, 7])

                def ti_of(j):
                    if j == 0:
                        return 0
                    if j == 1:
                        return 1
                    if NBLK // 2 - 2 <= j <= NBLK // 2 + 1:
                        return 3 + j - (NBLK // 2 - 2)
                    if j == NBLK - 2:
                        return 7
                    if j == NBLK - 1:
                        return 8
                    return 2

                loadset(0)
                build_comb(0)
                for h in range(8):
                    if h + 1 < 8:
                        loadset(h + 1)
                        build_comb(h + 1)
                    comb, combB = combt[h % 2], combBs[h % 2]
                    KTh, QTh, Vh, (kB_, qB_, vB_), _ = sets[h % 2]
                    st_ = {}

                    def stA(j):
                        ba, bb = SBR.next()
                        st_[j] = {"s": (ba, bb)}
                        for di in range(7):
                            c = min(max(j - 3 + di, 0), NCH - 1)
                            bk, off = (ba, di * 128) if di < 4 else (bb, (di - 4) * 128)
                            op("pe", lambda c=c, bk=bk, off=off: pe.matmul(
                                ps[bk][:, off:off + 128], lhsT=KTh[:, c * 128:(c + 1) * 128],
                                rhs=QTh[:, j * 128:(j + 1) * 128], start=True, stop=True),
                               reads=[kB_, qB_], writes=[psB[bk]], inc=(di == 3 or di == 6))

                    def stB(j):
                        ba, bb = st_[j]["s"]
                        ti = ti_of(j)
                        sT, sTB = sTR.next()
                        if ti == 2:
                            bsrc, bsB = comb, combB
                        else:
                            bsrc, bsB = master[:, h, :], masterB
                        op("dve", lambda: dve.scalar_tensor_tensor(out=sT[:, 0:512], in0=ps[ba][:, :], scalar=SCALE,
                                                                   in1=bsrc[:, 0:512], op0=ALU.mult, op1=ALU.add),
                           reads=[psB[ba], bsB], writes=[sTB])
                        op("dve", lambda: dve.scalar_tensor_tensor(out=sT[:, 512:896], in0=ps[bb][:, 0:384],
                                                                   scalar=SCALE, in1=bsrc[:, 512:896],
                                                                   op0=ALU.mult, op1=ALU.add),
                           reads=[psB[bb], bsB], writes=[sTB])
                        if ti != 2:
                            op("pool", lambda: pool.tensor_tensor(out=sT[:], in0=sT[:], in1=maskB[:, ti, :], op=ALU.add),
                               reads=[maskBB], writes=[sTB])
                        pb, pbB = PBR.next()
                        op("act", lambda: act.activation(out=pb[:], in_=sT[:], func=AF.Exp), reads=[sTB], writes=[pbB])
                        st_[j]["p"] = (pb, pbB)

                    def stC(j):
                        pb, pbB = st_[j]["p"]
                        ac = ACR.next()
                        st_[j]["ac"] = ac
                        for di in range(7):
                            c = min(max(j - 3 + di, 0), NCH - 1)
                            op("pe", lambda di=di, c=c: pe.matmul(ps[ac][:, 0:129], lhsT=pb[:, di * 128:(di + 1) * 128],
                                                                  rhs=Vh[:, c, :], start=(di == 0), stop=(di == 6)),
                               reads=[pbB, vB_], writes=[psB[ac]], inc=(di == 6))

                    def stD(j):
                        ac = st_[j]["ac"]
                        op("dve", lambda: dve.reciprocal(out=rinvb[:, j % 4:j % 4 + 1], in_=ps[ac][:, 128:129]),
                           reads=[psB[ac]], writes=[rinvbB])
                        yn, ynB = ynbR.next()
                        op("dve", lambda: dve.tensor_scalar(out=yn[:], in0=ps[ac][:, 0:128],
                                                            scalar1=rinvb[:, j % 4:j % 4 + 1], scalar2=None, op0=ALU.mult),
                           reads=[psB[ac], rinvbB], writes=[ynB])
                        st_[j]["yn"] = (yn, ynB)

                    tbs = {}

                    def stE(j):
                        yn, ynB = st_[j]["yn"]
                        if j % 4 == 0:
                            tbs[j // 4] = TBR.next()
                        tb = tbs[j // 4]
                        op("pe", lambda: pe.transpose(out=psb[tb][:, (j % 4) * 128:(j % 4 + 1) * 128], in_=yn[:],
                                                      identity=identb[:]), reads=[ynB, identbB], writes=[psB[tb]])
                        if j % 4 == 3:
                            yst, ystB, yds = ystR.next()
                            op("act", lambda: act.activation(out=yst[:], in_=psb[tb][:, 0:512], func=AF.Copy),
                               reads=[psB[tb]], writes=[ystB])
                            dma("pool", yds, ybT_s.ap()[h, :, (j - 3) * 128:(j + 1) * 128], yst[:], reads=[ystB],
                                writes=[noB], ndesc=8)
                        del st_[j]

                    for i in range(-2, NBLK + 2):
                        if 0 <= i + 2 < NBLK:
                            stA(i + 2)
                        if 0 <= i + 1 < NBLK:
                            stB(i + 1)
                        if 0 <= i < NBLK:
                            stC(i)
                        if 0 <= i - 1 < NBLK:
                            stD(i - 1)
                        if 0 <= i - 2 < NBLK:
                            stE(i - 2)
                tr.barrier()
                ckpt(6)

            with ExitStack() as p3:
                n = mk13(p3, 3)
                mT = n.actT[:, 0:16, :]
                yaT = n.actT[:, 16:24, :]
                ybT = n.actT[:, 24:32, :]
                yds = tr.dsem()
                ods = tr.dsem()
                noB = Buf("dram3")
                out_toks = []
                uds3 = tr.dsem()

                def prefetch_u(t):
                    dma("sp", uds3, n.xnT[:], uT_s.ap()[t].rearrange("p (k n) -> p k n", n=512), writes=[n.xnTB])

                def gslot(slot, c0):
                    wt, wtB, wds = n.wring.items[slot]
                    view = wt[:, 0:16 * 512].rearrange("p (k n) -> p k n", n=512)
                    dma("sp", wds, view, win_s.ap().rearrange("(k p) n -> p k n", p=128)[:, :, c0:c0 + 512],
                        reads=[winB], writes=[wtB])
                    return view, wtB

                def abslot(fb):
                    w2t, w2B, w2ds = n.w2ring.items[fb % 2]
                    view = w2t[:, :, :].rearrange("p a b -> p (a b)")[:, 0:8192].rearrange("p (r k n) -> p r k n", r=2, k=8)
                    for r_, (wsrc, wk) in enumerate(((wa_s, "wa"), (wb_s, "wb"))):
                        dma("sp", w2ds, view[:, r_, :, :],
                            wsrc.ap().rearrange("(k p) n -> p k n", p=128)[:, :, fb * 512:(fb + 1) * 512],
                            reads=[WB[wk]], writes=[w2B])
                    return view, w2B

                gpre = {}

                def issue_first(t):
                    gpre[t] = {"ab": {0: abslot(0)}, "ga": {0: gslot(0, O_GA)}, "gb": {0: gslot(1, O_GB)}}

                prefetch_u(0)
                issue_first(0)
                for t in range(NT):
                    tsl = slice(t * 512, (t + 1) * 512)
                    dma("sp", yds, yaT, yaT_s.ap()[:, :, tsl].rearrange("h d t -> d h t"), writes=[n.actTB])
                    dma("sp", yds, ybT, ybT_s.ap()[:, :, tsl].rearrange("h d t -> d h t"), writes=[n.actTB])
                    yB = Buf("y")
                    yB.w = n.actTB.w
                    mB = Buf("m")
                    mB.inherit([n.actTB])
                    G = gpre.pop(t)
                    for fb in range(4):
                        if fb + 1 < 4:
                            G["ab"][fb + 1] = abslot(fb + 1)
                        abv, abB = G["ab"][fb]
                        tA = []
                        for br in range(2):
                            yv = (yaT, ybT)[br]
                            wg, wgB = G[("ga", "gb")[br]][fb]
                            for fc in range(4):
                                ch = fb * 4 + fc
                                bz, bgt = n.psA.next(), n.psA.next()
                                for kc in range(16):
                                    op("pe", lambda kc=kc, fc=fc: pe.matmul(
                                        ps[bgt][:, :], lhsT=wg[:, kc, fc * 128:(fc + 1) * 128], rhs=n.xnT[:, kc, :],
                                        start=(kc == 0), stop=(kc == 15)),
                                       reads=[wgB, n.xnTB], writes=[psB[bgt]], inc=(kc == 15))
                                for h in range(8):
                                    op("pe", lambda h=h, fc=fc: pe.matmul(
                                        ps[bz][:, :], lhsT=abv[:, br, h, fc * 128:(fc + 1) * 128], rhs=yv[:, h, :],
                                        start=(h == 0), stop=(h == 7)),
                                       reads=[abB, yB], writes=[psB[bz]], inc=(h == 7))
                                if br == 0:
                                    tf, tfB = n.tmpR.next()
                                    tA.append((tf, tfB))
                                else:
                                    tf, tfB = n.sgR.next()
                                op("act", lambda: act.activation(out=tf[:], in_=ps[bgt][:, :], func=AF.Sigmoid,
                                                                 bias=bgc[:, br * 16 + ch: br * 16 + ch + 1]),
                                   reads=[psB[bgt], cB], writes=[tfB])
                                op("dve", lambda: dve.tensor_tensor(out=tf[:], in0=tf[:], in1=ps[bz][:, :], op=ALU.mult),
                                   reads=[tfB, psB[bz]], writes=[tfB])
                                if br == 1:
                                    ta, taB = tA[fc]
                                    op("pool", lambda: pool.tensor_tensor(out=mT[:, ch, :], in0=ta[:], in1=tf[:],
                                                                          op=ALU.add),
                                       reads=[taB, tfB], writes=[mB])
                            if fb + 1 < 4:
                                key = ("ga", "gb")[br]
                                G[key][fb + 1] = gslot(br, (O_GA, O_GB)[br] + (fb + 1) * 512)
                    load_rows(n, h_s, t)
                    for db in range(4):
                        wv, wvB = wblock(n, wo_s, WB["wo"], 16, db * 512)
                        for s in range(4):
                            b = n.psA.next()
                            for kc in range(16):
                                op("pe", lambda kc=kc, s=s: pe.matmul(ps[b][:, :], lhsT=mT[:, kc, s * 128:(s + 1) * 128],
                                                                      rhs=wv[:, kc, :], start=(kc == 0), stop=(kc == 15)),
                                   reads=[mB, wvB], writes=[psB[b]], inc=(kc == 15))
                            op("dve", lambda b=b, s=s, db=db: dve.tensor_tensor(
                                out=n.xt[:, s, db * 512:(db + 1) * 512], in0=ps[b][:, :],
                                in1=n.xt[:, s, db * 512:(db + 1) * 512], op=ALU.add),
                               reads=[psB[b]], writes=[n.xtB[s]])
                    n.actTB.inherit([yB, mB])
                    norm_to_T(n, g2c)
                    ffn(n, w2a_s, WB["w2a"], w2b_s, WB["w2b"],
                        mid_hook=((lambda t=t: prefetch_u(t + 1)) if t + 1 < NT else None))
                    if t + 1 < NT:
                        issue_first(t + 1)
                    row_stats(n)
                    for s in range(4):
                        eng_, e_ = ("dve", dve)
                        op(eng_, lambda s=s, e_=e_: e_.scalar_tensor_tensor(out=n.xt[:, s, :], in0=n.xt[:, s, :],
                                                                            scalar=n.stat[:, 8 + s:9 + s], in1=n.gfb[:],
                                                                            op0=ALU.mult, op1=ALU.mult),
                           reads=[n.statB, n.gfbB], writes=[n.xtB[s]])
                        out_toks.append(dma("pool", ods, out_d.ap()[t * 512 + s * 128: t * 512 + (s + 1) * 128, :],
                                            n.xt[:, s, :], reads=[n.xtB[s]], writes=[noB], ndesc=8))
                tr.barrier()
    except _Stop:
        pass
    return nc


def _geometry(T, mode):
    tok = np.arange(T)
    if mode == "prompt":
        return np.zeros(T, np.int64), tok, T // GRID_W
    half = T // 2
    return tok // half, tok % half, half // GRID_W


def _rope_table(T, mode):
    _, pos, _ = _geometry(T, mode)
    row = (pos // GRID_W).astype(np.float32)
    col = (pos % GRID_W).astype(np.float32)
    npairs = HD // 4
    inv = (10000.0 ** (-np.arange(npairs, dtype=np.float32) / npairs)).astype(np.float32)
    ang = np.concatenate([row[:, None] * inv[None], col[:, None] * inv[None]], axis=-1)
    return np.concatenate([np.cos(ang), np.sin(ang)], axis=-1).astype(np.float32)


def _ti_blocks(nblk):
    reps = [0, 1, None] + [nblk // 2 - 2 + i for i in range(4)] + [nblk - 2, nblk - 1]
    special = set(r for r in reps if r is not None)
    interior = [j for j in range(nblk) if j not in special]
    reps[2] = interior[0] if interior else None
    return reps


def _mask_b(T, mode):
    seq, pos, rows = _geometry(T, mode)
    nblk = T // 128
    r_of = pos // GRID_W
    c_of = pos % GRID_W
    rs = np.clip(r_of - 4, 0, rows - 8)
    cs = np.clip(c_of - 8, 0, GRID_W - 16)
    out = np.full((128, 9, 7, 128), NEG, np.float32)
    for ti, j in enumerate(_ti_blocks(nblk)):
        if j is None:
            continue
        q = j * 128 + np.arange(128)
        for di in range(7):
            c = j - 3 + di
            if c < 0 or c >= nblk:
                continue
            k = c * 128 + np.arange(128)
            ok = (seq[k][:, None] == seq[q][None, :])
            ok &= (r_of[k][:, None] >= rs[q][None, :]) & (r_of[k][:, None] < rs[q][None, :] + 8)
            ok &= (c_of[k][:, None] >= cs[q][None, :]) & (c_of[k][:, None] < cs[q][None, :] + 16)
            out[:, ti, di, :] = np.where(ok, 0.0, NEG)
    return out.reshape(128, 9 * 896)


def _mask_a(mode):
    m = np.zeros((128, 4), np.float32)
    if mode != "prompt":
        m[:, 1] = NEG
        m[:, 2] = NEG
    return m


def _shared_inputs(g_ffn1, w_ffn1_in, w_ffn1_out, g_mix, w_in, b_gate, g_q_a, g_k_a, rpb_b, w_branch_a,
                   w_branch_b, w_out, g_ffn2, w_ffn2_in, w_ffn2_out, g_final):
    f = lambda a: np.ascontiguousarray(np.asarray(a, dtype=np.float32))
    col = lambda g: f(np.asarray(g).reshape(-1, 128).T)
    return {
        "ident": np.eye(128, dtype=np.float32),
        "g1c": col(g_ffn1), "gmc": col(g_mix), "g2c": col(g_ffn2),
        "gfb": f(np.broadcast_to(np.asarray(g_final).reshape(1, D), (128, D))),
        "bgc": col(b_gate),
        "gqb": f(np.broadcast_to(np.asarray(g_q_a).reshape(1, 128), (128, 128))),
        "gkb": f(np.broadcast_to(np.asarray(g_k_a).reshape(1, 128), (128, 128))),
        "rpb": f(np.asarray(rpb_b).reshape(120, 31)),
        "w1a": f(np.asarray(w_ffn1_in)[0]), "w1b": f(np.asarray(w_ffn1_out)[0]),
        "win": f(np.asarray(w_in)[0]), "wa": f(np.asarray(w_branch_a)[0]), "wb": f(np.asarray(w_branch_b)[0]),
        "wo": f(np.asarray(w_out)[0]), "w2a": f(np.asarray(w_ffn2_in)[0]), "w2b": f(np.asarray(w_ffn2_out)[0]),
    }


def _slot_inputs(T, mode, x):
    return {"x": np.ascontiguousarray(x, dtype=np.float32), "rope": _rope_table(T, mode),
            "maskA": _mask_a(mode), "maskB": _mask_b(T, mode)}


_NC_CACHE = {}


def kernel(x_prompt, x_sample, g_ffn1, w_ffn1_in, w_ffn1_out, g_mix, w_in, b_gate, g_q_a, g_k_a,
           rpb_b, w_branch_a, w_branch_b, w_out, g_ffn2, w_ffn2_in, w_ffn2_out, g_final):
    x_prompt = np.asarray(x_prompt, dtype=np.float32)
    x_sample = np.asarray(x_sample, dtype=np.float32)
    T = x_prompt.shape[1]
    assert x_prompt.shape[0] == 2 and x_sample.shape[0] == 8 and x_sample.shape[1] * 2 == T
    shared = _shared_inputs(g_ffn1, w_ffn1_in, w_ffn1_out, g_mix, w_in, b_gate, g_q_a, g_k_a, rpb_b,
                            w_branch_a, w_branch_b, w_out, g_ffn2, w_ffn2_in, w_ffn2_out, g_final)
    in_maps = []
    for c in range(8):
        if c < 2:
            m = _slot_inputs(T, "prompt", x_prompt[c])
        elif c < 6:
            i = c - 2
            m = _slot_inputs(T, "sample", np.concatenate([x_sample[2 * i], x_sample[2 * i + 1]], axis=0))
        else:
            m = _slot_inputs(T, "prompt", np.zeros((T, D), np.float32))
        m.update(shared)
        in_maps.append(m)
    if T not in _NC_CACHE:
        _NC_CACHE[T] = build_nc(T)
    res = run_bass_kernel_spmd(_NC_CACHE[T], in_maps, core_ids=list(range(8)))
    outs = [np.asarray(r["out"], dtype=np.float32) for r in res.results]
    y_prompt = np.stack([outs[0], outs[1]], axis=0)
    half = T // 2
    y_sample = np.stack([outs[2 + i // 2][(i % 2) * half:(i % 2 + 1) * half] for i in range(8)], axis=0)
    return (y_prompt, y_sample)
```

```python
import math
from contextlib import ExitStack

import numpy as np

import concourse.bass as bass
import concourse.mybir as mybir
from concourse.bass_utils import run_bass_kernel_spmd

F32 = mybir.dt.float32
BF16 = mybir.dt.bfloat16
AF = mybir.ActivationFunctionType
ALU = mybir.AluOpType
AX = mybir.AxisListType

D = 2048
DFF = 5504
NFC = DFF // 128
HD = 128
INW = 8704
EPS = 1e-6
GRID_W = 64
NEG = -30000.0
SCALE = HD ** -0.5

O_QA, O_KA, O_VA, O_QB, O_KB, O_VB, O_GA, O_GB = 0, 1024, 1280, 1536, 2560, 3584, 4608, 6656


class Tok:
    __slots__ = ("key", "sem", "val")

    def __init__(self, key, sem, val):
        self.key, self.sem, self.val = key, sem, val


class Buf:
    def __init__(self, name=""):
        self.name = name
        self.w = None
        self.r = {}

    def inherit(self, others):
        for o in others:
            if isinstance(o.w, list):
                for i_, t_ in enumerate(o.w):
                    self.r[("wl", t_.key)] = t_
                continue
            if o.w is not None:
                self.r[("w", o.w.key)] = o.w if (("w", o.w.key) not in self.r or self.r[("w", o.w.key)].val < o.w.val) else self.r[("w", o.w.key)]
            for k, t in o.r.items():
                if k not in self.r or self.r[k].val < t.val:
                    self.r[k] = t


class DSem:
    def __init__(self, sem, idx):
        self.sem, self.val, self.key = sem, 0, ("d", idx)


class Tr:
    def __init__(self, nc, es):
        self.nc = nc
        self.es = es
        self.engs = {"pe": nc.tensor, "act": nc.scalar, "dve": nc.vector, "pool": nc.gpsimd, "sp": nc.sync}
        self.sem = {k: es.enter_context(nc.semaphore("c_" + k)) for k in ("pe", "act", "dve", "pool")}
        self.cnt = {k: 0 for k in self.sem}
        self.seen = {}
        self.nds = 0
        self.nwait = 0
        self.pend = {k: ([], []) for k in self.sem}
        self.dsems = []
        self.pool_fifo = []
        self.pool_out = []

    def dsem(self):
        self.nds += 1
        d = DSem(self.es.enter_context(self.nc.semaphore("d%d" % self.nds)), self.nds)
        self.dsems.append(d)
        return d

    def barrier(self):
        toks = [Tok(k, self.sem[k], self.cnt[k]) for k in self.sem if self.cnt[k] > 0]
        toks += [Tok(d.key, d.sem, d.val) for d in self.dsems if d.val > 0]
        for e in ("pe", "act", "dve", "pool", "sp"):
            self.wait(e, toks)

    def wait(self, eng, toks):
        for t in toks:
            if t is None:
                continue
            if eng == "pe" and t.key == "pe":
                continue
            k = (eng, t.key)
            if self.seen.get(k, 0) < t.val:
                self.engs[eng].wait_ge(t.sem, t.val)
                self.seen[k] = t.val
                self.nwait += 1

    @staticmethod
    def deps(reads, writes):
        d = []
        for b in reads:
            d.extend(b.w if isinstance(b.w, list) else [b.w])
            if b.name.startswith("ps"):
                d.extend(b.r.values())
        for b in writes:
            d.extend(b.w if isinstance(b.w, list) else [b.w])
            d.extend(b.r.values())
        return d

    @staticmethod
    def mark(tok, reads, writes):
        for b in writes:
            b.w = tok
            b.r = {}
        for b in reads:
            o = b.r.get(tok.key)
            if o is None or o.val < tok.val:
                b.r[tok.key] = tok

    def op(self, eng, fn, reads=(), writes=(), inc=True, extra=()):
        self.wait(eng, self.deps(reads, writes) + list(extra))
        ins = fn()
        if not inc:
            self.pend[eng][0].extend(reads)
            self.pend[eng][1].extend(writes)
            return None
        self.cnt[eng] += 1
        ins.then_inc(self.sem[eng], 1)
        tok = Tok(eng, self.sem[eng], self.cnt[eng])
        pr, pw = self.pend[eng]
        self.mark(tok, list(reads) + pr, list(writes) + pw)
        self.pend[eng] = ([], [])
        return tok

    def dma(self, q, ds, out, in_, reads=(), writes=(), extra=(), ndesc=32, **kw):
        self.wait(q, self.deps(reads, writes) + list(extra))
        if q == "pool":
            self.pool_out.append(None)
            while sum(n_ for _, n_ in self.pool_fifo) + ndesc > 480 and self.pool_fifo:
                t_, _ = self.pool_fifo.pop(0)
                self.wait("pool", [t_])
            self.pool_out.pop()
        if ds.val > 0:
            self.wait(q, [Tok(ds.key, ds.sem, ds.val)])
        ins = self.engs[q].dma_start(out=out, in_=in_, **kw)
        ds.val += 16
        ins.then_inc(ds.sem, 16)
        tok = Tok(ds.key, ds.sem, ds.val)
        self.mark(tok, reads, writes)
        if q == "pool":
            self.pool_fifo.append((tok, ndesc))
        return tok


class Ring:
    def __init__(self, items):
        self.items = items
        self.i = 0

    def next(self):
        it = self.items[self.i % len(self.items)]
        self.i += 1
        return it


class _Stop(Exception):
    pass


def build_nc(T, stop=None):
    NT = T // 512
    NCH = T // 128
    NBLK = NCH
    nc = bass.Bass("TRN2", target_bir_lowering=False)

    def din(name, shape, dt=F32):
        return nc.dram_tensor(name, list(shape), dt, kind="ExternalInput")

    x_d = din("x", [T, D])
    rope_d = din("rope", [T, 128])
    maskA_d = din("maskA", [128, 4])
    maskB_d = din("maskB", [128, 9 * 896])
    ident_d = din("ident", [128, 128])
    g1c_d = din("g1c", [128, 16])
    gmc_d = din("gmc", [128, 16])
    g2c_d = din("g2c", [128, 16])
    gfb_d = din("gfb", [128, D])
    bgc_d = din("bgc", [128, 32])
    gqb_d = din("gqb", [128, 128])
    gkb_d = din("gkb", [128, 128])
    rpb_d = din("rpb", [120, 31])
    w1a_d = din("w1a", [D, 2 * DFF])
    w1b_d = din("w1b", [DFF, D])
    win_d = din("win", [D, INW])
    wa_d = din("wa", [1024, D])
    wb_d = din("wb", [1024, D])
    wo_d = din("wo", [D, D])
    w2a_d = din("w2a", [D, 2 * DFF])
    w2b_d = din("w2b", [DFF, D])
    out_d = nc.dram_tensor("out", [T, D], F32, kind="ExternalOutput")

    def dscr(name, shape, dt=BF16):
        return nc.dram_tensor(name, list(shape), dt)

    w1a_s = dscr("w1a_s", [D, 2 * DFF])
    w1b_s = dscr("w1b_s", [DFF, D])
    win_s = dscr("win_s", [D, INW])
    wa_s = dscr("wa_s", [1024, D])
    wb_s = dscr("wb_s", [1024, D])
    wo_s = dscr("wo_s", [D, D])
    w2a_s = dscr("w2a_s", [D, 2 * DFF])
    w2b_s = dscr("w2b_s", [DFF, D])
    h_s = dscr("h_s", [T, D], F32)
    qaT_s = dscr("qaT_s", [8, 128, T])
    kaT_s = dscr("kaT_s", [2, 128, T])
    va_s = dscr("va_s", [T, 256])
    qbT_s = dscr("qbT_s", [8, 128, T])
    kbT_s = dscr("kbT_s", [8, 128, T])
    vb_s = dscr("vb_s", [T, 1024])
    yaT_s = dscr("yaT_s", [8, 128, T])
    ybT_s = dscr("ybT_s", [8, 128, T])
    pad_s = dscr("pad_s", [120, 160], F32)
    uT_s = dscr("uT_s", [NT, 128, 16 * 512])

    try:
        with ExitStack() as es:
            tr = Tr(nc, es)

            def ckpt(k):
                if stop == k:
                    tr.barrier()
                    raise _Stop()
            op, dma = tr.op, tr.dma
            pe, act, dve, pool = nc.tensor, nc.scalar, nc.vector, nc.gpsimd

            _uid = [0]

            def sb(name, shape, dt, stack=es):
                _uid[0] += 1
                return stack.enter_context(nc.sbuf_tensor("s%d_%s" % (_uid[0], name), list(shape), dt))

            ps = [es.enter_context(nc.psum_tensor("ps%d" % i, [128, 512], F32)) for i in range(8)]
            psb = [p.bitcast(BF16) for p in ps]
            psB = [Buf("ps%d" % i) for i in range(8)]

            ident = sb("ident", [128, 128], F32)
            identb = sb("identb", [128, 128], BF16)
            g1c = sb("g1c", [128, 16], F32)
            gmc = sb("gmc", [128, 16], F32)
            g2c = sb("g2c", [128, 16], F32)
            bgc = sb("bgc", [128, 32], F32)
            gqb = sb("gqb", [128, 128], F32)
            gkb = sb("gkb", [128, 128], F32)
            maskA = sb("maskA", [128, 4], F32)
            biasA = sb("biasA", [128, 4], F32)
            small = sb("small", [128, 8], F32)
            cB = Buf("consts")
            cds = tr.dsem()
            for t_sb, t_d in ((ident, ident_d), (g1c, g1c_d), (gmc, gmc_d), (g2c, g2c_d), (bgc, bgc_d),
                              (gqb, gqb_d), (gkb, gkb_d), (maskA, maskA_d)):
                dma("sp", cds, t_sb[:], t_d.ap(), writes=[cB])
            identbB = Buf("identb")
            op("dve", lambda: dve.tensor_copy(out=identb[:], in_=ident[:]), reads=[cB], writes=[identbB])
            smB = Buf("small")
            op("dve", lambda: dve.tensor_reduce(out=small[:, 0:1], in_=gqb[:], axis=AX.X, op=ALU.max,
                                                apply_absolute_value=True), reads=[cB], writes=[smB])
            op("dve", lambda: dve.tensor_reduce(out=small[:, 1:2], in_=gkb[:], axis=AX.X, op=ALU.max,
                                                apply_absolute_value=True), reads=[cB, smB], writes=[smB])
            op("dve", lambda: dve.tensor_tensor(out=small[:, 2:3], in0=small[:, 0:1], in1=small[:, 1:2], op=ALU.mult),
               reads=[smB], writes=[smB])
            op("dve", lambda: dve.tensor_scalar(out=small[:, 3:4], in0=small[:, 2:3], scalar1=-math.sqrt(128.0),
                                                scalar2=None, op0=ALU.mult), reads=[smB], writes=[smB])
            biasAB = Buf("biasA")
            op("dve", lambda: dve.tensor_scalar(out=biasA[:], in0=maskA[:], scalar1=small[:, 3:4], scalar2=None,
                                                op0=ALU.add), reads=[smB, cB], writes=[biasAB])
            op("dve", lambda: dve.memset(small[:, 4:5], EPS), writes=[smB])
            ckpt(0)

            def convert(src, dst, rows, cols):
                ds = tr.dsem()
                b = Buf("w")
                tok = None
                for r0 in range(0, rows, 128):
                    o = dst.ap()[r0:r0 + 128, :]
                    i = src.ap()[r0:r0 + 128, :]
                    if cols > 2048:
                        o = o.rearrange("r (a b) -> r a b", a=8)
                        i = i.rearrange("r (a b) -> r a b", a=8)
                    tok = dma("pool", ds, o, i, ndesc=(64 if cols > 2048 else 8))
                b.w = tok
                return b

            def convert_cols(src, dst, rows, bw, order):
                bufs = {}
                lanes = [tr.dsem() for _ in range(4)]
                for a in order:
                    toks = {}
                    for i_, r0 in enumerate(range(0, rows, 128)):
                        toks[i_ % 4] = dma("pool", lanes[i_ % 4], dst.ap()[r0:r0 + 128, a * bw:(a + 1) * bw],
                                           src.ap()[r0:r0 + 128, a * bw:(a + 1) * bw], ndesc=8)
                    b = Buf("w")
                    b.w = list(toks.values())
                    bufs[a] = b
                return bufs

            def conv_jobs(src, dst, rows, cols, b):
                ds = tr.dsem()
                jobs = []
                cw = cols // 8 if cols > 2048 else cols
                for r0 in range(0, rows, 128):
                    for c0 in range(0, cols, cw):
                        def job(r0=r0, c0=c0):
                            b.w = dma("pool", ds, dst.ap()[r0:r0 + 128, c0:c0 + cw], src.ap()[r0:r0 + 128, c0:c0 + cw],
                                      ndesc=8)
                        jobs.append(job)
                return jobs

            OCT = 2 * DFF // 8
            w1a_oct = convert_cols(w1a_d, w1a_s, D, OCT, [0, 4, 1, 5, 2, 6, 3, 7])
            w1b_q = convert_cols(w1b_d, w1b_s, DFF, 512, [0, 1, 2, 3])
            winB = convert(win_d, win_s, D, INW)

            def w1a_cols(c0, c1):
                return [w1a_oct[a] for a in range(c0 // OCT, (c1 - 1) // OCT + 1)]

            def w1b_cols(c0, c1):
                return [w1b_q[a] for a in range(c0 // 512, (c1 - 1) // 512 + 1)]
            ckpt(1)
            WB = {}

            class NS:
                pass

            def mk13(stk, phase):
                n = NS()
                n.xt = sb("xt", [128, 4, D], F32, stk)
                n.xtB = [Buf("xt%d" % i) for i in range(4)]
                n.xnT = sb("xnT", [128, 16, 512], BF16, stk)
                n.xnTB = Buf("xnT")
                n.actT = sb("actT", [128, NFC, 512], BF16, stk)
                n.actTB = Buf("actT")
                wt = [sb("wr%d" % i, [128, 16 * 512], BF16, stk) for i in range(2)]
                n.wring = Ring([(wt[i], Buf("wr%d" % i), tr.dsem()) for i in range(2)])
                n.wr2 = {id(wt[i]): (Buf("wu%d" % i), tr.dsem()) for i in range(2)}
                w2t = [sb("w2r%d" % i, [128, NFC, 256], BF16, stk) for i in range(2)]
                n.w2ring = Ring([(w2t[i], Buf("w2r%d" % i), tr.dsem()) for i in range(2)])
                xnb = [sb("xnb%d" % i, [128, D], BF16, stk) for i in range(2)]
                n.xnbR = Ring([(xnb[i], Buf("xnb%d" % i)) for i in range(2)])
                n.stat = sb("stat", [128, 16], F32, stk)
                n.statB = Buf("stat")
                tmpf = [sb("tmpf%d" % i, [128, 512], F32, stk) for i in range(4)]
                n.tmpR = Ring([(tmpf[i], Buf("tmpf%d" % i)) for i in range(4)])
                n.psA = Ring([0, 1, 2, 3, 4, 5])
                n.psT = Ring([6, 7])
                n.xds = [tr.dsem() for _ in range(4)]
                if phase == 1:
                    stg = [sb("stg%d" % i, [128, 4, 512], BF16, stk) for i in range(2)]
                    n.stgR = Ring([(stg[i], Buf("stg%d" % i), tr.dsem()) for i in range(2)])
                    n.ropet = sb("ropet", [128, 4, 128], F32, stk)
                    n.ropeB = Buf("rope")
                    n.rtmp = [sb("rtmp%d" % i, [128, 128], F32, stk) for i in range(5)]
                    n.rtmpB = [Buf("rtmp%d" % i) for i in range(5)]
                    n.qrot = sb("qrot", [128, 4, 128], BF16, stk)
                    n.qrotB = Buf("qrot")
                else:
                    n.gfb = sb("gfb", [128, D], F32, stk)
                    n.gfbB = Buf("gfb")
                    dma("sp", cds, n.gfb[:], gfb_d.ap(), writes=[n.gfbB])
                    sgt = [sb("sgt%d" % i, [128, 512], F32, stk) for i in range(2)]
                    n.sgR = Ring([(sgt[i], Buf("sgt%d" % i)) for i in range(2)])
                return n

            def load_rows(n, src, t):
                for s in range(4):
                    dma("sp", n.xds[s], n.xt[:, s, :], src.ap()[t * 512 + s * 128: t * 512 + (s + 1) * 128, :],
                        writes=[n.xtB[s]])

            def row_stats(n):
                junk = n.actT[:, 0:4, :]
                for s in range(4):
                    op("act", lambda s=s: act.activation(out=junk, in_=n.xt[:, s, :].rearrange("p (a b) -> p a b", a=4),
                                                         func=AF.Square, accum_out=n.stat[:, s:s + 1]),
                       reads=[n.xtB[s]], writes=[n.actTB, n.statB])
                op("act", lambda: act.activation(out=n.stat[:, 4:8], in_=n.stat[:, 0:4], func=AF.Sqrt,
                                                 bias=small[:, 4:5], scale=1.0 / D), reads=[n.statB, smB], writes=[n.statB])
                op("dve", lambda: dve.reciprocal(out=n.stat[:, 8:12], in_=n.stat[:, 4:8]), reads=[n.statB], writes=[n.statB])

            def norm_to_T(n, gcol):
                row_stats(n)
                for half in range(2):
                    cur = []
                    for s in (2 * half, 2 * half + 1):
                        xb, xbB = n.xnbR.next()
                        op("dve", lambda s=s, xb=xb: dve.tensor_scalar(out=xb[:], in0=n.xt[:, s, :],
                                                                      scalar1=n.stat[:, 8 + s:9 + s], scalar2=None,
                                                                      op0=ALU.mult),
                           reads=[n.xtB[s], n.statB], writes=[xbB])
                        cur.append((xb, xbB))
                    for kc in range(16):
                        b = n.psT.next()
                        for i, (xb, xbB) in enumerate(cur):
                            op("pe", lambda b=b, i=i, xb=xb, kc=kc: pe.transpose(
                                out=psb[b][:, i * 128:(i + 1) * 128], in_=xb[:, kc * 128:(kc + 1) * 128],
                                identity=identb[:]),
                               reads=[xbB, identbB], writes=[psB[b]], inc=(i == 1))
                        op("act", lambda b=b, kc=kc, half=half: act.activation(
                            out=n.xnT[:, kc, half * 256:(half + 1) * 256], in_=psb[b][:, 0:256], func=AF.Copy,
                            scale=gcol[:, kc:kc + 1]),
                           reads=[psB[b], cB], writes=[n.xnTB])

            def wblock(n, wsrc, wB, kch, c0, ncols=512):
                wt, wtB, wds = n.wring.next()
                wuB_ = n.wr2[id(wt)][0]
                view = wt[:, 0:kch * ncols].rearrange("p (k n) -> p k n", n=ncols)
                dma("sp", wds, view, wsrc.ap().rearrange("(k p) n -> p k n", p=128)[:, :, c0:c0 + ncols],
                    reads=[wB], writes=[wtB, wuB_])
                return view, wtB

            def ffn(n, w_in_s, w_in_B, w_out_s, w_out_B, mid_hook=None, bg=None):
                inB = w_in_B if callable(w_in_B) else (lambda c0, c1: [w_in_B])
                outB = w_out_B if callable(w_out_B) else (lambda c0, c1: [w_out_B])
                ngrp = (NFC + 1) // 2
                src = w_in_s.ap().rearrange("(k p) n -> p k n", p=128)
                for gi in range(ngrp):
                    nch = min(2, NFC - 2 * gi)
                    wt, wtB, wds = n.wring.next()
                    view = wt[:, :].rearrange("p (a k n) -> p a k n", a=2, k=16)
                    wuB, wuds = n.wr2[id(wt)]
                    ex_u = tr.deps([], [wtB])
                    ex_g = tr.deps([], [wuB])
                    dma("sp", wds, view[:, 0, :, 0:nch * 128], src[:, :, gi * 256: gi * 256 + nch * 128],
                        reads=inB(gi * 256, gi * 256 + nch * 128), writes=[wtB], extra=ex_g)
                    dma("sp", wuds, view[:, 1, :, 0:nch * 128], src[:, :, DFF + gi * 256: DFF + gi * 256 + nch * 128],
                        reads=inB(DFF + gi * 256, DFF + gi * 256 + nch * 128), writes=[wuB], extra=ex_u)
                    if bg is not None:
                        bg()
                    for c in range(nch):
                        fc = 2 * gi + c
                        bg_, bu = n.psA.next(), n.psA.next()
                        for a, b in ((0, bg_), (1, bu)):
                            for kc in range(16):
                                op("pe", lambda a=a, b=b, kc=kc, c=c, view=view: pe.matmul(
                                    ps[b][:, :], lhsT=view[:, a, kc, c * 128:(c + 1) * 128], rhs=n.xnT[:, kc, :],
                                    start=(kc == 0), stop=(kc == 15)),
                                   reads=[wtB, wuB, n.xnTB], writes=[psB[b]], inc=(kc == 15))
                        tf, tfB = n.tmpR.next()
                        op("act", lambda tf=tf: act.activation(out=tf[:], in_=ps[bg_][:, :], func=AF.Silu),
                           reads=[psB[bg_]], writes=[tfB])
                        op("dve", lambda bu=bu, tf=tf, fc=fc: dve.tensor_tensor(out=n.actT[:, fc, :], in0=tf[:],
                                                                                in1=ps[bu][:, :], op=ALU.mult),
                           reads=[tfB, psB[bu]], writes=[n.actTB])
                if mid_hook is not None:
                    mid_hook()
                srco = w_out_s.ap().rearrange("(k p) n -> p k n", p=128)
                for db in range(8):
                    if bg is not None:
                        bg()
                    w2t, w2B, w2ds = n.w2ring.next()
                    dma("sp", w2ds, w2t[:], srco[:, :, db * 256:(db + 1) * 256], reads=outB(db * 256, (db + 1) * 256),
                        writes=[w2B])
                    for s in range(4):
                        b = n.psA.next()
                        for fc in range(NFC):
                            op("pe", lambda b=b, fc=fc, s=s, w2t=w2t: pe.matmul(
                                ps[b][:, 0:256], lhsT=n.actT[:, fc, s * 128:(s + 1) * 128], rhs=w2t[:, fc, :],
                                start=(fc == 0), stop=(fc == NFC - 1)),
                               reads=[n.actTB, w2B], writes=[psB[b]], inc=(fc == NFC - 1))
                        op("dve", lambda b=b, s=s, db=db: dve.scalar_tensor_tensor(
                            out=n.xt[:, s, db * 256:(db + 1) * 256], in0=ps[b][:, 0:256], scalar=0.5,
                            in1=n.xt[:, s, db * 256:(db + 1) * 256], op0=ALU.mult, op1=ALU.add),
                           reads=[psB[b]], writes=[n.xtB[s]])

            with ExitStack() as p1:
                n = mk13(p1, 1)
                hds = [tr.dsem() for _ in range(4)]
                rds = tr.dsem()
                uds = tr.dsem()
                noB = Buf("dram")

                def tok_major_mm(b, s, wv, wvB):
                    for kc in range(16):
                        op("pe", lambda kc=kc: pe.matmul(ps[b][:, :], lhsT=n.xnT[:, kc, s * 128:(s + 1) * 128],
                                                         rhs=wv[:, kc, :], start=(kc == 0), stop=(kc == 15)),
                           reads=[n.xnTB, wvB], writes=[psB[b]], inc=(kc == 15))

                qrots = [sb("qrot%d" % i, [128, 4, 128], BF16, p1) for i in range(3)]
                qrotR = Ring([(qrots[i], Buf("qrot%d" % i)) for i in range(3)])

                def qk_chain(b, s, nh, gb_t):
                    tf, tfB = n.tmpR.next()
                    op("act", lambda: act.activation(out=tf[:, :], in_=ps[b][:, :], func=AF.Square),
                       reads=[psB[b]], writes=[tfB])
                    op("dve", lambda: dve.tensor_reduce(out=n.stat[:, 12:12 + nh],
                                                        in_=tf[:, 0:nh * 128].rearrange("p (h d) -> p h d", d=128),
                                                        axis=AX.X, op=ALU.add), reads=[tfB], writes=[n.statB])
                    op("act", lambda: act.activation(out=n.stat[:, 12:12 + nh], in_=n.stat[:, 12:12 + nh], func=AF.Sqrt,
                                                     bias=small[:, 4:5], scale=1.0 / 128), reads=[n.statB, smB],
                       writes=[n.statB])
                    op("dve", lambda: dve.reciprocal(out=n.stat[:, 12:12 + nh], in_=n.stat[:, 12:12 + nh]),
                       reads=[n.statB], writes=[n.statB])
                    cs = n.ropet[:, s, 0:64]
                    sn = n.ropet[:, s, 64:128]
                    rt, rtB = n.rtmp, n.rtmpB
                    qr, qrB = qrotR.next()
                    for i in range(nh):
                        op("dve", lambda i=i: dve.scalar_tensor_tensor(
                            out=rt[0][:], in0=ps[b][:, i * 128:(i + 1) * 128], scalar=n.stat[:, 12 + i:13 + i],
                            in1=gb_t[:], op0=ALU.mult, op1=ALU.mult),
                           reads=[psB[b], n.statB, cB], writes=[rtB[0]])
                        x0 = rt[0][:, 0:128:2]
                        x1 = rt[0][:, 1:128:2]
                        op("dve", lambda: dve.tensor_tensor(out=rt[1][:, 0:64], in0=x0, in1=cs, op=ALU.mult),
                           reads=[rtB[0], n.ropeB], writes=[rtB[1]])
                        op("dve", lambda: dve.tensor_tensor(out=rt[2][:, 0:64], in0=x1, in1=sn, op=ALU.mult),
                           reads=[rtB[0], n.ropeB], writes=[rtB[2]])
                        op("dve", lambda i=i: dve.tensor_tensor(out=qr[:, i, 0:128:2], in0=rt[1][:, 0:64],
                                                                in1=rt[2][:, 0:64], op=ALU.subtract),
                           reads=[rtB[1], rtB[2]], writes=[qrB])
                        op("pool", lambda: pool.tensor_tensor(out=rt[3][:, 0:64], in0=x0, in1=sn, op=ALU.mult),
                           reads=[rtB[0], n.ropeB], writes=[rtB[3]])
                        op("pool", lambda: pool.tensor_tensor(out=rt[4][:, 0:64], in0=x1, in1=cs, op=ALU.mult),
                           reads=[rtB[0], n.ropeB], writes=[rtB[4]])
                        op("pool", lambda i=i: pool.tensor_tensor(out=qr[:, i, 1:128:2], in0=rt[3][:, 0:64],
                                                                  in1=rt[4][:, 0:64], op=ALU.add),
                           reads=[rtB[3], rtB[4]], writes=[qrB])
                    return qr, qrB

                def wslot(slot, c0):
                    wt, wtB, wds = n.wring.items[slot]
                    wuB_ = n.wr2[id(wt)][0]
                    view = wt[:, 0:16 * 512].rearrange("p (k n) -> p k n", n=512)
                    dma("sp", wds, view, win_s.ap().rearrange("(k p) n -> p k n", p=128)[:, :, c0:c0 + 512],
                        reads=[winB], writes=[wtB, wuB_])
                    return view, wtB

                RBLK = [O_QA, O_QA + 512, O_KA]
                FBLK = [(O_QB, qbT_s, 0, "f"), (O_QB + 512, qbT_s, 1, "f"), (O_KB, kbT_s, 0, "f"),
                        (O_KB + 512, kbT_s, 1, "f"), (O_VB, vb_s, 0, "t"), (O_VB + 512, vb_s, 1, "t")]
                stFv = [n.actT[:, 4:8, :], n.actT[:, 8:12, :]]
                stFds = [tr.dsem(), tr.dsem()]
                stVv = n.actT[:, 12:16, :]
                stVds = tr.dsem()

                jobs = []
                for nm, (sd, ss, rr, cc) in (("wa", (wa_d, wa_s, 1024, D)), ("wb", (wb_d, wb_s, 1024, D)),
                                             ("wo", (wo_d, wo_s, D, D)), ("w2a", (w2a_d, w2a_s, D, 2 * DFF)),
                                             ("w2b", (w2b_d, w2b_s, DFF, D))):
                    WB[nm] = Buf("w")
                    jobs += conv_jobs(sd, ss, rr, cc, WB[nm])
                per_call = (len(jobs) + 30 * max(NT - 1, 1) - 1) // (30 * max(NT - 1, 1))

                def bgjob():
                    for _ in range(per_call):
                        if jobs:
                            jobs.pop(0)()

                for t in range(NT):
                    load_rows(n, x_d, t)
                    dma("sp", rds, n.ropet[:], rope_d.ap()[t * 512:(t + 1) * 512, :].rearrange("(s p) c -> p s c", p=128),
                        writes=[n.ropeB])
                    norm_to_T(n, g1c)
                    ckpt(2)
                    ffn(n, w1a_s, w1a_cols, w1b_s, w1b_cols, bg=(bgjob if t >= 1 else None))
                    ckpt(3)
                    for s_ in range(4):
                        dma("pool", hds[s_], h_s.ap()[t * 512 + s_ * 128: t * 512 + (s_ + 1) * 128, :], n.xt[:, s_, :],
                            reads=[n.xtB[s_]], writes=[noB], ndesc=8)
                    norm_to_T(n, gmc)
                    dma("pool", uds, uT_s.ap()[t].rearrange("p (k n) -> p k n", n=512), n.xnT[:],
                        reads=[n.xnTB], writes=[noB], ndesc=128)
                    tsl = slice(t * 512, (t + 1) * 512)
                    stFB = [Buf("stF0"), Buf("stF1")]
                    stVB = Buf("stV")
                    for b_ in stFB + [stVB]:
                        b_.inherit([n.actTB])
                    rstate = {}
                    Rw = {0: wslot(0, RBLK[0])}
                    Fw = {0: wslot(1, FBLK[0][0])}
                    Rst = {}

                    def R_mm(k):
                        blk, s = k // 4, k % 4
                        wv, wvB = Rw[blk]
                        if s == 0:
                            Rst[blk] = [n.stgR.next()] + ([(stVv, stVB, stVds)] if blk == 2 else [])
                        b = n.psA.next()
                        tok_major_mm(b, s, wv, wvB)
                        if s == 3 and blk + 1 < 3:
                            Rw[blk + 1] = wslot(0, RBLK[blk + 1])
                        if blk == 2:
                            st2, st2B, _ = Rst[blk][1]
                            op("dve", lambda: dve.tensor_copy(out=st2[:, s, 0:256], in_=ps[b][:, 256:512]),
                               reads=[psB[b]], writes=[st2B])
                        nh = 2 if blk == 2 else 4
                        rstate[k] = (qk_chain(b, s, nh, gkb if blk == 2 else gqb), nh)

                    def R_tail(k):
                        blk, s = k // 4, k % 4
                        (qr, qrB), nh = rstate.pop(k)
                        st, stB, sds = Rst[blk][0]
                        bt = n.psT.next()
                        for i in range(nh):
                            op("pe", lambda i=i: pe.transpose(out=psb[bt][:, i * 128:(i + 1) * 128], in_=qr[:, i, :],
                                                              identity=identb[:]),
                               reads=[qrB, identbB], writes=[psB[bt]], inc=(i == nh - 1))
                        op("act", lambda: act.activation(
                            out=st[:, 0:nh, s * 128:(s + 1) * 128],
                            in_=psb[bt][:, 0:nh * 128].rearrange("p (h t) -> p h t", h=nh), func=AF.Copy),
                           reads=[psB[bt]], writes=[stB])
                        if s == 3:
                            if blk < 2:
                                dma("pool", sds, qaT_s.ap()[blk * 4:(blk + 1) * 4, :, tsl].rearrange("h d t -> d h t"),
                                    st[:], reads=[stB], writes=[noB])
                            else:
                                st2, st2B, sds2 = Rst[blk][1]
                                dma("pool", sds, kaT_s.ap()[:, :, tsl].rearrange("h d t -> d h t"), st[:, 0:2, :],
                                    reads=[stB], writes=[noB])
                                dma("pool", sds2, va_s.ap()[tsl, :].rearrange("(s p) c -> p s c", p=128),
                                    st2[:, :, 0:256], reads=[st2B], writes=[noB])

                    def F_unit(m):
                        blk, u = m // 4, m % 4
                        c0, dst, half, kind = FBLK[blk]
                        wv, wvB = Fw[blk]
                        stv, stB_, sds_ = stFv[blk % 2], stFB[blk % 2], stFds[blk % 2]
                        b = n.psA.next()
                        if kind == "f":
                            for kc in range(16):
                                op("pe", lambda kc=kc: pe.matmul(
                                    ps[b][:, :], lhsT=wv[:, kc, u * 128:(u + 1) * 128], rhs=n.xnT[:, kc, :],
                                    start=(kc == 0), stop=(kc == 15)),
                                   reads=[n.xnTB, wvB], writes=[psB[b]], inc=(kc == 15))
                        else:
                            tok_major_mm(b, u, wv, wvB)
                        if u == 3 and blk + 1 < 6:
                            Fw[blk + 1] = wslot(1, FBLK[blk + 1][0])
                        if u % 2 == 0:
                            op("act", lambda: act.activation(out=stv[:, u, :], in_=ps[b][:, :], func=AF.Copy),
                               reads=[psB[b]], writes=[stB_])
                        else:
                            op("dve", lambda: dve.tensor_copy(out=stv[:, u, :], in_=ps[b][:, :]),
                               reads=[psB[b]], writes=[stB_])
                        if u == 3:
                            if kind == "f":
                                dma("pool", sds_, dst.ap()[half * 4:(half + 1) * 4, :, tsl].rearrange("h d t -> d h t"),
                                    stv, reads=[stB_], writes=[noB])
                            else:
                                dma("pool", sds_,
                                    dst.ap()[tsl, half * 512:(half + 1) * 512].rearrange("(s p) c -> p s c", p=128),
                                    stv, reads=[stB_], writes=[noB])

                    for k in range(12):
                        R_mm(k)
                        F_unit(2 * k)
                        F_unit(2 * k + 1)
                        if k >= 1:
                            R_tail(k - 1)
                    R_tail(11)
                    n.actTB.inherit(stFB + [stVB])
                    ckpt(372)
                    if t == NT - 1:
                        while jobs:
                            jobs.pop(0)()
                tr.barrier()
                ckpt(4)

            with ExitStack() as p2:
                ads = [tr.dsem() for _ in range(8)]
                zt = sb("zt", [120, 160], F32, p2)
                ztB = Buf("zt")
                padB = Buf("pad")
                op("dve", lambda: dve.memset(zt[:], 0.0), writes=[ztB])
                dma("pool", ads[0], pad_s.ap(), zt[:], reads=[ztB], writes=[padB])
                dma("pool", ads[0], pad_s.ap()[:, 64:95], rpb_d.ap(), reads=[], writes=[padB])
                BS = sb("BS", [128, 8, 14 * 64], F32, p2)
                BSB = Buf("BS")
                bslanes = [tr.dsem() for _ in range(8)]
                bstoks = {}
                for ql in range(2):
                    for qc in range(64):
                        p = ql * 64 + qc
                        src = bass.AP(pad_s, (1 - ql) * 160 + 79 - qc, [[1, 1], [2400, 8], [160, 14], [1, 64]])
                        tr.wait("pool", [padB.w])
                        bstoks[p % 8] = dma("pool", bslanes[p % 8],
                                            BS[p:p + 1, :, :].rearrange("p h (r c) -> p h r c", c=64), src, ndesc=16)
                BSB.w = list(bstoks.values())
                maskB = sb("maskB", [128, 9, 896], F32, p2)
                maskBB = Buf("maskB")
                ckpt(45)

                KT = sb("KT", [128, T], BF16, p2)
                KTB = Buf("KT")
                Vg = sb("Vg", [128, NCH, 129], BF16, p2)
                VgB = Buf("Vg")
                op("dve", lambda: dve.memset(Vg[:, :, 128:129], 1.0), writes=[VgB])
                QTt = [sb("QT%d" % i, [128, 4, 512], BF16, p2) for i in range(2)]
                QTR = Ring([(QTt[i], Buf("QT%d" % i), ads[3 + i]) for i in range(2)])
                PTt = [sb("PT%d" % i, [128, 512], BF16, p2) for i in range(3)]
                PTR = Ring([(PTt[i], Buf("PT%d" % i)) for i in range(3)])
                ynt = [sb("yn%d" % i, [128, 128], BF16, p2) for i in range(2)]
                ynR = Ring([(ynt[i], Buf("yn%d" % i)) for i in range(2)])
                ystt = [sb("yst%d" % i, [128, 512], BF16, p2) for i in range(2)]
                ystR = Ring([(ystt[i], Buf("yst%d" % i), ads[5 + i]) for i in range(2)])
                rinv = sb("rinv", [128, 4], F32, p2)
                rinvB = Buf("rinv")
                SR = Ring([4, 5, 6])
                noB = Buf("dram2")
                for g in range(2):
                    dma("sp", ads[7], KT[:], kaT_s.ap()[g, :, :], writes=[KTB])
                    dma("sp", ads[1], Vg[:, :, 0:128],
                        va_s.ap()[:, g * 128:(g + 1) * 128].rearrange("(c p) d -> p c d", p=128), writes=[VgB])
                    items = [(qt, hl, kc) for qt in range(NT) for hl in range(4) for kc in range(NCH)]
                    qts = {}

                    def getQ(qt):
                        if qt not in qts:
                            q_, qB_, qds_ = QTR.next()
                            dma("sp", qds_, q_[:],
                                qaT_s.ap()[g * 4:(g + 1) * 4, :, qt * 512:(qt + 1) * 512].rearrange("h d t -> d h t"),
                                writes=[qB_])
                            qts[qt] = (q_, qB_)
                        return qts[qt]

                    sbank = {}

                    def emitS(idx):
                        qt, hl, kc = items[idx]
                        q_, qB_ = getQ(qt)
                        b = SR.next()
                        sbank[idx] = b
                        op("pe", lambda: pe.matmul(ps[b][:, :], lhsT=KT[:, kc * 128:(kc + 1) * 128], rhs=q_[:, hl, :],
                                                   start=True, stop=True), reads=[KTB, qB_], writes=[psB[b]])

                    emitS(0)
                    emitS(1)
                    if g == 0:
                        dma("sp", ads[2], maskB[:], maskB_d.ap().rearrange("p (t c) -> p t c", c=896), writes=[maskBB])
                    for idx, (qt, hl, kc) in enumerate(items):
                        if idx + 2 < len(items):
                            emitS(idx + 2)
                        b = sbank.pop(idx)
                        mi = (2 if kc >= NCH // 2 else 0) + (1 if qt >= NT // 2 else 0)
                        pt, ptB = PTR.next()
                        op("act", lambda: act.activation(out=pt[:], in_=ps[b][:, :], func=AF.Exp,
                                                         bias=biasA[:, mi:mi + 1], scale=SCALE),
                           reads=[psB[b], biasAB], writes=[ptB])
                        for qs in range(4):
                            op("pe", lambda qs=qs: pe.matmul(ps[qs][:, 0:129], lhsT=pt[:, qs * 128:(qs + 1) * 128],
                                                             rhs=Vg[:, kc, :], start=(kc == 0), stop=(kc == NCH - 1)),
                               reads=[ptB, VgB], writes=[psB[qs]], inc=(qs == 3))
                        if kc == NCH - 1:
                            h = g * 4 + hl
                            yst, ystB, yds = ystR.next()
                            for qs in range(4):
                                op("dve", lambda qs=qs: dve.reciprocal(out=rinv[:, qs:qs + 1], in_=ps[qs][:, 128:129]),
                                   reads=[psB[qs]], writes=[rinvB])
                                yn, ynB = ynR.next()
                                op("dve", lambda qs=qs, yn=yn: dve.tensor_scalar(out=yn[:], in0=ps[qs][:, 0:128],
                                                                                 scalar1=rinv[:, qs:qs + 1], scalar2=None,
                                                                                 op0=ALU.mult),
                                   reads=[psB[qs], rinvB], writes=[ynB])
                                op("pe", lambda qs=qs, yn=yn: pe.transpose(out=psb[7][:, qs * 128:(qs + 1) * 128],
                                                                           in_=yn[:], identity=identb[:]),
                                   reads=[ynB, identbB], writes=[psB[7]])
                            op("act", lambda yst=yst: act.activation(out=yst[:], in_=psb[7][:, 0:512], func=AF.Copy),
                               reads=[psB[7]], writes=[ystB])
                            dma("pool", yds, yaT_s.ap()[h, :, qt * 512:(qt + 1) * 512], yst[:], reads=[ystB], writes=[noB])

                ckpt(5)
                master = sb("master", [128, 8, 896], F32, p2)
                masterB = Buf("master")
                for h in range(8):
                    for (bk, d0, nd) in ((4, 0, 4), (5, 4, 3)):
                        for i in range(nd):
                            di = d0 + i
                            op("pe", lambda i=i, di=di, bk=bk: pe.transpose(
                                out=ps[bk][:, i * 128:(i + 1) * 128], in_=BS[:, h, di * 128:(di + 1) * 128],
                                identity=ident[:]),
                               reads=[BSB, cB], writes=[psB[bk]], inc=(i == nd - 1))
                        op("dve", lambda bk=bk, d0=d0, nd=nd: dve.tensor_copy(
                            out=master[:, h, d0 * 128:(d0 + nd) * 128], in_=ps[bk][:, 0:nd * 128]),
                           reads=[psB[bk]], writes=[masterB])
                combt = [sb("comb%d" % i, [128, 896], F32, p2) for i in range(2)]
                combBs = [Buf("comb%d" % i) for i in range(2)]

                def build_comb(h):
                    op("pool", lambda: pool.tensor_tensor(out=combt[h % 2][:], in0=master[:, h, :], in1=maskB[:, 2, :],
                                                          op=ALU.add),
                       reads=[masterB, maskBB], writes=[combBs[h % 2]])
                KTb = sb("KTb", [128, T], BF16, p2)
                QTb = sb("QTb", [128, T], BF16, p2)
                Vb = sb("Vb", [128, NCH, 129], BF16, p2)
                QTb2 = sb("QTb2", [128, T], BF16, p2)
                vinitB = Buf("vinit")
                op("dve", lambda: dve.memset(Vb[:, :, 128:129], 1.0), writes=[vinitB])
                sets = [(KTb, QTb, Vb, [Buf("k0"), Buf("q0"), Buf("v0")], [tr.dsem(), tr.dsem(), tr.dsem()]),
                        (KT, QTb2, Vg, [Buf("k1"), Buf("q1"), Buf("v1")], [tr.dsem(), tr.dsem(), tr.dsem()])]
                sets[0][3][2].w = vinitB.w
                sets[1][3][0].inherit([KTB])
                sets[1][3][2].inherit([VgB])

                def loadset(h):
                    k_, q_, v_, bs_, ds_ = sets[h % 2]
                    dma("sp", ds_[0], k_[:], kbT_s.ap()[h, :, :], writes=[bs_[0]])
                    dma("sp", ds_[1], q_[:], qbT_s.ap()[h, :, :], writes=[bs_[1]])
                    dma("sp", ds_[2], v_[:, :, 0:128],
                        vb_s.ap()[:, h * 128:(h + 1) * 128].rearrange("(c p) d -> p c d", p=128), writes=[bs_[2]])

                sTt = [sb("sT%d" % i, [128, 896], F32, p2) for i in range(2)]
                sTR = Ring([(sTt[i], Buf("sT%d" % i)) for i in range(2)])
                PBt = [sb("PB%d" % i, [128, 896], BF16, p2) for i in range(3)]
                PBR = Ring([(PBt[i], Buf("PB%d" % i)) for i in range(3)])
                ynb = [sb("ynb%d" % i, [128, 128], BF16, p2) for i in range(3)]
                ynbR = Ring([(ynb[i], Buf("ynb%d" % i)) for i in range(3)])
                rinvb = sb("rinvb", [128, 4], F32, p2)
                rinvbB = Buf("rinvb")
                SBR = Ring([(0, 1), (2, 3)])
                ACR = Ring([4, 5])
                TBR = Ring([6, 7])

                def ti_of(j):
                    if j == 0:
                        return 0
                    if j == 1:
                        return 1
                    if NBLK // 2 - 2 <= j <= NBLK // 2 + 1:
                        return 3 + j - (NBLK // 2 - 2)
                    if j == NBLK - 2:
                        return 7
                    if j == NBLK - 1:
                        return 8
                    return 2

                loadset(0)
                build_comb(0)
                for h in range(8):
                    if h + 1 < 8:
                        loadset(h + 1)
                        build_comb(h + 1)
                    comb, combB = combt[h % 2], combBs[h % 2]
                    KTh, QTh, Vh, (kB_, qB_, vB_), _ = sets[h % 2]
                    st_ = {}

                    def stA(j):
                        ba, bb = SBR.next()
                        st_[j] = {"s": (ba, bb)}
                        for di in range(7):
                            c = min(max(j - 3 + di, 0), NCH - 1)
                            bk, off = (ba, di * 128) if di < 4 else (bb, (di - 4) * 128)
                            op("pe", lambda c=c, bk=bk, off=off: pe.matmul(
                                ps[bk][:, off:off + 128], lhsT=KTh[:, c * 128:(c + 1) * 128],
                                rhs=QTh[:, j * 128:(j + 1) * 128], start=True, stop=True),
                               reads=[kB_, qB_], writes=[psB[bk]], inc=(di == 3 or di == 6))

                    def stB(j):
                        ba, bb = st_[j]["s"]
                        ti = ti_of(j)
                        sT, sTB = sTR.next()
                        if ti == 2:
                            bsrc, bsB = comb, combB
                        else:
                            bsrc, bsB = master[:, h, :], masterB
                        op("dve", lambda: dve.scalar_tensor_tensor(out=sT[:, 0:512], in0=ps[ba][:, :], scalar=SCALE,
                                                                   in1=bsrc[:, 0:512], op0=ALU.mult, op1=ALU.add),
                           reads=[psB[ba], bsB], writes=[sTB])
                        op("dve", lambda: dve.scalar_tensor_tensor(out=sT[:, 512:896], in0=ps[bb][:, 0:384],
                                                                   scalar=SCALE, in1=bsrc[:, 512:896],
                                                                   op0=ALU.mult, op1=ALU.add),
                           reads=[psB[bb], bsB], writes=[sTB])
                        if ti != 2:
                            op("pool", lambda: pool.tensor_tensor(out=sT[:], in0=sT[:], in1=maskB[:, ti, :], op=ALU.add),
                               reads=[maskBB], writes=[sTB])
                        pb, pbB = PBR.next()
                        op("act", lambda: act.activation(out=pb[:], in_=sT[:], func=AF.Exp), reads=[sTB], writes=[pbB])
                        st_[j]["p"] = (pb, pbB)

                    def stC(j):
                        pb, pbB = st_[j]["p"]
                        ac = ACR.next()
                        st_[j]["ac"] = ac
                        for di in range(7):
                            c = min(max(j - 3 + di, 0), NCH - 1)
                            op("pe", lambda di=di, c=c: pe.matmul(ps[ac][:, 0:129], lhsT=pb[:, di * 128:(di + 1) * 128],
                                                                  rhs=Vh[:, c, :], start=(di == 0), stop=(di == 6)),
                               reads=[pbB, vB_], writes=[psB[ac]], inc=(di == 6))

                    def stD(j):
                        ac = st_[j]["ac"]
                        op("dve", lambda: dve.reciprocal(out=rinvb[:, j % 4:j % 4 + 1], in_=ps[ac][:, 128:129]),
                           reads=[psB[ac]], writes=[rinvbB])
                        yn, ynB = ynbR.next()
                        op("dve", lambda: dve.tensor_scalar(out=yn[:], in0=ps[ac][:, 0:128],
                                                            scalar1=rinvb[:, j % 4:j % 4 + 1], scalar2=None, op0=ALU.mult),
                           reads=[psB[ac], rinvbB], writes=[ynB])
                        st_[j]["yn"] = (yn, ynB)

                    tbs = {}

                    def stE(j):
                        yn, ynB = st_[j]["yn"]
                        if j % 4 == 0:
                            tbs[j // 4] = TBR.next()
                        tb = tbs[j // 4]
                        op("pe", lambda: pe.transpose(out=psb[tb][:, (j % 4) * 128:(j % 4 + 1) * 128], in_=yn[:],
                                                      identity=identb[:]), reads=[ynB, identbB], writes=[psB[tb]])
                        if j % 4 == 3:
                            yst, ystB, yds = ystR.next()
                            op("act", lambda: act.activation(out=yst[:], in_=psb[tb][:, 0:512], func=AF.Copy),
                               reads=[psB[tb]], writes=[ystB])
                            dma("pool", yds, ybT_s.ap()[h, :, (j - 3) * 128:(j + 1) * 128], yst[:], reads=[ystB],
                                writes=[noB], ndesc=8)
                        del st_[j]

                    for i in range(-2, NBLK + 2):
                        if 0 <= i + 2 < NBLK:
                            stA(i + 2)
                        if 0 <= i + 1 < NBLK:
                            stB(i + 1)
                        if 0 <= i < NBLK:
                            stC(i)
                        if 0 <= i - 1 < NBLK:
                            stD(i - 1)
                        if 0 <= i - 2 < NBLK:
                            stE(i - 2)
                tr.barrier()
                ckpt(6)

            with ExitStack() as p3:
                n = mk13(p3, 3)
                mT = n.actT[:, 0:16, :]
                yaT = n.actT[:, 16:24, :]
                ybT = n.actT[:, 24:32, :]
                yds = tr.dsem()
                ods = [tr.dsem() for _ in range(4)]
                noB = Buf("dram3")
                out_toks = []
                uds3 = tr.dsem()

                def prefetch_u(t):
                    dma("sp", uds3, n.xnT[:], uT_s.ap()[t].rearrange("p (k n) -> p k n", n=512), writes=[n.xnTB])

                def gslot(slot, c0):
                    wt, wtB, wds = n.wring.items[slot]
                    wuB_ = n.wr2[id(wt)][0]
                    view = wt[:, 0:16 * 512].rearrange("p (k n) -> p k n", n=512)
                    dma("sp", wds, view, win_s.ap().rearrange("(k p) n -> p k n", p=128)[:, :, c0:c0 + 512],
                        reads=[winB], writes=[wtB, wuB_])
                    return view, wtB

                def abslot(fb):
                    w2t, w2B, w2ds = n.w2ring.items[fb % 2]
                    view = w2t[:, :, :].rearrange("p a b -> p (a b)")[:, 0:8192].rearrange("p (r k n) -> p r k n", r=2, k=8)
                    for r_, (wsrc, wk) in enumerate(((wa_s, "wa"), (wb_s, "wb"))):
                        dma("sp", w2ds, view[:, r_, :, :],
                            wsrc.ap().rearrange("(k p) n -> p k n", p=128)[:, :, fb * 512:(fb + 1) * 512],
                            reads=[WB[wk]], writes=[w2B])
                    return view, w2B

                gpre = {}

                def issue_first(t):
                    gpre[t] = {"ab": {0: abslot(0)}, "ga": {0: gslot(0, O_GA)}, "gb": {0: gslot(1, O_GB)}}

                prefetch_u(0)
                issue_first(0)
                for t in range(NT):
                    tsl = slice(t * 512, (t + 1) * 512)
                    dma("sp", yds, yaT, yaT_s.ap()[:, :, tsl].rearrange("h d t -> d h t"), writes=[n.actTB])
                    dma("sp", yds, ybT, ybT_s.ap()[:, :, tsl].rearrange("h d t -> d h t"), writes=[n.actTB])
                    yB = Buf("y")
                    yB.w = n.actTB.w
                    mB = Buf("m")
                    mB.inherit([n.actTB])
                    G = gpre.pop(t)
                    for fb in range(4):
                        if fb + 1 < 4:
                            G["ab"][fb + 1] = abslot(fb + 1)
                        abv, abB = G["ab"][fb]
                        tA = []
                        for br in range(2):
                            yv = (yaT, ybT)[br]
                            wg, wgB = G[("ga", "gb")[br]][fb]
                            for fc in range(4):
                                ch = fb * 4 + fc
                                bz, bgt = n.psA.next(), n.psA.next()
                                for kc in range(16):
                                    op("pe", lambda kc=kc, fc=fc: pe.matmul(
                                        ps[bgt][:, :], lhsT=wg[:, kc, fc * 128:(fc + 1) * 128], rhs=n.xnT[:, kc, :],
                                        start=(kc == 0), stop=(kc == 15)),
                                       reads=[wgB, n.xnTB], writes=[psB[bgt]], inc=(kc == 15))
                                for h in range(8):
                                    op("pe", lambda h=h, fc=fc: pe.matmul(
                                        ps[bz][:, :], lhsT=abv[:, br, h, fc * 128:(fc + 1) * 128], rhs=yv[:, h, :],
                                        start=(h == 0), stop=(h == 7)),
                                       reads=[abB, yB], writes=[psB[bz]], inc=(h == 7))
                                if br == 0:
                                    tf, tfB = n.tmpR.next()
                                    tA.append((tf, tfB))
                                else:
                                    tf, tfB = n.sgR.next()
                                op("act", lambda: act.activation(out=tf[:], in_=ps[bgt][:, :], func=AF.Sigmoid,
                                                                 bias=bgc[:, br * 16 + ch: br * 16 + ch + 1]),
                                   reads=[psB[bgt], cB], writes=[tfB])
                                op("dve", lambda: dve.tensor_tensor(out=tf[:], in0=tf[:], in1=ps[bz][:, :], op=ALU.mult),
                                   reads=[tfB, psB[bz]], writes=[tfB])
                                if br == 1:
                                    ta, taB = tA[fc]
                                    op("pool", lambda: pool.tensor_tensor(out=mT[:, ch, :], in0=ta[:], in1=tf[:],
                                                                          op=ALU.add),
                                       reads=[taB, tfB], writes=[mB])
                            if fb + 1 < 4:
                                key = ("ga", "gb")[br]
                                G[key][fb + 1] = gslot(br, (O_GA, O_GB)[br] + (fb + 1) * 512)
                    load_rows(n, h_s, t)
                    for db in range(4):
                        wv, wvB = wblock(n, wo_s, WB["wo"], 16, db * 512)
                        for s in range(4):
                            b = n.psA.next()
                            for kc in range(16):
                                op("pe", lambda kc=kc, s=s: pe.matmul(ps[b][:, :], lhsT=mT[:, kc, s * 128:(s + 1) * 128],
                                                                      rhs=wv[:, kc, :], start=(kc == 0), stop=(kc == 15)),
                                   reads=[mB, wvB], writes=[psB[b]], inc=(kc == 15))
                            op("dve", lambda b=b, s=s, db=db: dve.tensor_tensor(
                                out=n.xt[:, s, db * 512:(db + 1) * 512], in0=ps[b][:, :],
                                in1=n.xt[:, s, db * 512:(db + 1) * 512], op=ALU.add),
                               reads=[psB[b]], writes=[n.xtB[s]])
                    n.actTB.inherit([yB, mB])
                    norm_to_T(n, g2c)
                    ffn(n, w2a_s, WB["w2a"], w2b_s, WB["w2b"],
                        mid_hook=((lambda t=t: prefetch_u(t + 1)) if t + 1 < NT else None))
                    if t + 1 < NT:
                        issue_first(t + 1)
                    row_stats(n)
                    for s in range(4):
                        eng_, e_ = ("dve", dve)
                        op(eng_, lambda s=s, e_=e_: e_.scalar_tensor_tensor(out=n.xt[:, s, :], in0=n.xt[:, s, :],
                                                                            scalar=n.stat[:, 8 + s:9 + s], in1=n.gfb[:],
                                                                            op0=ALU.mult, op1=ALU.mult),
                           reads=[n.statB, n.gfbB], writes=[n.xtB[s]])
                        out_toks.append(dma("pool", ods[s], out_d.ap()[t * 512 + s * 128: t * 512 + (s + 1) * 128, :],
                                            n.xt[:, s, :], reads=[n.xtB[s]], writes=[noB], ndesc=8))
                tr.barrier()
    except _Stop:
        pass
    return nc


def _geometry(T, mode):
    tok = np.arange(T)
    if mode == "prompt":
        return np.zeros(T, np.int64), tok, T // GRID_W
    half = T // 2
    return tok // half, tok % half, half // GRID_W


def _rope_table(T, mode):
    _, pos, _ = _geometry(T, mode)
    row = (pos // GRID_W).astype(np.float32)
    col = (pos % GRID_W).astype(np.float32)
    npairs = HD // 4
    inv = (10000.0 ** (-np.arange(npairs, dtype=np.float32) / npairs)).astype(np.float32)
    ang = np.concatenate([row[:, None] * inv[None], col[:, None] * inv[None]], axis=-1)
    return np.concatenate([np.cos(ang), np.sin(ang)], axis=-1).astype(np.float32)


def _ti_blocks(nblk):
    reps = [0, 1, None] + [nblk // 2 - 2 + i for i in range(4)] + [nblk - 2, nblk - 1]
    special = set(r for r in reps if r is not None)
    interior = [j for j in range(nblk) if j not in special]
    reps[2] = interior[0] if interior else None
    return reps


def _mask_b(T, mode):
    seq, pos, rows = _geometry(T, mode)
    nblk = T // 128
    r_of = pos // GRID_W
    c_of = pos % GRID_W
    rs = np.clip(r_of - 4, 0, rows - 8)
    cs = np.clip(c_of - 8, 0, GRID_W - 16)
    out = np.full((128, 9, 7, 128), NEG, np.float32)
    for ti, j in enumerate(_ti_blocks(nblk)):
        if j is None:
            continue
        q = j * 128 + np.arange(128)
        for di in range(7):
            c = j - 3 + di
            if c < 0 or c >= nblk:
                continue
            k = c * 128 + np.arange(128)
            ok = (seq[k][:, None] == seq[q][None, :])
            ok &= (r_of[k][:, None] >= rs[q][None, :]) & (r_of[k][:, None] < rs[q][None, :] + 8)
            ok &= (c_of[k][:, None] >= cs[q][None, :]) & (c_of[k][:, None] < cs[q][None, :] + 16)
            out[:, ti, di, :] = np.where(ok, 0.0, NEG)
    return out.reshape(128, 9 * 896)


def _mask_a(mode):
    m = np.zeros((128, 4), np.float32)
    if mode != "prompt":
        m[:, 1] = NEG
        m[:, 2] = NEG
    return m


def _shared_inputs(g_ffn1, w_ffn1_in, w_ffn1_out, g_mix, w_in, b_gate, g_q_a, g_k_a, rpb_b, w_branch_a,
                   w_branch_b, w_out, g_ffn2, w_ffn2_in, w_ffn2_out, g_final):
    f = lambda a: np.ascontiguousarray(np.asarray(a, dtype=np.float32))
    col = lambda g: f(np.asarray(g).reshape(-1, 128).T)
    return {
        "ident": np.eye(128, dtype=np.float32),
        "g1c": col(g_ffn1), "gmc": col(g_mix), "g2c": col(g_ffn2),
        "gfb": f(np.broadcast_to(np.asarray(g_final).reshape(1, D), (128, D))),
        "bgc": col(b_gate),
        "gqb": f(np.broadcast_to(np.asarray(g_q_a).reshape(1, 128), (128, 128))),
        "gkb": f(np.broadcast_to(np.asarray(g_k_a).reshape(1, 128), (128, 128))),
        "rpb": f(np.asarray(rpb_b).reshape(120, 31)),
        "w1a": f(np.asarray(w_ffn1_in)[0]), "w1b": f(np.asarray(w_ffn1_out)[0]),
        "win": f(np.asarray(w_in)[0]), "wa": f(np.asarray(w_branch_a)[0]), "wb": f(np.asarray(w_branch_b)[0]),
        "wo": f(np.asarray(w_out)[0]), "w2a": f(np.asarray(w_ffn2_in)[0]), "w2b": f(np.asarray(w_ffn2_out)[0]),
    }


def _slot_inputs(T, mode, x):
    return {"x": np.ascontiguousarray(x, dtype=np.float32), "rope": _rope_table(T, mode),
            "maskA": _mask_a(mode), "maskB": _mask_b(T, mode)}


_NC_CACHE = {}


def kernel(x_prompt, x_sample, g_ffn1, w_ffn1_in, w_ffn1_out, g_mix, w_in, b_gate, g_q_a, g_k_a,
           rpb_b, w_branch_a, w_branch_b, w_out, g_ffn2, w_ffn2_in, w_ffn2_out, g_final):
    x_prompt = np.asarray(x_prompt, dtype=np.float32)
    x_sample = np.asarray(x_sample, dtype=np.float32)
    T = x_prompt.shape[1]
    assert x_prompt.shape[0] == 2 and x_sample.shape[0] == 8 and x_sample.shape[1] * 2 == T
    shared = _shared_inputs(g_ffn1, w_ffn1_in, w_ffn1_out, g_mix, w_in, b_gate, g_q_a, g_k_a, rpb_b,
                            w_branch_a, w_branch_b, w_out, g_ffn2, w_ffn2_in, w_ffn2_out, g_final)
    in_maps = []
    for c in range(8):
        if c < 2:
            m = _slot_inputs(T, "prompt", x_prompt[c])
        elif c < 6:
            i = c - 2
            m = _slot_inputs(T, "sample", np.concatenate([x_sample[2 * i], x_sample[2 * i + 1]], axis=0))
        else:
            m = _slot_inputs(T, "prompt", np.zeros((T, D), np.float32))
        m.update(shared)
        in_maps.append(m)
    if T not in _NC_CACHE:
        _NC_CACHE[T] = build_nc(T)
    res = run_bass_kernel_spmd(_NC_CACHE[T], in_maps, core_ids=list(range(8)))
    outs = [np.asarray(r["out"], dtype=np.float32) for r in res.results]
    y_prompt = np.stack([outs[0], outs[1]], axis=0)
    half = T // 2
    y_sample = np.stack([outs[2 + i // 2][(i % 2) * half:(i % 2 + 1) * half] for i in range(8)], axis=0)
    return (y_prompt, y_sample)
```

```python
import math
from contextlib import ExitStack

import numpy as np

import concourse.bass as bass
import concourse.mybir as mybir
from concourse.bass_utils import run_bass_kernel_spmd

F32 = mybir.dt.float32
BF16 = mybir.dt.bfloat16
AF = mybir.ActivationFunctionType
ALU = mybir.AluOpType
AX = mybir.AxisListType

D = 2048
DFF = 5504
NFC = DFF // 128
HD = 128
INW = 8704
EPS = 1e-6
GRID_W = 64
NEG = -30000.0
SCALE = HD ** -0.5

O_QA, O_KA, O_VA, O_QB, O_KB, O_VB, O_GA, O_GB = 0, 1024, 1280, 1536, 2560, 3584, 4608, 6656


class Tok:
    __slots__ = ("key", "sem", "val")

    def __init__(self, key, sem, val):
        self.key, self.sem, self.val = key, sem, val


class Buf:
    def __init__(self, name=""):
        self.name = name
        self.w = None
        self.r = {}

    def inherit(self, others):
        for o in others:
            if isinstance(o.w, list):
                for i_, t_ in enumerate(o.w):
                    self.r[("wl", t_.key)] = t_
                continue
            if o.w is not None:
                self.r[("w", o.w.key)] = o.w if (("w", o.w.key) not in self.r or self.r[("w", o.w.key)].val < o.w.val) else self.r[("w", o.w.key)]
            for k, t in o.r.items():
                if k not in self.r or self.r[k].val < t.val:
                    self.r[k] = t


class DSem:
    def __init__(self, sem, idx):
        self.sem, self.val, self.key = sem, 0, ("d", idx)


class Tr:
    def __init__(self, nc, es):
        self.nc = nc
        self.es = es
        self.engs = {"pe": nc.tensor, "act": nc.scalar, "dve": nc.vector, "pool": nc.gpsimd, "sp": nc.sync}
        self.sem = {k: es.enter_context(nc.semaphore("c_" + k)) for k in ("pe", "act", "dve", "pool")}
        self.cnt = {k: 0 for k in self.sem}
        self.seen = {}
        self.nds = 0
        self.nwait = 0
        self.pend = {k: ([], []) for k in self.sem}
        self.dsems = []
        self.pool_fifo = []
        self.pool_out = []

    def dsem(self):
        self.nds += 1
        d = DSem(self.es.enter_context(self.nc.semaphore("d%d" % self.nds)), self.nds)
        self.dsems.append(d)
        return d

    def barrier(self):
        toks = [Tok(k, self.sem[k], self.cnt[k]) for k in self.sem if self.cnt[k] > 0]
        toks += [Tok(d.key, d.sem, d.val) for d in self.dsems if d.val > 0]
        for e in ("pe", "act", "dve", "pool", "sp"):
            self.wait(e, toks)

    def wait(self, eng, toks):
        for t in toks:
            if t is None:
                continue
            if eng == "pe" and t.key == "pe":
                continue
            k = (eng, t.key)
            if self.seen.get(k, 0) < t.val:
                self.engs[eng].wait_ge(t.sem, t.val)
                self.seen[k] = t.val
                self.nwait += 1

    @staticmethod
    def deps(reads, writes):
        d = []
        for b in reads:
            d.extend(b.w if isinstance(b.w, list) else [b.w])
            if b.name.startswith("ps"):
                d.extend(b.r.values())
        for b in writes:
            d.extend(b.w if isinstance(b.w, list) else [b.w])
            d.extend(b.r.values())
        return d

    @staticmethod
    def mark(tok, reads, writes):
        for b in writes:
            b.w = tok
            b.r = {}
        for b in reads:
            o = b.r.get(tok.key)
            if o is None or o.val < tok.val:
                b.r[tok.key] = tok

    def op(self, eng, fn, reads=(), writes=(), inc=True, extra=()):
        self.wait(eng, self.deps(reads, writes) + list(extra))
        ins = fn()
        if not inc:
            self.pend[eng][0].extend(reads)
            self.pend[eng][1].extend(writes)
            return None
        self.cnt[eng] += 1
        ins.then_inc(self.sem[eng], 1)
        tok = Tok(eng, self.sem[eng], self.cnt[eng])
        pr, pw = self.pend[eng]
        self.mark(tok, list(reads) + pr, list(writes) + pw)
        self.pend[eng] = ([], [])
        return tok

    def dma(self, q, ds, out, in_, reads=(), writes=(), extra=(), ndesc=32, **kw):
        self.wait(q, self.deps(reads, writes) + list(extra))
        if q == "pool":
            self.pool_out.append(None)
            while sum(n_ for _, n_ in self.pool_fifo) + ndesc > 480 and self.pool_fifo:
                t_, _ = self.pool_fifo.pop(0)
                self.wait("pool", [t_])
            self.pool_out.pop()
        if ds.val > 0:
            self.wait(q, [Tok(ds.key, ds.sem, ds.val)])
        ins = self.engs[q].dma_start(out=out, in_=in_, **kw)
        ds.val += 16
        ins.then_inc(ds.sem, 16)
        tok = Tok(ds.key, ds.sem, ds.val)
        self.mark(tok, reads, writes)
        if q == "pool":
            self.pool_fifo.append((tok, ndesc))
        return tok


class Ring:
    def __init__(self, items):
        self.items = items
        self.i = 0

    def next(self):
        it = self.items[self.i % len(self.items)]
        self.i += 1
        return it


class _Stop(Exception):
    pass


def build_nc(T, stop=None):
    NT = T // 512
    NCH = T // 128
    NBLK = NCH
    nc = bass.Bass("TRN2", target_bir_lowering=False)

    def din(name, shape, dt=F32):
        return nc.dram_tensor(name, list(shape), dt, kind="ExternalInput")

    x_d = din("x", [T, D])
    rope_d = din("rope", [T, 128])
    maskA_d = din("maskA", [128, 4])
    maskB_d = din("maskB", [128, 9 * 896])
    ident_d = din("ident", [128, 128])
    g1c_d = din("g1c", [128, 16])
    gmc_d = din("gmc", [128, 16])
    g2c_d = din("g2c", [128, 16])
    gfb_d = din("gfb", [128, D])
    bgc_d = din("bgc", [128, 32])
    gqb_d = din("gqb", [128, 128])
    gkb_d = din("gkb", [128, 128])
    rpb_d = din("rpb", [120, 31])
    w1a_d = din("w1a", [D, 2 * DFF])
    w1b_d = din("w1b", [DFF, D])
    win_d = din("win", [D, INW])
    wa_d = din("wa", [1024, D])
    wb_d = din("wb", [1024, D])
    wo_d = din("wo", [D, D])
    w2a_d = din("w2a", [D, 2 * DFF])
    w2b_d = din("w2b", [DFF, D])
    out_d = nc.dram_tensor("out", [T, D], F32, kind="ExternalOutput")

    def dscr(name, shape, dt=BF16):
        return nc.dram_tensor(name, list(shape), dt)

    w1a_s = dscr("w1a_s", [D, 2 * DFF])
    w1b_s = dscr("w1b_s", [DFF, D])
    win_s = dscr("win_s", [D, INW])
    wa_s = dscr("wa_s", [1024, D])
    wb_s = dscr("wb_s", [1024, D])
    wo_s = dscr("wo_s", [D, D])
    w2a_s = dscr("w2a_s", [D, 2 * DFF])
    w2b_s = dscr("w2b_s", [DFF, D])
    h_s = dscr("h_s", [T, D], F32)
    qaT_s = dscr("qaT_s", [8, 128, T])
    kaT_s = dscr("kaT_s", [2, 128, T])
    va_s = dscr("va_s", [T, 256])
    qbT_s = dscr("qbT_s", [8, 128, T])
    kbT_s = dscr("kbT_s", [8, 128, T])
    vb_s = dscr("vb_s", [T, 1024])
    yaT_s = dscr("yaT_s", [8, 128, T])
    ybT_s = dscr("ybT_s", [8, 128, T])
    pad_s = dscr("pad_s", [120, 160], F32)
    uT_s = dscr("uT_s", [NT, 128, 16 * 512])

    try:
        with ExitStack() as es:
            tr = Tr(nc, es)

            def ckpt(k):
                if stop == k:
                    tr.barrier()
                    raise _Stop()
            op, dma = tr.op, tr.dma
            pe, act, dve, pool = nc.tensor, nc.scalar, nc.vector, nc.gpsimd

            _uid = [0]

            def sb(name, shape, dt, stack=es):
                _uid[0] += 1
                return stack.enter_context(nc.sbuf_tensor("s%d_%s" % (_uid[0], name), list(shape), dt))

            ps = [es.enter_context(nc.psum_tensor("ps%d" % i, [128, 512], F32)) for i in range(8)]
            psb = [p.bitcast(BF16) for p in ps]
            psB = [Buf("ps%d" % i) for i in range(8)]

            ident = sb("ident", [128, 128], F32)
            identb = sb("identb", [128, 128], BF16)
            g1c = sb("g1c", [128, 16], F32)
            gmc = sb("gmc", [128, 16], F32)
            g2c = sb("g2c", [128, 16], F32)
            bgc = sb("bgc", [128, 32], F32)
            gqb = sb("gqb", [128, 128], F32)
            gkb = sb("gkb", [128, 128], F32)
            maskA = sb("maskA", [128, 4], F32)
            biasA = sb("biasA", [128, 4], F32)
            small = sb("small", [128, 8], F32)
            cB = Buf("consts")
            cds = tr.dsem()
            for t_sb, t_d in ((ident, ident_d), (g1c, g1c_d), (gmc, gmc_d), (g2c, g2c_d), (bgc, bgc_d),
                              (gqb, gqb_d), (gkb, gkb_d), (maskA, maskA_d)):
                dma("sp", cds, t_sb[:], t_d.ap(), writes=[cB])
            identbB = Buf("identb")
            op("dve", lambda: dve.tensor_copy(out=identb[:], in_=ident[:]), reads=[cB], writes=[identbB])
            smB = Buf("small")
            op("dve", lambda: dve.tensor_reduce(out=small[:, 0:1], in_=gqb[:], axis=AX.X, op=ALU.max,
                                                apply_absolute_value=True), reads=[cB], writes=[smB])
            op("dve", lambda: dve.tensor_reduce(out=small[:, 1:2], in_=gkb[:], axis=AX.X, op=ALU.max,
                                                apply_absolute_value=True), reads=[cB, smB], writes=[smB])
            op("dve", lambda: dve.tensor_tensor(out=small[:, 2:3], in0=small[:, 0:1], in1=small[:, 1:2], op=ALU.mult),
               reads=[smB], writes=[smB])
            op("dve", lambda: dve.tensor_scalar(out=small[:, 3:4], in0=small[:, 2:3], scalar1=-math.sqrt(128.0),
                                                scalar2=None, op0=ALU.mult), reads=[smB], writes=[smB])
            biasAB = Buf("biasA")
            op("dve", lambda: dve.tensor_scalar(out=biasA[:], in0=maskA[:], scalar1=small[:, 3:4], scalar2=None,
                                                op0=ALU.add), reads=[smB, cB], writes=[biasAB])
            op("dve", lambda: dve.memset(small[:, 4:5], EPS), writes=[smB])
            ckpt(0)

            def convert(src, dst, rows, cols):
                ds = tr.dsem()
                b = Buf("w")
                tok = None
                for r0 in range(0, rows, 128):
                    o = dst.ap()[r0:r0 + 128, :]
                    i = src.ap()[r0:r0 + 128, :]
                    if cols > 2048:
                        o = o.rearrange("r (a b) -> r a b", a=8)
                        i = i.rearrange("r (a b) -> r a b", a=8)
                    tok = dma("pool", ds, o, i, ndesc=(64 if cols > 2048 else 8))
                b.w = tok
                return b

            def convert_cols(src, dst, rows, bw, order):
                bufs = {}
                lanes = [tr.dsem() for _ in range(4)]
                for a in order:
                    toks = {}
                    for i_, r0 in enumerate(range(0, rows, 128)):
                        toks[i_ % 4] = dma("pool", lanes[i_ % 4], dst.ap()[r0:r0 + 128, a * bw:(a + 1) * bw],
                                           src.ap()[r0:r0 + 128, a * bw:(a + 1) * bw], ndesc=8)
                    b = Buf("w")
                    b.w = list(toks.values())
                    bufs[a] = b
                return bufs

            def conv_jobs(src, dst, rows, cols, b):
                ds = tr.dsem()
                jobs = []
                cw = cols // 8 if cols > 2048 else cols
                for r0 in range(0, rows, 128):
                    for c0 in range(0, cols, cw):
                        def job(r0=r0, c0=c0):
                            b.w = dma("pool", ds, dst.ap()[r0:r0 + 128, c0:c0 + cw], src.ap()[r0:r0 + 128, c0:c0 + cw],
                                      ndesc=8)
                        jobs.append(job)
                return jobs

            OCT = 2 * DFF // 8
            w1a_oct = convert_cols(w1a_d, w1a_s, D, OCT, [0, 4, 1, 5, 2, 6, 3, 7])
            w1b_q = convert_cols(w1b_d, w1b_s, DFF, 512, [0, 1, 2, 3])
            winB = convert(win_d, win_s, D, INW)

            def w1a_cols(c0, c1):
                return [w1a_oct[a] for a in range(c0 // OCT, (c1 - 1) // OCT + 1)]

            def w1b_cols(c0, c1):
                return [w1b_q[a] for a in range(c0 // 512, (c1 - 1) // 512 + 1)]
            ckpt(1)
            WB = {}

            class NS:
                pass

            def mk13(stk, phase):
                n = NS()
                n.xt = sb("xt", [128, 4, D], F32, stk)
                n.xtB = [Buf("xt%d" % i) for i in range(4)]
                n.xnT = sb("xnT", [128, 16, 512], BF16, stk)
                n.xnTB = Buf("xnT")
                n.actT = sb("actT", [128, NFC, 512], BF16, stk)
                n.actTB = Buf("actT")
                wt = [sb("wr%d" % i, [128, 16 * 512], BF16, stk) for i in range(2)]
                n.wring = Ring([(wt[i], Buf("wr%d" % i), tr.dsem()) for i in range(2)])
                n.wr2 = {id(wt[i]): (Buf("wu%d" % i), tr.dsem()) for i in range(2)}
                w2t = [sb("w2r%d" % i, [128, NFC, 256], BF16, stk) for i in range(2)]
                n.w2ring = Ring([(w2t[i], Buf("w2r%d" % i), tr.dsem()) for i in range(2)])
                xnb = [sb("xnb%d" % i, [128, D], BF16, stk) for i in range(2)]
                n.xnbR = Ring([(xnb[i], Buf("xnb%d" % i)) for i in range(2)])
                n.stat = sb("stat", [128, 16], F32, stk)
                n.statB = Buf("stat")
                tmpf = [sb("tmpf%d" % i, [128, 512], F32, stk) for i in range(4)]
                n.tmpR = Ring([(tmpf[i], Buf("tmpf%d" % i)) for i in range(4)])
                n.psA = Ring([0, 1, 2, 3, 4, 5])
                n.psT = Ring([6, 7])
                n.xds = [tr.dsem() for _ in range(4)]
                if phase == 1:
                    stg = [sb("stg%d" % i, [128, 4, 512], BF16, stk) for i in range(2)]
                    n.stgR = Ring([(stg[i], Buf("stg%d" % i), tr.dsem()) for i in range(2)])
                    n.ropet = sb("ropet", [128, 4, 128], F32, stk)
                    n.ropeB = Buf("rope")
                    n.rtmp = [sb("rtmp%d" % i, [128, 128], F32, stk) for i in range(5)]
                    n.rtmpB = [Buf("rtmp%d" % i) for i in range(5)]
                    n.qrot = sb("qrot", [128, 4, 128], BF16, stk)
                    n.qrotB = Buf("qrot")
                else:
                    n.gfb = sb("gfb", [128, D], F32, stk)
                    n.gfbB = Buf("gfb")
                    dma("sp", cds, n.gfb[:], gfb_d.ap(), writes=[n.gfbB])
                    sgt = [sb("sgt%d" % i, [128, 512], F32, stk) for i in range(2)]
                    n.sgR = Ring([(sgt[i], Buf("sgt%d" % i)) for i in range(2)])
                return n

            def load_rows(n, src, t):
                for s in range(4):
                    dma("sp", n.xds[s], n.xt[:, s, :], src.ap()[t * 512 + s * 128: t * 512 + (s + 1) * 128, :],
                        writes=[n.xtB[s]])

            def row_stats(n):
                junk = n.actT[:, 0:4, :]
                for s in range(4):
                    op("act", lambda s=s: act.activation(out=junk, in_=n.xt[:, s, :].rearrange("p (a b) -> p a b", a=4),
                                                         func=AF.Square, accum_out=n.stat[:, s:s + 1]),
                       reads=[n.xtB[s]], writes=[n.actTB, n.statB])
                op("act", lambda: act.activation(out=n.stat[:, 4:8], in_=n.stat[:, 0:4], func=AF.Sqrt,
                                                 bias=small[:, 4:5], scale=1.0 / D), reads=[n.statB, smB], writes=[n.statB])
                op("dve", lambda: dve.reciprocal(out=n.stat[:, 8:12], in_=n.stat[:, 4:8]), reads=[n.statB], writes=[n.statB])

            def norm_to_T(n, gcol):
                row_stats(n)
                for half in range(2):
                    cur = []
                    for s in (2 * half, 2 * half + 1):
                        xb, xbB = n.xnbR.next()
                        op("dve", lambda s=s, xb=xb: dve.tensor_scalar(out=xb[:], in0=n.xt[:, s, :],
                                                                      scalar1=n.stat[:, 8 + s:9 + s], scalar2=None,
                                                                      op0=ALU.mult),
                           reads=[n.xtB[s], n.statB], writes=[xbB])
                        cur.append((xb, xbB))
                    for kc in range(16):
                        b = n.psT.next()
                        for i, (xb, xbB) in enumerate(cur):
                            op("pe", lambda b=b, i=i, xb=xb, kc=kc: pe.transpose(
                                out=psb[b][:, i * 128:(i + 1) * 128], in_=xb[:, kc * 128:(kc + 1) * 128],
                                identity=identb[:]),
                               reads=[xbB, identbB], writes=[psB[b]], inc=(i == 1))
                        op("act", lambda b=b, kc=kc, half=half: act.activation(
                            out=n.xnT[:, kc, half * 256:(half + 1) * 256], in_=psb[b][:, 0:256], func=AF.Copy,
                            scale=gcol[:, kc:kc + 1]),
                           reads=[psB[b], cB], writes=[n.xnTB])

            def wblock(n, wsrc, wB, kch, c0, ncols=512):
                wt, wtB, wds = n.wring.next()
                wuB_ = n.wr2[id(wt)][0]
                view = wt[:, 0:kch * ncols].rearrange("p (k n) -> p k n", n=ncols)
                dma("sp", wds, view, wsrc.ap().rearrange("(k p) n -> p k n", p=128)[:, :, c0:c0 + ncols],
                    reads=[wB], writes=[wtB, wuB_])
                return view, wtB

            def ffn(n, w_in_s, w_in_B, w_out_s, w_out_B, mid_hook=None, bg=None):
                inB = w_in_B if callable(w_in_B) else (lambda c0, c1: [w_in_B])
                outB = w_out_B if callable(w_out_B) else (lambda c0, c1: [w_out_B])
                ngrp = (NFC + 1) // 2
                src = w_in_s.ap().rearrange("(k p) n -> p k n", p=128)
                for gi in range(ngrp):
                    nch = min(2, NFC - 2 * gi)
                    wt, wtB, wds = n.wring.next()
                    view = wt[:, :].rearrange("p (a k n) -> p a k n", a=2, k=16)
                    wuB, wuds = n.wr2[id(wt)]
                    ex_u = tr.deps([], [wtB])
                    ex_g = tr.deps([], [wuB])
                    dma("sp", wds, view[:, 0, :, 0:nch * 128], src[:, :, gi * 256: gi * 256 + nch * 128],
                        reads=inB(gi * 256, gi * 256 + nch * 128), writes=[wtB], extra=ex_g)
                    dma("sp", wuds, view[:, 1, :, 0:nch * 128], src[:, :, DFF + gi * 256: DFF + gi * 256 + nch * 128],
                        reads=inB(DFF + gi * 256, DFF + gi * 256 + nch * 128), writes=[wuB], extra=ex_u)
                    if bg is not None:
                        bg()
                    for c in range(nch):
                        fc = 2 * gi + c
                        bg_, bu = n.psA.next(), n.psA.next()
                        for a, b in ((0, bg_), (1, bu)):
                            for kc in range(16):
                                op("pe", lambda a=a, b=b, kc=kc, c=c, view=view: pe.matmul(
                                    ps[b][:, :], lhsT=view[:, a, kc, c * 128:(c + 1) * 128], rhs=n.xnT[:, kc, :],
                                    start=(kc == 0), stop=(kc == 15)),
                                   reads=[wtB, wuB, n.xnTB], writes=[psB[b]], inc=(kc == 15))
                        tf, tfB = n.tmpR.next()
                        op("act", lambda tf=tf: act.activation(out=tf[:], in_=ps[bg_][:, :], func=AF.Silu),
                           reads=[psB[bg_]], writes=[tfB])
                        op("dve", lambda bu=bu, tf=tf, fc=fc: dve.tensor_tensor(out=n.actT[:, fc, :], in0=tf[:],
                                                                                in1=ps[bu][:, :], op=ALU.mult),
                           reads=[tfB, psB[bu]], writes=[n.actTB])
                if mid_hook is not None:
                    mid_hook()
                srco = w_out_s.ap().rearrange("(k p) n -> p k n", p=128)
                for db in range(8):
                    if bg is not None:
                        bg()
                    w2t, w2B, w2ds = n.w2ring.next()
                    dma("sp", w2ds, w2t[:], srco[:, :, db * 256:(db + 1) * 256], reads=outB(db * 256, (db + 1) * 256),
                        writes=[w2B])
                    for s in range(4):
                        b = n.psA.next()
                        for fc in range(NFC):
                            op("pe", lambda b=b, fc=fc, s=s, w2t=w2t: pe.matmul(
                                ps[b][:, 0:256], lhsT=n.actT[:, fc, s * 128:(s + 1) * 128], rhs=w2t[:, fc, :],
                                start=(fc == 0), stop=(fc == NFC - 1)),
                               reads=[n.actTB, w2B], writes=[psB[b]], inc=(fc == NFC - 1))
                        op("dve", lambda b=b, s=s, db=db: dve.scalar_tensor_tensor(
                            out=n.xt[:, s, db * 256:(db + 1) * 256], in0=ps[b][:, 0:256], scalar=0.5,
                            in1=n.xt[:, s, db * 256:(db + 1) * 256], op0=ALU.mult, op1=ALU.add),
                           reads=[psB[b]], writes=[n.xtB[s]])

            with ExitStack() as p1:
                n = mk13(p1, 1)
                hds = [tr.dsem() for _ in range(4)]
                rds = tr.dsem()
                uds = tr.dsem()
                noB = Buf("dram")

                def tok_major_mm(b, s, wv, wvB):
                    for kc in range(16):
                        op("pe", lambda kc=kc: pe.matmul(ps[b][:, :], lhsT=n.xnT[:, kc, s * 128:(s + 1) * 128],
                                                         rhs=wv[:, kc, :], start=(kc == 0), stop=(kc == 15)),
                           reads=[n.xnTB, wvB], writes=[psB[b]], inc=(kc == 15))

                qrots = [sb("qrot%d" % i, [128, 4, 128], BF16, p1) for i in range(3)]
                qrotR = Ring([(qrots[i], Buf("qrot%d" % i)) for i in range(3)])

                def qk_chain(b, s, nh, gb_t):
                    tf, tfB = n.tmpR.next()
                    op("act", lambda: act.activation(out=tf[:, :], in_=ps[b][:, :], func=AF.Square),
                       reads=[psB[b]], writes=[tfB])
                    op("dve", lambda: dve.tensor_reduce(out=n.stat[:, 12:12 + nh],
                                                        in_=tf[:, 0:nh * 128].rearrange("p (h d) -> p h d", d=128),
                                                        axis=AX.X, op=ALU.add), reads=[tfB], writes=[n.statB])
                    op("act", lambda: act.activation(out=n.stat[:, 12:12 + nh], in_=n.stat[:, 12:12 + nh], func=AF.Sqrt,
                                                     bias=small[:, 4:5], scale=1.0 / 128), reads=[n.statB, smB],
                       writes=[n.statB])
                    op("dve", lambda: dve.reciprocal(out=n.stat[:, 12:12 + nh], in_=n.stat[:, 12:12 + nh]),
                       reads=[n.statB], writes=[n.statB])
                    cs = n.ropet[:, s, 0:64]
                    sn = n.ropet[:, s, 64:128]
                    rt, rtB = n.rtmp, n.rtmpB
                    qr, qrB = qrotR.next()
                    for i in range(nh):
                        op("dve", lambda i=i: dve.scalar_tensor_tensor(
                            out=rt[0][:], in0=ps[b][:, i * 128:(i + 1) * 128], scalar=n.stat[:, 12 + i:13 + i],
                            in1=gb_t[:], op0=ALU.mult, op1=ALU.mult),
                           reads=[psB[b], n.statB, cB], writes=[rtB[0]])
                        x0 = rt[0][:, 0:128:2]
                        x1 = rt[0][:, 1:128:2]
                        op("dve", lambda: dve.tensor_tensor(out=rt[1][:, 0:64], in0=x0, in1=cs, op=ALU.mult),
                           reads=[rtB[0], n.ropeB], writes=[rtB[1]])
                        op("dve", lambda: dve.tensor_tensor(out=rt[2][:, 0:64], in0=x1, in1=sn, op=ALU.mult),
                           reads=[rtB[0], n.ropeB], writes=[rtB[2]])
                        op("dve", lambda i=i: dve.tensor_tensor(out=qr[:, i, 0:128:2], in0=rt[1][:, 0:64],
                                                                in1=rt[2][:, 0:64], op=ALU.subtract),
                           reads=[rtB[1], rtB[2]], writes=[qrB])
                        op("pool", lambda: pool.tensor_tensor(out=rt[3][:, 0:64], in0=x0, in1=sn, op=ALU.mult),
                           reads=[rtB[0], n.ropeB], writes=[rtB[3]])
                        op("pool", lambda: pool.tensor_tensor(out=rt[4][:, 0:64], in0=x1, in1=cs, op=ALU.mult),
                           reads=[rtB[0], n.ropeB], writes=[rtB[4]])
                        op("pool", lambda i=i: pool.tensor_tensor(out=qr[:, i, 1:128:2], in0=rt[3][:, 0:64],
                                                                  in1=rt[4][:, 0:64], op=ALU.add),
                           reads=[rtB[3], rtB[4]], writes=[qrB])
                    return qr, qrB

                def wslot(slot, c0):
                    wt, wtB, wds = n.wring.items[slot]
                    wuB_ = n.wr2[id(wt)][0]
                    view = wt[:, 0:16 * 512].rearrange("p (k n) -> p k n", n=512)
                    dma("sp", wds, view, win_s.ap().rearrange("(k p) n -> p k n", p=128)[:, :, c0:c0 + 512],
                        reads=[winB], writes=[wtB, wuB_])
                    return view, wtB

                RBLK = [O_QA, O_QA + 512, O_KA]
                FBLK = [(O_QB, qbT_s, 0, "f"), (O_QB + 512, qbT_s, 1, "f"), (O_KB, kbT_s, 0, "f"),
                        (O_KB + 512, kbT_s, 1, "f"), (O_VB, vb_s, 0, "t"), (O_VB + 512, vb_s, 1, "t")]
                stFv = [n.actT[:, 4:8, :], n.actT[:, 8:12, :]]
                stFds = [tr.dsem(), tr.dsem()]
                stVv = n.actT[:, 12:16, :]
                stVds = tr.dsem()

                jobs = []
                for nm, (sd, ss, rr, cc) in (("wa", (wa_d, wa_s, 1024, D)), ("wb", (wb_d, wb_s, 1024, D)),
                                             ("wo", (wo_d, wo_s, D, D)), ("w2a", (w2a_d, w2a_s, D, 2 * DFF)),
                                             ("w2b", (w2b_d, w2b_s, DFF, D))):
                    WB[nm] = Buf("w")
                    jobs += conv_jobs(sd, ss, rr, cc, WB[nm])
                per_call = (len(jobs) + 30 * max(NT - 1, 1) - 1) // (30 * max(NT - 1, 1))

                def bgjob():
                    for _ in range(per_call):
                        if jobs:
                            jobs.pop(0)()

                for t in range(NT):
                    load_rows(n, x_d, t)
                    dma("sp", rds, n.ropet[:], rope_d.ap()[t * 512:(t + 1) * 512, :].rearrange("(s p) c -> p s c", p=128),
                        writes=[n.ropeB])
                    norm_to_T(n, g1c)
                    ckpt(2)
                    ffn(n, w1a_s, w1a_cols, w1b_s, w1b_cols, bg=(bgjob if t >= 1 else None))
                    ckpt(3)
                    for s_ in range(4):
                        dma("pool", hds[s_], h_s.ap()[t * 512 + s_ * 128: t * 512 + (s_ + 1) * 128, :], n.xt[:, s_, :],
                            reads=[n.xtB[s_]], writes=[noB], ndesc=8)
                    norm_to_T(n, gmc)
                    dma("pool", uds, uT_s.ap()[t].rearrange("p (k n) -> p k n", n=512), n.xnT[:],
                        reads=[n.xnTB], writes=[noB], ndesc=128)
                    tsl = slice(t * 512, (t + 1) * 512)
                    stFB = [Buf("stF0"), Buf("stF1")]
                    stVB = Buf("stV")
                    for b_ in stFB + [stVB]:
                        b_.inherit([n.actTB])
                    rstate = {}
                    Rw = {0: wslot(0, RBLK[0])}
                    Fw = {0: wslot(1, FBLK[0][0])}
                    Rst = {}

                    def R_mm(k):
                        blk, s = k // 4, k % 4
                        wv, wvB = Rw[blk]
                        if s == 0:
                            Rst[blk] = [n.stgR.next()] + ([(stVv, stVB, stVds)] if blk == 2 else [])
                        b = n.psA.next()
                        tok_major_mm(b, s, wv, wvB)
                        if s == 3 and blk + 1 < 3:
                            Rw[blk + 1] = wslot(0, RBLK[blk + 1])
                        if blk == 2:
                            st2, st2B, _ = Rst[blk][1]
                            op("dve", lambda: dve.tensor_copy(out=st2[:, s, 0:256], in_=ps[b][:, 256:512]),
                               reads=[psB[b]], writes=[st2B])
                        nh = 2 if blk == 2 else 4
                        rstate[k] = (qk_chain(b, s, nh, gkb if blk == 2 else gqb), nh)

                    def R_tail(k):
                        blk, s = k // 4, k % 4
                        (qr, qrB), nh = rstate.pop(k)
                        st, stB, sds = Rst[blk][0]
                        bt = n.psT.next()
                        for i in range(nh):
                            op("pe", lambda i=i: pe.transpose(out=psb[bt][:, i * 128:(i + 1) * 128], in_=qr[:, i, :],
                                                              identity=identb[:]),
                               reads=[qrB, identbB], writes=[psB[bt]], inc=(i == nh - 1))
                        op("act", lambda: act.activation(
                            out=st[:, 0:nh, s * 128:(s + 1) * 128],
                            in_=psb[bt][:, 0:nh * 128].rearrange("p (h t) -> p h t", h=nh), func=AF.Copy),
                           reads=[psB[bt]], writes=[stB])
                        if s == 3:
                            if blk < 2:
                                dma("pool", sds, qaT_s.ap()[blk * 4:(blk + 1) * 4, :, tsl].rearrange("h d t -> d h t"),
                                    st[:], reads=[stB], writes=[noB])
                            else:
                                st2, st2B, sds2 = Rst[blk][1]
                                dma("pool", sds, kaT_s.ap()[:, :, tsl].rearrange("h d t -> d h t"), st[:, 0:2, :],
                                    reads=[stB], writes=[noB])
                                dma("pool", sds2, va_s.ap()[tsl, :].rearrange("(s p) c -> p s c", p=128),
                                    st2[:, :, 0:256], reads=[st2B], writes=[noB])

                    def F_unit(m):
                        blk, u = m // 4, m % 4
                        c0, dst, half, kind = FBLK[blk]
                        wv, wvB = Fw[blk]
                        stv, stB_, sds_ = stFv[blk % 2], stFB[blk % 2], stFds[blk % 2]
                        b = n.psA.next()
                        if kind == "f":
                            for kc in range(16):
                                op("pe", lambda kc=kc: pe.matmul(
                                    ps[b][:, :], lhsT=wv[:, kc, u * 128:(u + 1) * 128], rhs=n.xnT[:, kc, :],
                                    start=(kc == 0), stop=(kc == 15)),
                                   reads=[n.xnTB, wvB], writes=[psB[b]], inc=(kc == 15))
                        else:
                            tok_major_mm(b, u, wv, wvB)
                        if u == 3 and blk + 1 < 6:
                            Fw[blk + 1] = wslot(1, FBLK[blk + 1][0])
                        if u % 2 == 0:
                            op("act", lambda: act.activation(out=stv[:, u, :], in_=ps[b][:, :], func=AF.Copy),
                               reads=[psB[b]], writes=[stB_])
                        else:
                            op("dve", lambda: dve.tensor_copy(out=stv[:, u, :], in_=ps[b][:, :]),
                               reads=[psB[b]], writes=[stB_])
                        if u == 3:
                            if kind == "f":
                                dma("pool", sds_, dst.ap()[half * 4:(half + 1) * 4, :, tsl].rearrange("h d t -> d h t"),
                                    stv, reads=[stB_], writes=[noB])
                            else:
                                dma("pool", sds_,
                                    dst.ap()[tsl, half * 512:(half + 1) * 512].rearrange("(s p) c -> p s c", p=128),
                                    stv, reads=[stB_], writes=[noB])

                    for k in range(12):
                        R_mm(k)
                        F_unit(2 * k)
                        F_unit(2 * k + 1)
                        if k >= 2:
                            R_tail(k - 2)
                    R_tail(10)
                    R_tail(11)
                    n.actTB.inherit(stFB + [stVB])
                    ckpt(372)
                    if t == NT - 1:
                        while jobs:
                            jobs.pop(0)()
                tr.barrier()
                ckpt(4)

            with ExitStack() as p2:
                ads = [tr.dsem() for _ in range(8)]
                zt = sb("zt", [120, 160], F32, p2)
                ztB = Buf("zt")
                padB = Buf("pad")
                op("dve", lambda: dve.memset(zt[:], 0.0), writes=[ztB])
                dma("pool", ads[0], pad_s.ap(), zt[:], reads=[ztB], writes=[padB])
                dma("pool", ads[0], pad_s.ap()[:, 64:95], rpb_d.ap(), reads=[], writes=[padB])
                BS = sb("BS", [128, 8, 14 * 64], F32, p2)
                BSB = Buf("BS")
                bslanes = [tr.dsem() for _ in range(8)]
                bstoks = {}
                for ql in range(2):
                    for qc in range(64):
                        p = ql * 64 + qc
                        src = bass.AP(pad_s, (1 - ql) * 160 + 79 - qc, [[1, 1], [2400, 8], [160, 14], [1, 64]])
                        tr.wait("pool", [padB.w])
                        bstoks[p % 8] = dma("pool", bslanes[p % 8],
                                            BS[p:p + 1, :, :].rearrange("p h (r c) -> p h r c", c=64), src, ndesc=16)
                BSB.w = list(bstoks.values())
                maskB = sb("maskB", [128, 9, 896], F32, p2)
                maskBB = Buf("maskB")
                ckpt(45)

                KT = sb("KT", [128, T], BF16, p2)
                KTB = Buf("KT")
                Vg = sb("Vg", [128, NCH, 129], BF16, p2)
                VgB = Buf("Vg")
                op("dve", lambda: dve.memset(Vg[:, :, 128:129], 1.0), writes=[VgB])
                QTt = [sb("QT%d" % i, [128, 4, 512], BF16, p2) for i in range(2)]
                QTR = Ring([(QTt[i], Buf("QT%d" % i), ads[3 + i]) for i in range(2)])
                PTt = [sb("PT%d" % i, [128, 512], BF16, p2) for i in range(3)]
                PTR = Ring([(PTt[i], Buf("PT%d" % i)) for i in range(3)])
                ynt = [sb("yn%d" % i, [128, 128], BF16, p2) for i in range(2)]
                ynR = Ring([(ynt[i], Buf("yn%d" % i)) for i in range(2)])
                ystt = [sb("yst%d" % i, [128, 512], BF16, p2) for i in range(2)]
                ystR = Ring([(ystt[i], Buf("yst%d" % i), ads[5 + i]) for i in range(2)])
                rinv = sb("rinv", [128, 4], F32, p2)
                rinvB = Buf("rinv")
                SR = Ring([4, 5, 6])
                noB = Buf("dram2")
                for g in range(2):
                    dma("sp", ads[7], KT[:], kaT_s.ap()[g, :, :], writes=[KTB])
                    dma("sp", ads[1], Vg[:, :, 0:128],
                        va_s.ap()[:, g * 128:(g + 1) * 128].rearrange("(c p) d -> p c d", p=128), writes=[VgB])
                    items = [(qt, hl, kc) for qt in range(NT) for hl in range(4) for kc in range(NCH)]
                    qts = {}

                    def getQ(qt):
                        if qt not in qts:
                            q_, qB_, qds_ = QTR.next()
                            dma("sp", qds_, q_[:],
                                qaT_s.ap()[g * 4:(g + 1) * 4, :, qt * 512:(qt + 1) * 512].rearrange("h d t -> d h t"),
                                writes=[qB_])
                            qts[qt] = (q_, qB_)
                        return qts[qt]

                    sbank = {}

                    def emitS(idx):
                        qt, hl, kc = items[idx]
                        q_, qB_ = getQ(qt)
                        b = SR.next()
                        sbank[idx] = b
                        op("pe", lambda: pe.matmul(ps[b][:, :], lhsT=KT[:, kc * 128:(kc + 1) * 128], rhs=q_[:, hl, :],
                                                   start=True, stop=True), reads=[KTB, qB_], writes=[psB[b]])

                    emitS(0)
                    emitS(1)
                    if g == 0:
                        dma("sp", ads[2], maskB[:], maskB_d.ap().rearrange("p (t c) -> p t c", c=896), writes=[maskBB])
                    for idx, (qt, hl, kc) in enumerate(items):
                        if idx + 2 < len(items):
                            emitS(idx + 2)
                        b = sbank.pop(idx)
                        mi = (2 if kc >= NCH // 2 else 0) + (1 if qt >= NT // 2 else 0)
                        pt, ptB = PTR.next()
                        op("act", lambda: act.activation(out=pt[:], in_=ps[b][:, :], func=AF.Exp,
                                                         bias=biasA[:, mi:mi + 1], scale=SCALE),
                           reads=[psB[b], biasAB], writes=[ptB])
                        for qs in range(4):
                            op("pe", lambda qs=qs: pe.matmul(ps[qs][:, 0:129], lhsT=pt[:, qs * 128:(qs + 1) * 128],
                                                             rhs=Vg[:, kc, :], start=(kc == 0), stop=(kc == NCH - 1)),
                               reads=[ptB, VgB], writes=[psB[qs]], inc=(qs == 3))
                        if kc == NCH - 1:
                            h = g * 4 + hl
                            yst, ystB, yds = ystR.next()
                            for qs in range(4):
                                op("dve", lambda qs=qs: dve.reciprocal(out=rinv[:, qs:qs + 1], in_=ps[qs][:, 128:129]),
                                   reads=[psB[qs]], writes=[rinvB])
                                yn, ynB = ynR.next()
                                op("dve", lambda qs=qs, yn=yn: dve.tensor_scalar(out=yn[:], in0=ps[qs][:, 0:128],
                                                                                 scalar1=rinv[:, qs:qs + 1], scalar2=None,
                                                                                 op0=ALU.mult),
                                   reads=[psB[qs], rinvB], writes=[ynB])
                                op("pe", lambda qs=qs, yn=yn: pe.transpose(out=psb[7][:, qs * 128:(qs + 1) * 128],
                                                                           in_=yn[:], identity=identb[:]),
                                   reads=[ynB, identbB], writes=[psB[7]])
                            op("act", lambda yst=yst: act.activation(out=yst[:], in_=psb[7][:, 0:512], func=AF.Copy),
                               reads=[psB[7]], writes=[ystB])
                            dma("pool", yds, yaT_s.ap()[h, :, qt * 512:(qt + 1) * 512], yst[:], reads=[ystB], writes=[noB])

                ckpt(5)
                master = sb("master", [128, 8, 896], F32, p2)
                masterB = Buf("master")
                for h in range(8):
                    for (bk, d0, nd) in ((4, 0, 4), (5, 4, 3)):
                        for i in range(nd):
                            di = d0 + i
                            op("pe", lambda i=i, di=di, bk=bk: pe.transpose(
                                out=ps[bk][:, i * 128:(i + 1) * 128], in_=BS[:, h, di * 128:(di + 1) * 128],
                                identity=ident[:]),
                               reads=[BSB, cB], writes=[psB[bk]], inc=(i == nd - 1))
                        op("dve", lambda bk=bk, d0=d0, nd=nd: dve.tensor_copy(
                            out=master[:, h, d0 * 128:(d0 + nd) * 128], in_=ps[bk][:, 0:nd * 128]),
                           reads=[psB[bk]], writes=[masterB])
                combt = [sb("comb%d" % i, [128, 896], F32, p2) for i in range(2)]
                combBs = [Buf("comb%d" % i) for i in range(2)]

                def build_comb(h):
                    op("pool", lambda: pool.tensor_tensor(out=combt[h % 2][:], in0=master[:, h, :], in1=maskB[:, 2, :],
                                                          op=ALU.add),
                       reads=[masterB, maskBB], writes=[combBs[h % 2]])
                KTb = sb("KTb", [128, T], BF16, p2)
                QTb = sb("QTb", [128, T], BF16, p2)
                Vb = sb("Vb", [128, NCH, 129], BF16, p2)
                QTb2 = sb("QTb2", [128, T], BF16, p2)
                vinitB = Buf("vinit")
                op("dve", lambda: dve.memset(Vb[:, :, 128:129], 1.0), writes=[vinitB])
                sets = [(KTb, QTb, Vb, [Buf("k0"), Buf("q0"), Buf("v0")], [tr.dsem(), tr.dsem(), tr.dsem()]),
                        (KT, QTb2, Vg, [Buf("k1"), Buf("q1"), Buf("v1")], [tr.dsem(), tr.dsem(), tr.dsem()])]
                sets[0][3][2].w = vinitB.w
                sets[1][3][0].inherit([KTB])
                sets[1][3][2].inherit([VgB])

                def loadset(h):
                    k_, q_, v_, bs_, ds_ = sets[h % 2]
                    dma("sp", ds_[0], k_[:], kbT_s.ap()[h, :, :], writes=[bs_[0]])
                    dma("sp", ds_[1], q_[:], qbT_s.ap()[h, :, :], writes=[bs_[1]])
                    dma("sp", ds_[2], v_[:, :, 0:128],
                        vb_s.ap()[:, h * 128:(h + 1) * 128].rearrange("(c p) d -> p c d", p=128), writes=[bs_[2]])

                sTt = [sb("sT%d" % i, [128, 896], F32, p2) for i in range(2)]
                sTR = Ring([(sTt[i], Buf("sT%d" % i)) for i in range(2)])
                PBt = [sb("PB%d" % i, [128, 896], BF16, p2) for i in range(3)]
                PBR = Ring([(PBt[i], Buf("PB%d" % i)) for i in range(3)])
                ynb = [sb("ynb%d" % i, [128, 128], BF16, p2) for i in range(3)]
                ynbR = Ring([(ynb[i], Buf("ynb%d" % i)) for i in range(3)])
                rinvb = sb("rinvb", [128, 4], F32, p2)
                rinvbB = Buf("rinvb")
                SBR = Ring([(0, 1), (2, 3)])
                ACR = Ring([4, 5])
                TBR = Ring([6, 7])

                def ti_of(j):
                    if j == 0:
                        return 0
                    if j == 1:
                        return 1
                    if NBLK // 2 - 2 <= j <= NBLK // 2 + 1:
                        return 3 + j - (NBLK // 2 - 2)
                    if j == NBLK - 2:
                        return 7
                    if j == NBLK - 1:
                        return 8
                    return 2

                loadset(0)
                build_comb(0)
                for h in range(8):
                    if h + 1 < 8:
                        loadset(h + 1)
                        build_comb(h + 1)
                    comb, combB = combt[h % 2], combBs[h % 2]
                    KTh, QTh, Vh, (kB_, qB_, vB_), _ = sets[h % 2]
                    st_ = {}

                    def stA(j):
                        ba, bb = SBR.next()
                        st_[j] = {"s": (ba, bb)}
                        for di in range(7):
                            c = min(max(j - 3 + di, 0), NCH - 1)
                            bk, off = (ba, di * 128) if di < 4 else (bb, (di - 4) * 128)
                            op("pe", lambda c=c, bk=bk, off=off: pe.matmul(
                                ps[bk][:, off:off + 128], lhsT=KTh[:, c * 128:(c + 1) * 128],
                                rhs=QTh[:, j * 128:(j + 1) * 128], start=True, stop=True),
                               reads=[kB_, qB_], writes=[psB[bk]], inc=(di == 3 or di == 6))

                    def stB(j):
                        ba, bb = st_[j]["s"]
                        ti = ti_of(j)
                        sT, sTB = sTR.next()
                        if ti == 2:
                            bsrc, bsB = comb, combB
                        else:
                            bsrc, bsB = master[:, h, :], masterB
                        op("dve", lambda: dve.scalar_tensor_tensor(out=sT[:, 0:512], in0=ps[ba][:, :], scalar=SCALE,
                                                                   in1=bsrc[:, 0:512], op0=ALU.mult, op1=ALU.add),
                           reads=[psB[ba], bsB], writes=[sTB])
                        op("dve", lambda: dve.scalar_tensor_tensor(out=sT[:, 512:896], in0=ps[bb][:, 0:384],
                                                                   scalar=SCALE, in1=bsrc[:, 512:896],
                                                                   op0=ALU.mult, op1=ALU.add),
                           reads=[psB[bb], bsB], writes=[sTB])
                        if ti != 2:
                            op("pool", lambda: pool.tensor_tensor(out=sT[:], in0=sT[:], in1=maskB[:, ti, :], op=ALU.add),
                               reads=[maskBB], writes=[sTB])
                        pb, pbB = PBR.next()
                        op("act", lambda: act.activation(out=pb[:], in_=sT[:], func=AF.Exp), reads=[sTB], writes=[pbB])
                        st_[j]["p"] = (pb, pbB)

                    def stC(j):
                        pb, pbB = st_[j]["p"]
                        ac = ACR.next()
                        st_[j]["ac"] = ac
                        for di in range(7):
                            c = min(max(j - 3 + di, 0), NCH - 1)
                            op("pe", lambda di=di, c=c: pe.matmul(ps[ac][:, 0:129], lhsT=pb[:, di * 128:(di + 1) * 128],
                                                                  rhs=Vh[:, c, :], start=(di == 0), stop=(di == 6)),
                               reads=[pbB, vB_], writes=[psB[ac]], inc=(di == 6))

                    def stD(j):
                        ac = st_[j]["ac"]
                        op("dve", lambda: dve.reciprocal(out=rinvb[:, j % 4:j % 4 + 1], in_=ps[ac][:, 128:129]),
                           reads=[psB[ac]], writes=[rinvbB])
                        yn, ynB = ynbR.next()
                        op("dve", lambda: dve.tensor_scalar(out=yn[:], in0=ps[ac][:, 0:128],
                                                            scalar1=rinvb[:, j % 4:j % 4 + 1], scalar2=None, op0=ALU.mult),
                           reads=[psB[ac], rinvbB], writes=[ynB])
                        st_[j]["yn"] = (yn, ynB)

                    tbs = {}

                    def stE(j):
                        yn, ynB = st_[j]["yn"]
                        if j % 4 == 0:
                            tbs[j // 4] = TBR.next()
                        tb = tbs[j // 4]
                        op("pe", lambda: pe.transpose(out=psb[tb][:, (j % 4) * 128:(j % 4 + 1) * 128], in_=yn[:],
                                                      identity=identb[:]), reads=[ynB, identbB], writes=[psB[tb]])
                        if j % 4 == 3:
                            yst, ystB, yds = ystR.next()
                            op("act", lambda: act.activation(out=yst[:], in_=psb[tb][:, 0:512], func=AF.Copy),
                               reads=[psB[tb]], writes=[ystB])
                            dma("pool", yds, ybT_s.ap()[h, :, (j - 3) * 128:(j + 1) * 128], yst[:], reads=[ystB],
                                writes=[noB], ndesc=8)
                        del st_[j]

                    for i in range(-2, NBLK + 2):
                        if 0 <= i + 2 < NBLK:
                            stA(i + 2)
                        if 0 <= i + 1 < NBLK:
                            stB(i + 1)
                        if 0 <= i < NBLK:
                            stC(i)
                        if 0 <= i - 1 < NBLK:
                            stD(i - 1)
                        if 0 <= i - 2 < NBLK:
                            stE(i - 2)
                tr.barrier()
                ckpt(6)

            with ExitStack() as p3:
                n = mk13(p3, 3)
                mT = n.actT[:, 0:16, :]
                yaT = n.actT[:, 16:24, :]
                ybT = n.actT[:, 24:32, :]
                yds = tr.dsem()
                ods = [tr.dsem() for _ in range(4)]
                noB = Buf("dram3")
                out_toks = []
                uds3 = tr.dsem()

                def prefetch_u(t):
                    dma("sp", uds3, n.xnT[:], uT_s.ap()[t].rearrange("p (k n) -> p k n", n=512), writes=[n.xnTB])

                def gslot(slot, c0):
                    wt, wtB, wds = n.wring.items[slot]
                    wuB_ = n.wr2[id(wt)][0]
                    view = wt[:, 0:16 * 512].rearrange("p (k n) -> p k n", n=512)
                    dma("sp", wds, view, win_s.ap().rearrange("(k p) n -> p k n", p=128)[:, :, c0:c0 + 512],
                        reads=[winB], writes=[wtB, wuB_])
                    return view, wtB

                def abslot(fb):
                    w2t, w2B, w2ds = n.w2ring.items[fb % 2]
                    view = w2t[:, :, :].rearrange("p a b -> p (a b)")[:, 0:8192].rearrange("p (r k n) -> p r k n", r=2, k=8)
                    for r_, (wsrc, wk) in enumerate(((wa_s, "wa"), (wb_s, "wb"))):
                        dma("sp", w2ds, view[:, r_, :, :],
                            wsrc.ap().rearrange("(k p) n -> p k n", p=128)[:, :, fb * 512:(fb + 1) * 512],
                            reads=[WB[wk]], writes=[w2B])
                    return view, w2B

                gpre = {}

                def issue_first(t):
                    gpre[t] = {"ab": {0: abslot(0)}, "ga": {0: gslot(0, O_GA)}, "gb": {0: gslot(1, O_GB)}}

                prefetch_u(0)
                issue_first(0)
                for t in range(NT):
                    tsl = slice(t * 512, (t + 1) * 512)
                    dma("sp", yds, yaT, yaT_s.ap()[:, :, tsl].rearrange("h d t -> d h t"), writes=[n.actTB])
                    dma("sp", yds, ybT, ybT_s.ap()[:, :, tsl].rearrange("h d t -> d h t"), writes=[n.actTB])
                    yB = Buf("y")
                    yB.w = n.actTB.w
                    mB = Buf("m")
                    mB.inherit([n.actTB])
                    G = gpre.pop(t)
                    for fb in range(4):
                        if fb + 1 < 4:
                            G["ab"][fb + 1] = abslot(fb + 1)
                        abv, abB = G["ab"][fb]
                        tA = []
                        for br in range(2):
                            yv = (yaT, ybT)[br]
                            wg, wgB = G[("ga", "gb")[br]][fb]
                            for fc in range(4):
                                ch = fb * 4 + fc
                                bz, bgt = n.psA.next(), n.psA.next()
                                for kc in range(16):
                                    op("pe", lambda kc=kc, fc=fc: pe.matmul(
                                        ps[bgt][:, :], lhsT=wg[:, kc, fc * 128:(fc + 1) * 128], rhs=n.xnT[:, kc, :],
                                        start=(kc == 0), stop=(kc == 15)),
                                       reads=[wgB, n.xnTB], writes=[psB[bgt]], inc=(kc == 15))
                                for h in range(8):
                                    op("pe", lambda h=h, fc=fc: pe.matmul(
                                        ps[bz][:, :], lhsT=abv[:, br, h, fc * 128:(fc + 1) * 128], rhs=yv[:, h, :],
                                        start=(h == 0), stop=(h == 7)),
                                       reads=[abB, yB], writes=[psB[bz]], inc=(h == 7))
                                if br == 0:
                                    tf, tfB = n.tmpR.next()
                                    tA.append((tf, tfB))
                                else:
                                    tf, tfB = n.sgR.next()
                                op("act", lambda: act.activation(out=tf[:], in_=ps[bgt][:, :], func=AF.Sigmoid,
                                                                 bias=bgc[:, br * 16 + ch: br * 16 + ch + 1]),
                                   reads=[psB[bgt], cB], writes=[tfB])
                                op("dve", lambda: dve.tensor_tensor(out=tf[:], in0=tf[:], in1=ps[bz][:, :], op=ALU.mult),
                                   reads=[tfB, psB[bz]], writes=[tfB])
                                if br == 1:
                                    ta, taB = tA[fc]
                                    op("pool", lambda: pool.tensor_tensor(out=mT[:, ch, :], in0=ta[:], in1=tf[:],
                                                                          op=ALU.add),
                                       reads=[taB, tfB], writes=[mB])
                            if fb + 1 < 4:
                                key = ("ga", "gb")[br]
                                G[key][fb + 1] = gslot(br, (O_GA, O_GB)[br] + (fb + 1) * 512)
                    load_rows(n, h_s, t)
                    for db in range(4):
                        wv, wvB = wblock(n, wo_s, WB["wo"], 16, db * 512)
                        for s in range(4):
                            b = n.psA.next()
                            for kc in range(16):
                                op("pe", lambda kc=kc, s=s: pe.matmul(ps[b][:, :], lhsT=mT[:, kc, s * 128:(s + 1) * 128],
                                                                      rhs=wv[:, kc, :], start=(kc == 0), stop=(kc == 15)),
                                   reads=[mB, wvB], writes=[psB[b]], inc=(kc == 15))
                            op("dve", lambda b=b, s=s, db=db: dve.tensor_tensor(
                                out=n.xt[:, s, db * 512:(db + 1) * 512], in0=ps[b][:, :],
                                in1=n.xt[:, s, db * 512:(db + 1) * 512], op=ALU.add),
                               reads=[psB[b]], writes=[n.xtB[s]])
                    n.actTB.inherit([yB, mB])
                    norm_to_T(n, g2c)
                    ffn(n, w2a_s, WB["w2a"], w2b_s, WB["w2b"],
                        mid_hook=((lambda t=t: prefetch_u(t + 1)) if t + 1 < NT else None))
                    if t + 1 < NT:
                        issue_first(t + 1)
                    row_stats(n)
                    for s in range(4):
                        eng_, e_ = ("dve", dve)
                        op(eng_, lambda s=s, e_=e_: e_.scalar_tensor_tensor(out=n.xt[:, s, :], in0=n.xt[:, s, :],
                                                                            scalar=n.stat[:, 8 + s:9 + s], in1=n.gfb[:],
                                                                            op0=ALU.mult, op1=ALU.mult),
                           reads=[n.statB, n.gfbB], writes=[n.xtB[s]])
                        out_toks.append(dma("pool", ods[s], out_d.ap()[t * 512 + s * 128: t * 512 + (s + 1) * 128, :],
                                            n.xt[:, s, :], reads=[n.xtB[s]], writes=[noB], ndesc=8))
                tr.barrier()
    except _Stop:
        pass
    return nc


def _geometry(T, mode):
    tok = np.arange(T)
    if mode == "prompt":
        return np.zeros(T, np.int64), tok, T // GRID_W
    half = T // 2
    return tok // half, tok % half, half // GRID_W


def _rope_table(T, mode):
    _, pos, _ = _geometry(T, mode)
    row = (pos // GRID_W).astype(np.float32)
    col = (pos % GRID_W).astype(np.float32)
    npairs = HD // 4
    inv = (10000.0 ** (-np.arange(npairs, dtype=np.float32) / npairs)).astype(np.float32)
    ang = np.concatenate([row[:, None] * inv[None], col[:, None] * inv[None]], axis=-1)
    return np.concatenate([np.cos(ang), np.sin(ang)], axis=-1).astype(np.float32)


def _ti_blocks(nblk):
    reps = [0, 1, None] + [nblk // 2 - 2 + i for i in range(4)] + [nblk - 2, nblk - 1]
    special = set(r for r in reps if r is not None)
    interior = [j for j in range(nblk) if j not in special]
    reps[2] = interior[0] if interior else None
    return reps


def _mask_b(T, mode):
    seq, pos, rows = _geometry(T, mode)
    nblk = T // 128
    r_of = pos // GRID_W
    c_of = pos % GRID_W
    rs = np.clip(r_of - 4, 0, rows - 8)
    cs = np.clip(c_of - 8, 0, GRID_W - 16)
    out = np.full((128, 9, 7, 128), NEG, np.float32)
    for ti, j in enumerate(_ti_blocks(nblk)):
        if j is None:
            continue
        q = j * 128 + np.arange(128)
        for di in range(7):
            c = j - 3 + di
            if c < 0 or c >= nblk:
                continue
            k = c * 128 + np.arange(128)
            ok = (seq[k][:, None] == seq[q][None, :])
            ok &= (r_of[k][:, None] >= rs[q][None, :]) & (r_of[k][:, None] < rs[q][None, :] + 8)
            ok &= (c_of[k][:, None] >= cs[q][None, :]) & (c_of[k][:, None] < cs[q][None, :] + 16)
            out[:, ti, di, :] = np.where(ok, 0.0, NEG)
    return out.reshape(128, 9 * 896)


def _mask_a(mode):
    m = np.zeros((128, 4), np.float32)
    if mode != "prompt":
        m[:, 1] = NEG
        m[:, 2] = NEG
    return m


def _shared_inputs(g_ffn1, w_ffn1_in, w_ffn1_out, g_mix, w_in, b_gate, g_q_a, g_k_a, rpb_b, w_branch_a,
                   w_branch_b, w_out, g_ffn2, w_ffn2_in, w_ffn2_out, g_final):
    f = lambda a: np.ascontiguousarray(np.asarray(a, dtype=np.float32))
    col = lambda g: f(np.asarray(g).reshape(-1, 128).T)
    return {
        "ident": np.eye(128, dtype=np.float32),
        "g1c": col(g_ffn1), "gmc": col(g_mix), "g2c": col(g_ffn2),
        "gfb": f(np.broadcast_to(np.asarray(g_final).reshape(1, D), (128, D))),
        "bgc": col(b_gate),
        "gqb": f(np.broadcast_to(np.asarray(g_q_a).reshape(1, 128), (128, 128))),
        "gkb": f(np.broadcast_to(np.asarray(g_k_a).reshape(1, 128), (128, 128))),
        "rpb": f(np.asarray(rpb_b).reshape(120, 31)),
        "w1a": f(np.asarray(w_ffn1_in)[0]), "w1b": f(np.asarray(w_ffn1_out)[0]),
        "win": f(np.asarray(w_in)[0]), "wa": f(np.asarray(w_branch_a)[0]), "wb": f(np.asarray(w_branch_b)[0]),
        "wo": f(np.asarray(w_out)[0]), "w2a": f(np.asarray(w_ffn2_in)[0]), "w2b": f(np.asarray(w_ffn2_out)[0]),
    }


def _slot_inputs(T, mode, x):
    return {"x": np.ascontiguousarray(x, dtype=np.float32), "rope": _rope_table(T, mode),
            "maskA": _mask_a(mode), "maskB": _mask_b(T, mode)}


_NC_CACHE = {}


def kernel(x_prompt, x_sample, g_ffn1, w_ffn1_in, w_ffn1_out, g_mix, w_in, b_gate, g_q_a, g_k_a,
           rpb_b, w_branch_a, w_branch_b, w_out, g_ffn2, w_ffn2_in, w_ffn2_out, g_final):
    x_prompt = np.asarray(x_prompt, dtype=np.float32)
    x_sample = np.asarray(x_sample, dtype=np.float32)
    T = x_prompt.shape[1]
    assert x_prompt.shape[0] == 2 and x_sample.shape[0] == 8 and x_sample.shape[1] * 2 == T
    shared = _shared_inputs(g_ffn1, w_ffn1_in, w_ffn1_out, g_mix, w_in, b_gate, g_q_a, g_k_a, rpb_b,
                            w_branch_a, w_branch_b, w_out, g_ffn2, w_ffn2_in, w_ffn2_out, g_final)
    in_maps = []
    for c in range(8):
        if c < 2:
            m = _slot_inputs(T, "prompt", x_prompt[c])
        elif c < 6:
            i = c - 2
            m = _slot_inputs(T, "sample", np.concatenate([x_sample[2 * i], x_sample[2 * i + 1]], axis=0))
        else:
            m = _slot_inputs(T, "prompt", np.zeros((T, D), np.float32))
        m.update(shared)
        in_maps.append(m)
    if T not in _NC_CACHE:
        _NC_CACHE[T] = build_nc(T)
    res = run_bass_kernel_spmd(_NC_CACHE[T], in_maps, core_ids=list(range(8)))
    outs = [np.asarray(r["out"], dtype=np.float32) for r in res.results]
    y_prompt = np.stack([outs[0], outs[1]], axis=0)
    half = T // 2
    y_sample = np.stack([outs[2 + i // 2][(i % 2) * half:(i % 2 + 1) * half] for i in range(8)], axis=0)
    return (y_prompt, y_sample)
```
